# Optimizing a Trainium2 kernel written in Bass

```python
import jax, jax.numpy as jnp
from jax import lax
import numpy as np

D_MODEL = 2048
BATCH = 4
SEQ = 2048
DEPTH = 4

HEAD_DIM = 128
N_Q_HEADS = 12
N_KV_HEADS = 4
GROUP = N_Q_HEADS // N_KV_HEADS
N_MEM_HEADS = 4
MEM_LEN = 256
WINDOW = 128
SW_BLOCK = 128
MOBA_BLOCK = 256
MOBA_TOPK = 3
MOBA_Q_CHUNK = 16
D_FF = 4 * D_MODEL
N_A_LAYERS = DEPTH // 2
N_B_LAYERS = DEPTH - N_A_LAYERS
ALPHA = (2 * DEPTH) ** 0.25
BETA = (8 * DEPTH) ** -0.25
LN_EPS = 1e-5
NEG_INF = -1e30
Q_W = N_Q_HEADS * HEAD_DIM
KV_W = N_KV_HEADS * HEAD_DIM
MQ_W = N_MEM_HEADS * HEAD_DIM
MIX_W = Q_W + MQ_W
IN_A_W = Q_W + 2 * KV_W + MQ_W

kernel_name = "yoco_swa_sink_moba_memory_deepnorm"


def alibi_slopes():
    return 2.0 ** (-8.0 * jnp.arange(1, N_Q_HEADS + 1, dtype=jnp.float32) / N_Q_HEADS)


def layer_norm(x, g, b):
    xf = x.astype(jnp.float32)
    mu = jnp.mean(xf, axis=-1, keepdims=True)
    var = jnp.mean(jnp.square(xf - mu), axis=-1, keepdims=True)
    return ((xf - mu) * lax.rsqrt(var + LN_EPS) * g.astype(jnp.float32)
            + b.astype(jnp.float32)).astype(x.dtype)


def sliding_window_attention(q, k, v, sinks, slopes):
    B, T = q.shape[0], q.shape[1]
    nb = T // SW_BLOCK
    scale = HEAD_DIM ** -0.5
    qb = q.reshape(B, nb, SW_BLOCK, N_KV_HEADS, GROUP, HEAD_DIM)
    kb = k.reshape(B, nb, SW_BLOCK, N_KV_HEADS, HEAD_DIM)
    vb = v.reshape(B, nb, SW_BLOCK, N_KV_HEADS, HEAD_DIM)
    prev = lambda a: jnp.concatenate([jnp.zeros_like(a[:, :1]), a[:, :-1]], axis=1)
    kw = jnp.concatenate([prev(kb), kb], axis=2)
    vw = jnp.concatenate([prev(vb), vb], axis=2)
    logits = jnp.einsum('bnqkgd,bnskd->bnkgqs', qb, kw).astype(jnp.float32) * scale
    qi = jnp.arange(SW_BLOCK)[:, None] + SW_BLOCK
    sj = jnp.arange(2 * SW_BLOCK)[None, :]
    dist = qi - sj
    blk = jnp.arange(nb)[:, None, None]
    valid = (dist >= 0) & (dist < WINDOW) & ((blk > 0) | (sj >= SW_BLOCK))
    slopes_kg = slopes.reshape(N_KV_HEADS, GROUP)[:, :, None, None]
    logits = logits - slopes_kg * dist.astype(jnp.float32)
    logits = jnp.where(valid[None, :, None, None], logits, NEG_INF)
    sink = sinks.astype(jnp.float32).reshape(N_KV_HEADS, GROUP)[:, :, None, None]
    m = jnp.maximum(jnp.max(logits, axis=-1, keepdims=True), sink)
    p = jnp.exp(logits - m)
    denom = jnp.sum(p, axis=-1, keepdims=True) + jnp.exp(sink - m)
    out = jnp.einsum('bnkgqs,bnskd->bnqkgd', (p / denom).astype(v.dtype), vw)
    return out.reshape(B, T, Q_W)


def moba_attention(q, k_blocks, v_blocks, k_means, slopes):
    B, T = q.shape[0], q.shape[1]
    nb = k_blocks.shape[2]
    k_sel = min(MOBA_TOPK, nb)
    n_chunks = T // MOBA_Q_CHUNK
    scale = HEAD_DIM ** -0.5
    slopes_kg = slopes.reshape(N_KV_HEADS, GROUP)
    b_idx = jnp.arange(B)[:, None, None, None, None]
    h_idx = jnp.arange(N_KV_HEADS)[None, None, :, None, None]
    q_chunks = q.reshape(B, n_chunks, MOBA_Q_CHUNK, N_KV_HEADS, GROUP, HEAD_DIM)
    q_chunks = jnp.moveaxis(q_chunks, 1, 0)

    def chunk(args):
        c, qc = args
        start = c * MOBA_Q_CHUNK
        t = start + jnp.arange(MOBA_Q_CHUNK)
        j = start // MOBA_BLOCK
        gate = jnp.einsum('bqkgd,bknd->bqkgn', qc, k_means).astype(jnp.float32)
        gate = jnp.where(jnp.arange(nb) < j, gate, NEG_INF)
        _, sel = lax.top_k(gate, k_sel)
        slot_valid = jnp.arange(k_sel) < j
        kg = k_blocks[b_idx, h_idx, sel]
        vg = v_blocks[b_idx, h_idx, sel]
        lg = jnp.einsum('bqkgd,bqkgjsd->bqkgjs', qc, kg).astype(jnp.float32) * scale
        kpos = sel[..., None] * MOBA_BLOCK + jnp.arange(MOBA_BLOCK)
        dist = (t[None, :, None, None, None, None] - kpos).astype(jnp.float32)
        lg = lg - slopes_kg[None, None, :, :, None, None] * dist
        lg = jnp.where(slot_valid[:, None], lg, NEG_INF)
        lg = lg.reshape(B, MOBA_Q_CHUNK, N_KV_HEADS, GROUP, k_sel * MOBA_BLOCK)
        ko = lax.dynamic_index_in_dim(k_blocks, j, axis=2, keepdims=False)
        vo = lax.dynamic_index_in_dim(v_blocks, j, axis=2, keepdims=False)
        lo = jnp.einsum('bqkgd,bksd->bqkgs', qc, ko).astype(jnp.float32) * scale
        dist_o = (t[:, None] - (j * MOBA_BLOCK + jnp.arange(MOBA_BLOCK))[None, :])
        dist_o = dist_o[None, :, None, None, :]
        lo = lo - slopes_kg[None, None, :, :, None] * dist_o.astype(jnp.float32)
        lo = jnp.where(dist_o >= 0, lo, NEG_INF)
        p = jax.nn.softmax(jnp.concatenate([lg, lo], axis=-1), axis=-1)
        pg = p[..., :k_sel * MOBA_BLOCK].reshape(B, MOBA_Q_CHUNK, N_KV_HEADS, GROUP, k_sel, MOBA_BLOCK)
        po = p[..., k_sel * MOBA_BLOCK:]
        return (jnp.einsum('bqkgjs,bqkgjsd->bqkgd', pg.astype(vg.dtype), vg)
                + jnp.einsum('bqkgs,bksd->bqkgd', po.astype(vo.dtype), vo))

    out = lax.map(chunk, (jnp.arange(n_chunks), q_chunks))
    return jnp.moveaxis(out, 0, 1).reshape(B, T, Q_W)


def shared_moba_kv(h, w_kv_shared):
    B, T = h.shape[0], h.shape[1]
    kv = h @ w_kv_shared
    k = kv[..., :KV_W].reshape(B, T, N_KV_HEADS, HEAD_DIM)
    v = kv[..., KV_W:].reshape(B, T, N_KV_HEADS, HEAD_DIM)
    nb = -(-T // MOBA_BLOCK)
    pad = nb * MOBA_BLOCK - T
    k = jnp.pad(k, ((0, 0), (0, pad), (0, 0), (0, 0)))
    v = jnp.pad(v, ((0, 0), (0, pad), (0, 0), (0, 0)))
    k_blocks = k.reshape(B, nb, MOBA_BLOCK, N_KV_HEADS, HEAD_DIM).transpose(0, 3, 1, 2, 4)
    v_blocks = v.reshape(B, nb, MOBA_BLOCK, N_KV_HEADS, HEAD_DIM).transpose(0, 3, 1, 2, 4)
    k_means = jnp.mean(k_blocks.astype(jnp.float32), axis=3).astype(k.dtype)
    return k_blocks, v_blocks, k_means


def memory_attention(qm, mem, w_mem_kv):
    B, T = qm.shape[0], qm.shape[1]
    M = mem.shape[1]
    mkv = mem @ w_mem_kv
    mk = mkv[..., :MQ_W].reshape(B, M, N_MEM_HEADS, HEAD_DIM)
    mv = mkv[..., MQ_W:].reshape(B, M, N_MEM_HEADS, HEAD_DIM)
    qh = qm.reshape(B, T, N_MEM_HEADS, HEAD_DIM)
    lg = jnp.einsum('bqhd,bmhd->bhqm', qh, mk).astype(jnp.float32) * (HEAD_DIM ** -0.5)
    p = jax.nn.softmax(lg, axis=-1)
    return jnp.einsum('bhqm,bmhd->bqhd', p.astype(mv.dtype), mv).reshape(B, T, MQ_W)


def setup_inputs(seed: int = 0) -> dict:
    key = jax.random.key(seed)
    ks = jax.random.split(key, 12)
    nrm = jax.random.normal
    s = D_MODEL ** -0.5
    x = nrm(ks[0], (BATCH, SEQ, D_MODEL), jnp.float32)
    mem = nrm(ks[1], (BATCH, MEM_LEN, D_MODEL), jnp.float32)
    col_a = jnp.concatenate([jnp.ones((Q_W + KV_W,)), jnp.full((KV_W,), BETA),
                             jnp.ones((MQ_W,))]).astype(jnp.float32)
    w_in_a = nrm(ks[2], (N_A_LAYERS, D_MODEL, IN_A_W), jnp.float32) * s * col_a
    sinks_a = nrm(ks[3], (N_A_LAYERS, N_Q_HEADS), jnp.float32) * 0.5
    w_q_b = nrm(ks[4], (N_B_LAYERS, D_MODEL, MIX_W), jnp.float32) * s
    col_kv = jnp.concatenate([jnp.ones((KV_W,)), jnp.full((KV_W,), BETA)]).astype(jnp.float32)
    w_kv_shared = nrm(ks[5], (D_MODEL, 2 * KV_W), jnp.float32) * s * col_kv
    col_m = jnp.concatenate([jnp.ones((MQ_W,)), jnp.full((MQ_W,), BETA)]).astype(jnp.float32)
    w_mem_kv = nrm(ks[6], (DEPTH, D_MODEL, 2 * MQ_W), jnp.float32) * s * col_m
    w_o = nrm(ks[7], (DEPTH, MIX_W, D_MODEL), jnp.float32) * (MIX_W ** -0.5) * BETA
    w_up = nrm(ks[8], (DEPTH, D_MODEL, D_FF), jnp.float32) * s
    w_down = nrm(ks[9], (DEPTH, D_FF, D_MODEL), jnp.float32) * (D_FF ** -0.5) * BETA
    ln_g = 1.0 + 0.02 * nrm(ks[10], (DEPTH, 2, D_MODEL), jnp.float32)
    ln_b = 0.02 * nrm(ks[11], (DEPTH, 2, D_MODEL), jnp.float32)
    return {"x": x, "mem": mem, "w_in_a": w_in_a, "sinks_a": sinks_a, "w_q_b": w_q_b,
            "w_kv_shared": w_kv_shared, "w_mem_kv": w_mem_kv, "w_o": w_o,
            "w_up": w_up, "w_down": w_down, "ln_g": ln_g, "ln_b": ln_b}


def reference(x, mem, w_in_a, sinks_a, w_q_b, w_kv_shared, w_mem_kv, w_o,
              w_up, w_down, ln_g, ln_b):
    B, T = x.shape[0], x.shape[1]
    slopes = alibi_slopes()
    h = x
    shared = None
    for layer in range(DEPTH):
        if layer < N_A_LAYERS:
            proj = h @ w_in_a[layer]
            q = proj[..., :Q_W].reshape(B, T, N_KV_HEADS, GROUP, HEAD_DIM)
            k = proj[..., Q_W:Q_W + KV_W].reshape(B, T, N_KV_HEADS, HEAD_DIM)
            v = proj[..., Q_W + KV_W:Q_W + 2 * KV_W].reshape(B, T, N_KV_HEADS, HEAD_DIM)
            qm = proj[..., Q_W + 2 * KV_W:]
            self_out = sliding_window_attention(q, k, v, sinks_a[layer], slopes)
        else:
            if layer == N_A_LAYERS:
                shared = shared_moba_kv(h, w_kv_shared)
            proj = h @ w_q_b[layer - N_A_LAYERS]
            q = proj[..., :Q_W].reshape(B, T, N_KV_HEADS, GROUP, HEAD_DIM)
            qm = proj[..., Q_W:]
            self_out = moba_attention(q, shared[0], shared[1], shared[2], slopes)
        mem_out = memory_attention(qm, mem, w_mem_kv[layer])
        mix = jnp.concatenate([self_out, mem_out], axis=-1) @ w_o[layer]
        h = layer_norm(ALPHA * h + mix, ln_g[layer, 0], ln_b[layer, 0])
        ffn = jnp.square(jax.nn.relu(h @ w_up[layer])) @ w_down[layer]
        h = layer_norm(ALPHA * h + ffn, ln_g[layer, 1], ln_b[layer, 1])
    return h
```

```python
import numpy as np
import concourse.bass as bass
import concourse.mybir as mybir
from concourse.bass_utils import run_bass_kernel_spmd

F32 = mybir.dt.float32
BF16 = mybir.dt.bfloat16
U8 = mybir.dt.uint8
AF = mybir.ActivationFunctionType
ALU = mybir.AluOpType
AX = mybir.AxisListType

NEG = -30000.0
BIG = 1.0e6
SC = 128.0 ** -0.5
ALPHA = 8.0 ** 0.25
EPS = 1e-5
SLOPES = [2.0 ** (-8.0 * (i + 1) / 12.0) for i in range(12)]
NT = 8
TOK = 1024


class Prog:
    def __init__(self, nc):
        self.nc = nc
        self.ops = {e: [] for e in ("pe", "act", "dve", "pool", "sp")}
        self.cnt = {}
        self.seen = {e: {} for e in self.ops}
        self.lastw = {}
        self.readers = {}

    def _deps(self, eng, reads, writes):
        need = {}

        def add(d):
            if d is None:
                return
            s, v = d
            if v > need.get(s, 0):
                need[s] = v
        for t in reads:
            add(self.lastw.get(t))
        for t in writes:
            add(self.lastw.get(t))
            for s, v in self.readers.get(t, {}).items():
                add((s, v))
        waits = []
        for s, v in need.items():
            if self.seen[eng].get(s, 0) < v:
                self.seen[eng][s] = v
                waits.append((s, v))
        return waits

    def _commit(self, stamp, reads, writes):
        s, v = stamp
        for t in reads:
            self.readers.setdefault(t, {})[s] = v
        for t in writes:
            self.lastw[t] = stamp
            self.readers[t] = {}

    def op(self, eng, fns, reads=(), writes=()):
        waits = self._deps(eng, reads, writes)
        self.cnt[eng] = self.cnt.get(eng, 0) + 1
        self._commit((eng, self.cnt[eng]), reads, writes)
        if not isinstance(fns, (list, tuple)):
            fns = [fns]
        self.ops[eng].append((waits, list(fns), eng, 1))

    def dma(self, queue, fn, sem, reads=(), writes=()):
        waits = self._deps(queue, reads, writes)
        key = "d_" + sem
        prev = self.cnt.get(key, 0)
        if prev > 0 and self.seen[queue].get(key, 0) < prev:
            self.seen[queue][key] = prev
            waits.append((key, prev))
        self.cnt[key] = prev + 16
        self._commit((key, self.cnt[key]), reads, writes)
        self.ops[queue].append((waits, [fn], key, 16))

    def special(self, queue, fn, sem, reads=(), writes=()):
        waits = self._deps(queue, reads, writes)
        key = "c_" + sem
        assert key not in self.cnt
        self.cnt[key] = 1
        self._commit((key, 1), reads, writes)
        self.ops[queue].append((waits, [fn], key, None))

    def barrier(self, engs=("pe", "act", "dve", "sp")):
        for e in engs:
            waits = []
            for s, v in self.cnt.items():
                if self.seen[e].get(s, 0) < v:
                    self.seen[e][s] = v
                    waits.append((s, v))
            if waits:
                self.ops[e].append((waits, [], None, 0))

    def emit(self, final_waits_engine="sp"):
        nc = self.nc
        fw = [(s, v) for s, v in self.cnt.items()]
        self.ops[final_waits_engine].append((fw, [], None, 0))
        import contextlib
        with contextlib.ExitStack() as st:
            sems = {}
            for s in self.cnt:
                sems[s] = st.enter_context(nc.semaphore("s_" + s))
            block = st.enter_context(nc.Block())
            engmap = {"pe": block.tensor, "act": block.scalar, "dve": block.vector,
                      "pool": block.gpsimd, "sp": block.sync}
            for ename, deco in engmap.items():
                ops = self.ops[ename]

                def body(e, ops=ops):
                    for waits, fns, incsem, incv in ops:
                        for s, v in waits:
                            e.wait_ge(sems[s], v)
                        ins = None
                        for f in fns:
                            ins = f(e)
                        if ins is not None and incsem is not None:
                            if incv is None:
                                ins.then_inc(sems[incsem])
                            else:
                                ins.then_inc(sems[incsem], incv)
                deco(body)


def build_fused(nlayers=4, ncores=8):
    nc = bass.Bass("TRN2", target_bir_lowering=False)
    P = Prog(nc)
    groups = [[2 * i, 2 * i + 1] for i in range(ncores // 2)]

    def din(name, shape):
        return nc.dram_tensor(name, shape, F32, kind="ExternalInput").ap()

    xin = din("x_own", [TOK, 2048])
    xhalo = din("x_halo", [128, 2048])
    halobig = din("halobig", [128, 1])
    pastmask = din("pastmask", [128, 1])
    mem = din("mem", [256, 2048])
    w_in_a = din("w_in_a", [2, 2048, 3072])
    sinks = din("sinks", [1, 24])
    w_q_b = din("w_q_b", [2, 2048, 2048])
    w_kvs = din("w_kvs", [2048, 1024])
    w_mem_kv = din("w_mem_kv", [4, 2048, 1024])
    w_o_all = din("w_o", [4, 2048, 2048])
    w_up_all = din("w_up", [4, 2048, 8192])
    w_dn_all = din("w_dn", [4, 8192, 2048])
    lng_all = din("lng", [4, 2, 2048])
    lnb_all = din("lnb", [4, 2, 2048])
    hout = nc.dram_tensor("hout", [TOK, 2048], F32, kind="ExternalOutput").ap()
    halo_in = nc.dram_tensor("halo_in", [128, 2048], F32)
    halo_out = nc.dram_tensor("halo_out", [256, 2048], F32)
    kin = nc.dram_tensor("kin", [128, 4096], F32)
    kout = nc.dram_tensor("kout", [256, 4096], F32)
    vin = nc.dram_tensor("vin", [128, 4096], F32)
    vout = nc.dram_tensor("vout", [256, 4096], F32)
    kmin = nc.dram_tensor("kmin", [128, 256], F32)
    kmout = nc.dram_tensor("kmout", [256, 256], F32)
    kin_ap, kout_ap, vin_ap, vout_ap, kmin_ap, kmout_ap = (kin.ap(), kout.ap(), vin.ap(), vout.ap(), kmin.ap(),
                                                            kmout.ap())

    ARENA = 212480
    arena = nc.alloc_sbuf_tensor("arena", [128, ARENA], U8)
    ps = nc.alloc_psum_tensor("ps", [128, 8, 512], F32)

    def view(off, shape, dt):
        n = 1
        for s in shape:
            n *= s
        nb = n * (4 if dt == F32 else 2)
        a = arena[:, off:off + nb].bitcast(dt)
        if len(shape) == 2:
            return a.rearrange("p (a b) -> p a b", b=shape[1])
        return a

    OH, ORING, OA, OB, OKV, OC = 0, 65536, 98304, 131072, 163840, 196608
    h = view(OH, [8, 2048], F32)
    ring = [view(ORING + i * 16384, [16, 512], BF16) for i in range(2)]
    ringd = [view(ORING + i * 16384, [4, 2048], BF16) for i in range(2)]
    hT = view(OA, [16, 1024], BF16)
    L = view(OA, [2048], F32)
    memb = view(OA, [2, 2048], BF16)
    memT = view(OA + 8192, [16, 256], BF16)
    Pn = view(OA + 8192, [2048], BF16)
    PT = [view(OA + 12288 + i * 4096, [16, 128], BF16) for i in range(2)]
    qT = view(OB, [16, 1024], BF16)
    actT = [view(OB + i * 8192, [4, 1024], BF16) for i in range(2)]
    rtmp = [view(OB + 16384 + i * 2048, [512], F32) for i in range(2)]
    kmacc = view(OB, [4, 4], F32)
    kT = view(OKV, [4, 2048], BF16)
    vv = view(OKV + 16384, [16, 512], BF16)
    gtab = view(OKV, [2048], F32)
    btab = view(OKV + 8192, [2048], F32)
    hb = [view(OKV + 16384 + i * 4096, [2048], BF16) for i in range(2)]
    kvf = view(OKV + 24576, [512], F32)
    c = OC
    mkT = view(c, [4, 256], BF16); c += 2048
    mv = view(c, [2, 512], BF16); c += 2048
    ident = view(c, [128], BF16); c += 256
    identf = view(c, [128], F32); c += 512
    ca = c
    hTh = view(ca, [16, 128], BF16); ca += 4096
    hhb = view(ca, [2048], BF16); ca += 4096
    dist = view(ca, [256], F32); ca += 1024
    dist0 = view(ca, [256], F32); ca += 1024
    sinkb = view(ca, [24], F32); ca += 96
    halob = view(ca, [1], F32); ca += 32
    cb = c
    R0 = view(cb, [2048], F32); cb += 8192
    caus = view(cb, [128], F32); cb += 512
    kmT = view(cb, [4, 8], BF16); cb += 64
    gbias = view(cb, [4, 8], F32); cb += 128
    pmask = view(cb, [1], F32); cb += 32
    gm = view(cb, [8], F32); cb += 32
    top8 = view(cb, [8], F32); cb += 32
    bm = view(cb, [8], F32); cb += 32
    c = max(ca, cb)
    stt = view(c, [4, 6], F32); c += 96
    mv2 = view(c, [2], F32); c += 32
    sm = view(c, [16], F32); c += 64
    epsb = view(c, [1], F32); c += 32
    assert c <= ARENA, c
    SM_MX, SM_NEGM, SM_RS, SM_ES, SM_DEN, SM_RDEN, SM_SD, SM_RSTD, SM_NMR = range(9)
    A_TABLE_TOKS = ["hTh", "hhb", "dist", "dist0", "sinkb", "halob"]

    def smc(i):
        return sm[:, i:i + 1]

    def psb(b):
        return ps[:, b, :].bitcast(BF16)

    P.op("pool", lambda e: e.memset(identf, 0.0), writes=["identf"])
    P.op("pool", lambda e: e.affine_select(out=identf, in_=identf, pattern=[[-1, 128]], base=0,
                                           channel_multiplier=1, compare_op=ALU.not_equal, fill=1.0),
         reads=["identf"], writes=["identf"])
    P.op("pool", lambda e: e.tensor_copy(out=ident, in_=identf), reads=["identf"], writes=["ident"])
    P.op("pool", lambda e: e.memset(epsb, EPS), writes=["epsb"])

    def setup_A():
        P.op("pool", lambda e: e.iota(dist, pattern=[[-1, 256]], base=128, channel_multiplier=1,
                                      allow_small_or_imprecise_dtypes=True), writes=["dist"])
        P.op("pool", lambda e: e.affine_select(out=dist, in_=dist, pattern=[[-1, 256]], base=128,
                                               channel_multiplier=1, compare_op=ALU.is_ge, fill=BIG),
             reads=["dist"], writes=["dist"])
        P.op("pool", lambda e: e.affine_select(out=dist, in_=dist, pattern=[[1, 256]], base=-1,
                                               channel_multiplier=-1, compare_op=ALU.is_ge, fill=BIG),
             reads=["dist"], writes=["dist"])
        P.dma("sp", lambda e: e.dma_start(out=halob, in_=halobig), "misc", writes=["halob"])
        P.dma("sp", lambda e: e.dma_start(out=sinkb, in_=sinks.partition_broadcast(128)), "misc", writes=["sinkb"])
        P.op("pool", lambda e: e.tensor_copy(out=dist0, in_=dist), reads=["dist"], writes=["dist0"])
        P.op("pool", lambda e: e.tensor_scalar(out=dist0[:, 0:128], in0=dist0[:, 0:128], scalar1=halob[:, 0:1],
                                               scalar2=None, op0=ALU.max),
             reads=["dist0", "halob"], writes=["dist0"])

    def setup_B():
        P.op("pool", lambda e: e.iota(R0, pattern=[[1, 2048]], base=0, channel_multiplier=0,
                                      allow_small_or_imprecise_dtypes=True), writes=["R0"] + A_TABLE_TOKS)
        P.op("pool", lambda e: e.memset(caus, 0.0), writes=["caus"] + A_TABLE_TOKS)
        P.op("pool", lambda e: e.affine_select(out=caus, in_=caus, pattern=[[-1, 128]], base=0,
                                               channel_multiplier=1, compare_op=ALU.is_ge, fill=NEG),
             reads=["caus"], writes=["caus"])
        P.dma("sp", lambda e: e.dma_start(out=pmask, in_=pastmask), "misc", writes=["pmask"] + A_TABLE_TOKS)
        P.op("pool", lambda e: e.memset(gbias, NEG), writes=["gbias"] + A_TABLE_TOKS)
        for jl in range(4):
            P.op("pool", lambda e, jl=jl: e.memset(gbias[:, jl, 0:4 + jl], 0.0), reads=["gbias"], writes=["gbias"])
            P.op("pool", lambda e, jl=jl: e.tensor_scalar(out=gbias[:, jl, 0:4], in0=gbias[:, jl, 0:4],
                                                          scalar1=pmask[:, 0:1], scalar2=None, op0=ALU.add),
                 reads=["pmask", "gbias"], writes=["gbias"])

    ring_state = {"n": 0}

    def load_unit_cols(W, c0):
        i = ring_state["n"] % 2
        ring_state["n"] += 1
        src = W.rearrange("(c p) f -> p c f", p=128)[:, :, c0:c0 + 512]
        P.dma("pool", lambda e: e.dma_start(out=ring[i], in_=src), "ring%d" % i, writes=[("ring", i)])
        return i

    def load_unit_rows(W, r0):
        i = ring_state["n"] % 2
        ring_state["n"] += 1
        src = W[r0:r0 + 512, :].rearrange("(c p) d -> p c d", p=128)
        P.dma("pool", lambda e: e.dma_start(out=ringd[i], in_=src), "ring%d" % i, writes=[("ring", i)])
        return i

    bank_rr = {"n": 0}

    def mm_group(out_ap, pairs, reads, writes):
        n = len(pairs)
        fns = []
        for i, (l, r) in enumerate(pairs):
            fns.append(lambda e, l=l, r=r, i=i: e.matmul(out=out_ap, lhsT=l, rhs=r, start=(i == 0), stop=(i == n - 1)))
        P.op("pe", fns, reads=reads, writes=writes)

    def transposes(srcs, reads, writes):
        fns = [lambda e, d=d, s=s: e.transpose(out=d, in_=s, identity=ident) for d, s in srcs]
        P.op("pe", fns, reads=list(reads) + ["ident"], writes=writes)

    pb6 = ps[:, 6:8, :].bitcast(BF16).rearrange("p a b -> p (a b)")

    def to_hT(src_ap, src_tok, dst, dst_cols, dst_tok):
        srcs = [(pb6[:, cc * 128:(cc + 1) * 128], src_ap[:, cc * 128:(cc + 1) * 128]) for cc in range(16)]
        transposes(srcs, reads=[src_tok], writes=[("ps", 6), ("ps", 7)])
        P.op("act", lambda e: e.activation(out=dst[:, :, dst_cols], in_=pb6.rearrange("p (a b) -> p a b", b=128),
                                           func=AF.Copy),
             reads=[("ps", 6), ("ps", 7)], writes=[dst_tok])

    hT_all = [("hT", t) for t in range(NT)]

    def evac_act(out_ap, in_ap, scale, reads, writes):
        P.op("act", lambda e: e.activation(out=out_ap, in_=in_ap, func=AF.Copy, scale=scale), reads=reads, writes=writes)

    def next_bank(lo=0, hi=6):
        b = lo + bank_rr["n"] % (hi - lo)
        bank_rr["n"] += 1
        return b

    def proj_feat(slot, j, dst, dst_slot, col_off, scale, tokname):
        for th in range(2):
            b = next_bank()
            pairs = [(ring[slot][:, cc, j * 128:(j + 1) * 128], hT[:, cc, th * 512:(th + 1) * 512]) for cc in range(16)]
            mm_group(ps[:, b, :], pairs, reads=[("ring", slot)] + hT_all[th * 4:th * 4 + 4], writes=[("ps", b)])
            evac_act(dst[:, dst_slot, col_off + th * 512: col_off + (th + 1) * 512], ps[:, b, :], scale,
                     reads=[("ps", b)], writes=[(tokname, dst_slot, 4 * th + k) for k in range(4)])

    cnt = {"ot": 0, "pt": 0, "s": 0}
    KV_ALIAS = ["gtab", "btab", ("hb", 0), ("hb", 1), "kvf"]

    def softmax_pv(isA, h_slot, t, nk, s_psum, sink_col, v_aps, s_banks):
        src = L[:, 0:nk] if s_psum is None else s_psum
        src_reads = ["L"] if s_psum is None else [("ps", b) for b in s_banks]
        P.op("dve", lambda e: e.reduce_max(out=smc(SM_MX), in_=src, axis=AX.X), reads=src_reads, writes=["mx"])
        if sink_col is not None:
            P.op("dve", lambda e: e.tensor_scalar(out=smc(SM_NEGM), in0=smc(SM_MX), scalar1=sinkb[:, sink_col:sink_col + 1],
                                                  scalar2=-1.0, op0=ALU.max, op1=ALU.mult),
                 reads=["mx", "sinkb"], writes=["negm"])
        else:
            P.op("dve", lambda e: e.tensor_scalar(out=smc(SM_NEGM), in0=smc(SM_MX), scalar1=-1.0, scalar2=None,
                                                  op0=ALU.mult), reads=["mx"], writes=["negm"])
        P.op("act", lambda e: e.activation(out=L[:, 0:nk], in_=src, func=AF.Exp, bias=smc(SM_NEGM), scale=1.0,
                                           accum_out=smc(SM_RS)),
             reads=src_reads + ["negm"], writes=["L", "rs"])
        if sink_col is not None:
            P.op("act", lambda e: e.activation(out=smc(SM_ES), in_=sinkb[:, sink_col:sink_col + 1], func=AF.Exp,
                                               bias=smc(SM_NEGM), scale=1.0), reads=["negm", "sinkb"], writes=["es"])
            P.op("dve", lambda e: e.tensor_tensor(out=smc(SM_DEN), in0=smc(SM_RS), in1=smc(SM_ES), op=ALU.add),
                 reads=["rs", "es"], writes=["den"])
            P.op("dve", lambda e: e.reciprocal(out=smc(SM_RDEN), in_=smc(SM_DEN)), reads=["den"], writes=["rden"])
        else:
            P.op("dve", lambda e: e.reciprocal(out=smc(SM_RDEN), in_=smc(SM_RS)), reads=["rs"], writes=["rden"])
        P.op("dve", lambda e: e.tensor_scalar(out=Pn[:, 0:nk], in0=L[:, 0:nk], scalar1=smc(SM_RDEN), scalar2=None,
                                              op0=ALU.mult), reads=["L", "rden"], writes=["Pn"])
        nb = nk // 128
        pi = cnt["pt"] % 2
        cnt["pt"] += 1
        if nb <= 8:
            pbank = [6 + pi]
            pdst = psb(6 + pi)
        else:
            pbank = [6, 7]
            pdst = pb6
        srcs = [(pdst[:, k * 128:(k + 1) * 128], Pn[:, k * 128:(k + 1) * 128]) for k in range(nb)]
        transposes(srcs, reads=["Pn"], writes=[("ps", b) for b in pbank])
        P.op("act", lambda e: e.activation(out=PT[pi][:, 0:nb, :],
                                           in_=pdst[:, 0:nb * 128].rearrange("p (a b) -> p a b", b=128), func=AF.Copy),
             reads=[("ps", b) for b in pbank], writes=[("PT", pi)])
        ob = (4 + cnt["ot"] % 2) if isA else 4
        cnt["ot"] += 1
        pairs = [(v_aps[k][0], PT[pi][:, k, :]) for k in range(nb)]
        vreads = sorted(set(v_aps[k][1] for k in range(nb)), key=str)
        mm_group(ps[:, ob, 0:128], pairs, reads=[("PT", pi)] + vreads, writes=[("ps", ob)])
        P.op("act", lambda e: e.activation(out=qT[:, h_slot, t * 128:(t + 1) * 128], in_=ps[:, ob, 0:128], func=AF.Copy),
             reads=[("ps", ob)], writes=[("qT", h_slot, t)])

    def load_gb(layer, i):
        P.dma("sp", lambda e: e.dma_start(out=gtab, in_=lng_all[layer, i:i + 1, :].partition_broadcast(128)), "misc",
              writes=["gtab"])
        P.dma("sp", lambda e: e.dma_start(out=btab, in_=lnb_all[layer, i:i + 1, :].partition_broadcast(128)), "misc",
              writes=["btab"])

    hout_v = hout.rearrange("(n p) d -> p n d", p=128)

    def layer_norm(t, store, make_hT, halo_send):
        for q4 in range(4):
            P.op("dve", lambda e, q4=q4: e.bn_stats(out=stt[:, q4, :], in_=h[:, t, q4 * 512:(q4 + 1) * 512]),
                 reads=[("h", t)], writes=[("stt", q4)])
        P.op("dve", lambda e: e.bn_aggr(out=mv2, in_=stt.rearrange("p a b -> p (a b)")),
             reads=[("stt", q4) for q4 in range(4)], writes=["mv2"])
        P.op("act", lambda e: e.activation(out=smc(SM_SD), in_=mv2[:, 1:2], func=AF.Sqrt, bias=epsb[:, 0:1], scale=1.0),
             reads=["mv2", "epsb"], writes=["sd"])
        P.op("dve", lambda e: e.reciprocal(out=smc(SM_RSTD), in_=smc(SM_SD)), reads=["sd"], writes=["rstd"])
        P.op("dve", lambda e: e.scalar_tensor_tensor(out=smc(SM_NMR), in0=mv2[:, 0:1], scalar=-1.0, in1=smc(SM_RSTD),
                                                     op0=ALU.mult, op1=ALU.mult), reads=["mv2", "rstd"], writes=["nmr"])
        P.op("act", lambda e: e.activation(out=h[:, t, :], in_=h[:, t, :], func=AF.Identity, scale=smc(SM_RSTD),
                                           bias=smc(SM_NMR)), reads=[("h", t), "rstd", "nmr"], writes=[("h", t)])
        P.op("pool", lambda e: e.tensor_tensor(out=h[:, t, :], in0=h[:, t, :], in1=gtab, op=ALU.mult),
             reads=[("h", t), "gtab"], writes=[("h", t)])
        P.op("dve", lambda e: e.tensor_tensor(out=h[:, t, :], in0=h[:, t, :], in1=btab, op=ALU.add),
             reads=[("h", t), "btab"], writes=[("h", t)])
        if store:
            P.dma("sp", lambda e: e.dma_start(out=hout_v[:, t, :], in_=h[:, t, :]), "hstore%d" % (t % 4), reads=[("h", t)])
        if make_hT:
            i = t % 2
            P.op("act", lambda e: e.activation(out=hb[i], in_=h[:, t, :], func=AF.Copy),
                 reads=[("h", t)], writes=[("hb", i)])
            to_hT(hb[i], ("hb", i), hT, slice(t * 128, (t + 1) * 128), ("hT", t))
            if halo_send and t == NT - 1:
                P.dma("sp", lambda e: e.dma_start(out=halo_in.ap(), in_=h[:, t, :]), "halo", reads=[("h", t)],
                      writes=["halo_in"])
                P.special("pool", lambda e: e.collective_compute(
                    "AllGather", ALU.bypass, replica_groups=groups, ins=[halo_in.ap().opt()],
                    outs=[halo_out.ap().opt()]), "halo", reads=["halo_in"], writes=["halo_out"])

    xin_v = xin.rearrange("(n p) d -> p n d", p=128)
    for t in range(NT):
        P.dma("sp", lambda e, t=t: e.dma_start(out=h[:, t, :], in_=xin_v[:, t, :]), "hload%d" % (t % 4), writes=[("h", t)])
    for t in range(NT):
        i = t % 2
        P.op("act", lambda e, t=t, i=i: e.activation(out=hb[i], in_=h[:, t, :], func=AF.Copy),
             reads=[("h", t)], writes=[("hb", i)])
        to_hT(hb[i], ("hb", i), hT, slice(t * 128, (t + 1) * 128), ("hT", t))
    setup_A()

    for layer in range(nlayers):
        isA = layer < 2
        last = layer == nlayers - 1
        w_mkv = w_mem_kv[layer]
        w_o = w_o_all[layer]
        w_up = w_up_all[layer]
        w_dn = w_dn_all[layer]
        if layer > 0:
            P.barrier()
        if isA:
            w_qkv = w_in_a[layer]
            if layer == 0:
                P.dma("pool", lambda e: e.dma_start(out=hhb, in_=xhalo), "misc2", writes=["hhb"])
            else:
                P.dma("pool", lambda e: e.dma_start(out=hhb, in_=halo_out.ap()[0:128, :]), "misc2", reads=["halo_out"],
                      writes=["hhb"])
            to_hT(hhb, "hhb", hTh, slice(0, 128), "hTh")
            units = [("q", 0), ("q", 1), ("q", 2), ("k", 3), ("v", 4), ("qm", 5)]
        else:
            w_qkv = w_q_b[layer - 2]
            if layer == 2:
                setup_B()
            units = [("q", 0), ("q", 1), ("q", 2), ("qm", 3)]
            kparts = [(kout_ap[0:128, :], kT[:, :, 0:1024]), (kin_ap, kT[:, :, 1024:2048])]
            for si, di in kparts:
                P.dma("pool", lambda e, si=si, di=di: e.dma_start(out=di, in_=si.rearrange("p (a b) -> p a b", b=1024)),
                      "misc2", reads=["kout", "vout", "kmout", "kin", "vin", "kmin"], writes=["kT"] + KV_ALIAS)
            vparts = [(vout_ap[0:128, :], vv[:, 0:8, :]), (vin_ap, vv[:, 8:16, :])]
            for si, di in vparts:
                P.dma("pool", lambda e, si=si, di=di: e.dma_start(out=di, in_=si.rearrange("p (a b) -> p a b", b=512)),
                      "misc2", reads=["kout", "vout", "kmout", "kin", "vin", "kmin"], writes=["vv"] + KV_ALIAS)
            mparts = [(kmout_ap[0:128, 0:16], kmT[:, :, 0:4]), (kmin_ap[:, 0:16], kmT[:, :, 4:8])]
            for si, di in mparts:
                P.dma("pool", lambda e, si=si, di=di: e.dma_start(out=di, in_=si.rearrange("p (a b) -> p a b", b=4)),
                      "misc2", reads=["kout", "vout", "kmout", "kin", "vin", "kmin"], writes=["kmT"] + (A_TABLE_TOKS if layer == 2 else []))
        for typ, u in units:
            slot = load_unit_cols(w_qkv, u * 512)
            if typ in ("q", "qm"):
                base = (u * 4) if typ == "q" else 12
                for j in range(4):
                    proj_feat(slot, j, qT, base + j, 0, SC, "qT")
            elif typ == "k":
                for j in range(4):
                    proj_feat(slot, j, kT, j, 128, 1.0, "kTt")
                    b = next_bank()
                    pairs = [(ring[slot][:, cc, j * 128:(j + 1) * 128], hTh[:, cc, :]) for cc in range(16)]
                    mm_group(ps[:, b, 0:128], pairs, reads=[("ring", slot), "hTh"], writes=[("ps", b)])
                    evac_act(kT[:, j, 0:128], ps[:, b, 0:128], 1.0, reads=[("ps", b)], writes=[("kTh", j)])
            elif typ == "v":
                for t in range(-1, NT):
                    b = next_bank()
                    if t < 0:
                        pairs = [(hTh[:, cc, :], ring[slot][:, cc, :]) for cc in range(16)]
                        rd = ["hTh"]
                    else:
                        pairs = [(hT[:, cc, t * 128:(t + 1) * 128], ring[slot][:, cc, :]) for cc in range(16)]
                        rd = [("hT", t)]
                    mm_group(ps[:, b, :], pairs, reads=[("ring", slot)] + rd, writes=[("ps", b)])
                    evac_act(vv[:, t + 1, :], ps[:, b, :], 1.0, reads=[("ps", b)], writes=[("vvt", t + 1)])
        P.barrier()
        P.dma("pool", lambda e: e.dma_start(out=memb, in_=mem.rearrange("(n p) d -> p n d", p=128)), "misc2",
              writes=["memb"] + hT_all)
        for mt in range(2):
            srcs = [(pb6[:, cc * 128:(cc + 1) * 128], memb[:, mt, cc * 128:(cc + 1) * 128]) for cc in range(16)]
            transposes(srcs, reads=["memb"], writes=[("ps", 6), ("ps", 7)])
            P.op("act", lambda e, mt=mt: e.activation(out=memT[:, :, mt * 128:(mt + 1) * 128],
                                                      in_=pb6.rearrange("p (a b) -> p a b", b=128), func=AF.Copy),
                 reads=[("ps", 6), ("ps", 7)], writes=["memT"])
        slot = load_unit_cols(w_mkv, 0)
        for j in range(4):
            b = next_bank()
            pairs = [(ring[slot][:, cc, j * 128:(j + 1) * 128], memT[:, cc, :]) for cc in range(16)]
            mm_group(ps[:, b, 0:256], pairs, reads=[("ring", slot), "memT"], writes=[("ps", b)])
            evac_act(mkT[:, j, :], ps[:, b, 0:256], 1.0, reads=[("ps", b)], writes=["mkT"])
        slot = load_unit_cols(w_mkv, 512)
        for mt in range(2):
            b = next_bank()
            pairs = [(memT[:, cc, mt * 128:(mt + 1) * 128], ring[slot][:, cc, :]) for cc in range(16)]
            mm_group(ps[:, b, :], pairs, reads=[("ring", slot), "memT"], writes=[("ps", b)])
            evac_act(mv[:, mt, :], ps[:, b, :], 1.0, reads=[("ps", b)], writes=["mv"])
        P.barrier()
        for hh in range(12):
            kvh = hh // 3
            for t in range(NT):
                qap = qT[:, hh, t * 128:(t + 1) * 128]
                if isA:
                    sb = cnt["s"] % 4
                    cnt["s"] += 1
                    mm_group(ps[:, sb, 0:256], [(qap, kT[:, kvh, t * 128:t * 128 + 256])],
                             reads=[("qT", hh, t), ("kTh", kvh)] + [("kTt", kvh, k) for k in range(NT)],
                             writes=[("ps", sb)])
                    dtab = dist0 if t == 0 else dist
                    P.op("dve", lambda e, dtab=dtab, sb=sb, hh=hh: e.scalar_tensor_tensor(
                        out=L[:, 0:256], in0=dtab, scalar=-SLOPES[hh], in1=ps[:, sb, 0:256], op0=ALU.mult, op1=ALU.add),
                        reads=[("ps", sb), "dist", "dist0"], writes=["L"])
                    v_aps = [(vv[:, t + k, kvh * 128:(kvh + 1) * 128], ("vvt", t + k)) for k in range(2)]
                    softmax_pv(True, hh, t, 256, None, layer * 12 + hh, v_aps, None)
                else:
                    jl = t // 2
                    nk = 1024 + (t + 1) * 128
                    nmm = (nk + 511) // 512
                    fns = []
                    for m in range(nmm):
                        w = min(512, nk - m * 512)
                        fns.append(lambda e, m=m, w=w, qap=qap, kvh=kvh: e.matmul(
                            out=ps[:, m, 0:w], lhsT=qap, rhs=kT[:, kvh, m * 512:m * 512 + w], start=True, stop=True))
                    fns.append(lambda e, qap=qap, kvh=kvh: e.matmul(out=ps[:, 5, 0:8], lhsT=qap, rhs=kmT[:, kvh, :],
                                                                    start=True, stop=True))
                    P.op("pe", fns, reads=[("qT", hh, t), "kT", "kmT"],
                         writes=[("ps", m) for m in range(nmm)] + [("ps", 5)])
                    P.op("dve", lambda e, jl=jl: e.tensor_tensor(out=gm, in0=ps[:, 5, 0:8], in1=gbias[:, jl, :], op=ALU.add),
                         reads=[("ps", 5), "gbias"], writes=["gm"])
                    P.op("dve", lambda e: e.max(out=top8, in_=gm), reads=["gm"], writes=["top8"])
                    P.op("dve", lambda e: e.tensor_scalar(out=bm, in0=gm, scalar1=top8[:, 2:3], scalar2=NEG,
                                                          op0=ALU.is_lt, op1=ALU.mult), reads=["gm", "top8"], writes=["bm"])
                    P.op("dve", lambda e, jl=jl: e.tensor_tensor(out=bm, in0=bm, in1=gbias[:, jl, :], op=ALU.add),
                         reads=["bm", "gbias"], writes=["bm"])
                    sall = ps[:, 0:4, :].rearrange("p a b -> p (a b)")
                    P.op("dve", lambda e, nk=nk, hh=hh, sall=sall: e.scalar_tensor_tensor(
                        out=L[:, 0:nk], in0=R0[:, 0:nk], scalar=SLOPES[hh], in1=sall[:, 0:nk], op0=ALU.mult, op1=ALU.add),
                        reads=[("ps", m) for m in range(nmm)] + ["R0"], writes=["L"])
                    for n in range(4 + jl):
                        P.op("pool", lambda e, n=n: e.tensor_scalar(out=L[:, n * 256:(n + 1) * 256],
                                                                    in0=L[:, n * 256:(n + 1) * 256],
                                                                    scalar1=bm[:, n:n + 1], scalar2=None, op0=ALU.add),
                             reads=["L", "bm"], writes=["L"])
                    P.op("pool", lambda e, nk=nk: e.tensor_tensor(out=L[:, nk - 128:nk], in0=L[:, nk - 128:nk], in1=caus,
                                                                  op=ALU.add), reads=["L", "caus"], writes=["L"])
                    v_aps = [(vv[:, k, kvh * 128:(kvh + 1) * 128], "vv") for k in range(nk // 128)]
                    softmax_pv(False, hh, t, nk, None, None, v_aps, None)
        for j in range(4):
            for t in range(NT):
                sb = cnt["s"] % 4
                cnt["s"] += 1
                mm_group(ps[:, sb, 0:256], [(qT[:, 12 + j, t * 128:(t + 1) * 128], mkT[:, j, :])],
                         reads=[("qT", 12 + j, t), "mkT"], writes=[("ps", sb)])
                v_aps = [(mv[:, k, j * 128:(j + 1) * 128], "mv") for k in range(2)]
                softmax_pv(isA, 12 + j, t, 256, ps[:, sb, 0:256], None, v_aps, [sb])
        P.barrier()
        load_gb(layer, 0)
        for dq in range(4):
            slot = load_unit_cols(w_o, dq * 512)
            for t in range(NT):
                b = next_bank(0, 6)
                pairs = [(qT[:, cc, t * 128:(t + 1) * 128], ring[slot][:, cc, :]) for cc in range(16)]
                mm_group(ps[:, b, :], pairs, reads=[("ring", slot)] + [("qT", s, t) for s in range(16)],
                         writes=[("ps", b)])
                P.op("dve", lambda e, t=t, dq=dq, b=b: e.scalar_tensor_tensor(
                    out=h[:, t, dq * 512:(dq + 1) * 512], in0=h[:, t, dq * 512:(dq + 1) * 512], scalar=ALPHA,
                    in1=ps[:, b, :], op0=ALU.mult, op1=ALU.add), reads=[("ps", b), ("h", t)], writes=[("h", t)])
        P.barrier()
        for t in range(NT):
            layer_norm(t, False, True, False)
        P.barrier()
        load_gb(layer, 1)
        upb = {"n": 0}
        dnb = {"n": 0}
        rt = {"n": 0}

        def ffn_up(g):
            slot = load_unit_cols(w_up, g * 512)
            ai = g % 2
            for j in range(4):
                for th in range(2):
                    b = upb["n"] % 4
                    upb["n"] += 1
                    pairs = [(ring[slot][:, cc, j * 128:(j + 1) * 128], hT[:, cc, th * 512:(th + 1) * 512])
                             for cc in range(16)]
                    mm_group(ps[:, b, :], pairs, reads=[("ring", slot)] + hT_all[th * 4:th * 4 + 4], writes=[("ps", b)])
                    ri = rt["n"] % 2
                    rt["n"] += 1
                    P.op("act", lambda e, b=b, ri=ri: e.activation(out=rtmp[ri], in_=ps[:, b, :], func=AF.Relu),
                         reads=[("ps", b)], writes=[("rtmp", ri)])
                    P.op("pool", lambda e, ri=ri, ai=ai, j=j, th=th: e.tensor_tensor(
                        out=actT[ai][:, j, th * 512:(th + 1) * 512], in0=rtmp[ri], in1=rtmp[ri], op=ALU.mult),
                        reads=[("rtmp", ri)], writes=[("actT", ai, j, th)])

        def ffn_down(g):
            slot = load_unit_rows(w_dn, g * 512)
            ai = g % 2
            for t in range(NT):
                for dq in range(4):
                    b = 4 + dnb["n"] % 4
                    dnb["n"] += 1
                    pairs = [(actT[ai][:, j, t * 128:(t + 1) * 128], ringd[slot][:, j, dq * 512:(dq + 1) * 512])
                             for j in range(4)]
                    mm_group(ps[:, b, :], pairs, reads=[("ring", slot)] + [("actT", ai, j, t // 4) for j in range(4)],
                             writes=[("ps", b)])
                    hs = h[:, t, dq * 512:(dq + 1) * 512]
                    if g == 0:
                        P.op("dve", lambda e, hs=hs, b=b: e.scalar_tensor_tensor(out=hs, in0=hs, scalar=ALPHA,
                                                                                 in1=ps[:, b, :], op0=ALU.mult, op1=ALU.add),
                             reads=[("ps", b), ("h", t)], writes=[("h", t)])
                    else:
                        P.op("dve", lambda e, hs=hs, b=b: e.tensor_tensor(out=hs, in0=hs, in1=ps[:, b, :], op=ALU.add),
                             reads=[("ps", b), ("h", t)], writes=[("h", t)])

        NG = 16
        for g in range(NG):
            ffn_up(g)
            if g > 0:
                ffn_down(g - 1)
        ffn_down(NG - 1)
        P.barrier()
        for t in range(NT):
            layer_norm(t, last, not last, layer == 0 and nlayers > 1)
        if layer == 1 and nlayers > 2:
            P.barrier()
            for u in range(2):
                slot = load_unit_cols(w_kvs, u * 512)
                if u == 0:
                    for j in range(4):
                        for th in range(2):
                            b = next_bank()
                            pairs = [(ring[slot][:, cc, j * 128:(j + 1) * 128], hT[:, cc, th * 512:(th + 1) * 512])
                                     for cc in range(16)]
                            mm_group(ps[:, b, :], pairs, reads=[("ring", slot)] + hT_all[th * 4:th * 4 + 4],
                                     writes=[("ps", b)])
                            P.op("act", lambda e, b=b: e.activation(out=kvf, in_=ps[:, b, :], func=AF.Copy),
                                 reads=[("ps", b)], writes=["kvf"])
                            co = j * 1024 + th * 512
                            P.dma("sp", lambda e, co=co: e.dma_start(out=kin_ap[:, co:co + 512], in_=kvf),
                                  "kvst", reads=["kvf"], writes=["kin"])
                            P.op("dve", lambda e, j=j, th=th: e.tensor_reduce(
                                out=kmacc[:, j, 2 * th:2 * th + 2], in_=kvf.rearrange("p (a b) -> p a b", b=256),
                                axis=AX.X, op=ALU.add), reads=["kvf"], writes=["kmacc"])
                    P.op("dve", lambda e: e.tensor_scalar(out=kmacc, in0=kmacc, scalar1=1.0 / 256.0, scalar2=None,
                                                          op0=ALU.mult), reads=["kmacc"], writes=["kmacc"])
                else:
                    for t in range(NT):
                        b = next_bank()
                        pairs = [(hT[:, cc, t * 128:(t + 1) * 128], ring[slot][:, cc, :]) for cc in range(16)]
                        mm_group(ps[:, b, :], pairs, reads=[("ring", slot), ("hT", t)], writes=[("ps", b)])
                        P.op("act", lambda e, b=b: e.activation(out=kvf, in_=ps[:, b, :], func=AF.Copy),
                             reads=[("ps", b)], writes=["kvf"])
                        co = t * 512
                        P.dma("sp", lambda e, co=co: e.dma_start(out=vin_ap[:, co:co + 512], in_=kvf), "kvst",
                              reads=["kvf"], writes=["vin"])
            P.op("dve", lambda e: e.memset(kvf[:, 0:256], 0.0), writes=["kvf"])
            P.op("dve", lambda e: e.tensor_copy(out=kvf[:, 0:16].rearrange("p (a b) -> p a b", b=4), in_=kmacc),
                 reads=["kmacc", "kvf"], writes=["kvf"])
            P.dma("sp", lambda e: e.dma_start(out=kmin_ap, in_=kvf[:, 0:256]), "kvst2", reads=["kvf"], writes=["kmin"])
            for nm, ti, to in (("kx", kin, kout), ("vx", vin, vout), ("kmx", kmin, kmout)):
                P.special("pool", lambda e, ti=ti, to=to: e.collective_compute(
                    "AllGather", ALU.bypass, replica_groups=groups, ins=[ti.ap().opt()], outs=[to.ap().opt()]),
                    nm, reads=[ti.name], writes=[to.name])
    P.emit()
    return nc


_CACHE = {}
_DBG = None


def _prep(x, mem, w_in_a, sinks_a, w_q_b, w_kv_shared, w_mem_kv, w_o, w_up, w_down, ln_g, ln_b, ncores=8):
    f = lambda a: np.ascontiguousarray(np.asarray(a, dtype=np.float32))
    x = f(x); mem = f(mem)
    shared = {"w_in_a": f(w_in_a), "sinks": f(sinks_a).reshape(1, 24), "w_q_b": f(w_q_b), "w_kvs": f(w_kv_shared),
              "w_mem_kv": f(w_mem_kv), "w_o": f(w_o), "w_up": f(w_up), "w_dn": f(w_down), "lng": f(ln_g), "lnb": f(ln_b)}
    zeros_halo = np.zeros((128, 2048), np.float32)
    in_maps = []
    for c in range(ncores):
        b, hf = c // 2, c % 2
        m = dict(shared)
        m["x_own"] = np.ascontiguousarray(x[b, hf * 1024:(hf + 1) * 1024])
        m["x_halo"] = np.ascontiguousarray(x[b, 896:1024]) if hf == 1 else zeros_halo
        m["halobig"] = np.full((128, 1), 0.0 if hf == 1 else BIG, np.float32)
        m["pastmask"] = np.full((128, 1), 0.0 if hf == 1 else NEG, np.float32)
        m["mem"] = np.ascontiguousarray(mem[b])
        in_maps.append(m)
    return in_maps


def kernel(x, mem, w_in_a, sinks_a, w_q_b, w_kv_shared, w_mem_kv, w_o, w_up, w_down, ln_g, ln_b):
    ncores = 8
    if "nc" not in _CACHE:
        _CACHE["nc"] = build_fused(4, ncores)
    nc = _CACHE["nc"]
    in_maps = _prep(x, mem, w_in_a, sinks_a, w_q_b, w_kv_shared, w_mem_kv, w_o, w_up, w_down, ln_g, ln_b, ncores)
    res = run_bass_kernel_spmd(nc, in_maps, core_ids=list(range(ncores)))
    out = np.empty((4, 2048, 2048), np.float32)
    for c in range(ncores):
        b, hf = c // 2, c % 2
        out[b, hf * 1024:(hf + 1) * 1024] = res.results[c]["hout"]
    return out
```

```python
import numpy as np
import concourse.bass as bass
import concourse.mybir as mybir
from concourse.bass_utils import run_bass_kernel_spmd

F32 = mybir.dt.float32
BF16 = mybir.dt.bfloat16
U8 = mybir.dt.uint8
AF = mybir.ActivationFunctionType
ALU = mybir.AluOpType
AX = mybir.AxisListType

NEG = -30000.0
BIG = 1.0e6
SC = 128.0 ** -0.5
ALPHA = 8.0 ** 0.25
EPS = 1e-5
SLOPES = [2.0 ** (-8.0 * (i + 1) / 12.0) for i in range(12)]
NT = 8
TOK = 1024


class Prog:
    def __init__(self, nc):
        self.nc = nc
        self.ops = {e: [] for e in ("pe", "act", "dve", "pool", "sp")}
        self.cnt = {}
        self.seen = {e: {} for e in self.ops}
        self.lastw = {}
        self.readers = {}

    def _deps(self, eng, reads, writes):
        need = {}

        def add(d):
            if d is None:
                return
            s, v = d
            if v > need.get(s, 0):
                need[s] = v
        for t in reads:
            add(self.lastw.get(t))
        for t in writes:
            add(self.lastw.get(t))
            for s, v in self.readers.get(t, {}).items():
                add((s, v))
        waits = []
        for s, v in need.items():
            if self.seen[eng].get(s, 0) < v:
                self.seen[eng][s] = v
                waits.append((s, v))
        return waits

    def _commit(self, stamp, reads, writes):
        s, v = stamp
        for t in reads:
            self.readers.setdefault(t, {})[s] = v
        for t in writes:
            self.lastw[t] = stamp
            self.readers[t] = {}

    def op(self, eng, fns, reads=(), writes=()):
        waits = self._deps(eng, reads, writes)
        self.cnt[eng] = self.cnt.get(eng, 0) + 1
        self._commit((eng, self.cnt[eng]), reads, writes)
        if not isinstance(fns, (list, tuple)):
            fns = [fns]
        self.ops[eng].append((waits, list(fns), eng, 1))

    def dma(self, queue, fn, sem, reads=(), writes=()):
        waits = self._deps(queue, reads, writes)
        key = "d_" + sem
        prev = self.cnt.get(key, 0)
        if prev > 0 and self.seen[queue].get(key, 0) < prev:
            self.seen[queue][key] = prev
            waits.append((key, prev))
        self.cnt[key] = prev + 16
        self._commit((key, self.cnt[key]), reads, writes)
        self.ops[queue].append((waits, [fn], key, 16))

    def special(self, queue, fn, sem, reads=(), writes=()):
        waits = self._deps(queue, reads, writes)
        key = "c_" + sem
        assert key not in self.cnt
        self.cnt[key] = 1
        self._commit((key, 1), reads, writes)
        self.ops[queue].append((waits, [fn], key, None))

    def barrier(self, engs=("pe", "act", "dve", "sp")):
        for e in engs:
            waits = []
            for s, v in self.cnt.items():
                if self.seen[e].get(s, 0) < v:
                    self.seen[e][s] = v
                    waits.append((s, v))
            if waits:
                self.ops[e].append((waits, [], None, 0))

    def emit(self, final_waits_engine="sp"):
        nc = self.nc
        fw = [(s, v) for s, v in self.cnt.items()]
        self.ops[final_waits_engine].append((fw, [], None, 0))
        import contextlib
        with contextlib.ExitStack() as st:
            sems = {}
            for s in self.cnt:
                sems[s] = st.enter_context(nc.semaphore("s_" + s))
            block = st.enter_context(nc.Block())
            engmap = {"pe": block.tensor, "act": block.scalar, "dve": block.vector,
                      "pool": block.gpsimd, "sp": block.sync}
            for ename, deco in engmap.items():
                ops = self.ops[ename]

                def body(e, ops=ops):
                    for waits, fns, incsem, incv in ops:
                        for s, v in waits:
                            e.wait_ge(sems[s], v)
                        ins = None
                        for f in fns:
                            ins = f(e)
                        if ins is not None and incsem is not None:
                            if incv is None:
                                ins.then_inc(sems[incsem])
                            else:
                                ins.then_inc(sems[incsem], incv)
                deco(body)


def build_fused(nlayers=4, ncores=8):
    nc = bass.Bass("TRN2", target_bir_lowering=False)
    P = Prog(nc)
    groups = [[2 * i, 2 * i + 1] for i in range(ncores // 2)]

    def din(name, shape):
        return nc.dram_tensor(name, shape, F32, kind="ExternalInput").ap()

    xin = din("x_own", [TOK, 2048])
    xhalo = din("x_halo", [128, 2048])
    halobig = din("halobig", [128, 1])
    pastmask = din("pastmask", [128, 1])
    mem = din("mem", [256, 2048])
    w_in_a = din("w_in_a", [2, 2048, 3072])
    sinks = din("sinks", [1, 24])
    w_q_b = din("w_q_b", [2, 2048, 2048])
    w_kvs = din("w_kvs", [2048, 1024])
    w_mem_kv = din("w_mem_kv", [4, 2048, 1024])
    w_o_all = din("w_o", [4, 2048, 2048])
    w_up_all = din("w_up", [4, 2048, 8192])
    w_dn_all = din("w_dn", [4, 8192, 2048])
    lng_all = din("lng", [4, 2, 2048])
    lnb_all = din("lnb", [4, 2, 2048])
    hout = nc.dram_tensor("hout", [TOK, 2048], F32, kind="ExternalOutput").ap()
    halo_in = nc.dram_tensor("halo_in", [128, 2048], F32)
    halo_out = nc.dram_tensor("halo_out", [256, 2048], F32)
    kin = nc.dram_tensor("kin", [128, 4096], F32)
    kout = nc.dram_tensor("kout", [256, 4096], F32)
    vin = nc.dram_tensor("vin", [128, 4096], F32)
    vout = nc.dram_tensor("vout", [256, 4096], F32)
    kmin = nc.dram_tensor("kmin", [128, 256], F32)
    kmout = nc.dram_tensor("kmout", [256, 256], F32)
    kin_ap, kout_ap, vin_ap, vout_ap, kmin_ap, kmout_ap = (kin.ap(), kout.ap(), vin.ap(), vout.ap(), kmin.ap(),
                                                            kmout.ap())

    ARENA = 212480
    arena = nc.alloc_sbuf_tensor("arena", [128, ARENA], U8)
    ps = nc.alloc_psum_tensor("ps", [128, 8, 512], F32)

    def view(off, shape, dt):
        n = 1
        for s in shape:
            n *= s
        nb = n * (4 if dt == F32 else 2)
        a = arena[:, off:off + nb].bitcast(dt)
        if len(shape) == 2:
            return a.rearrange("p (a b) -> p a b", b=shape[1])
        return a

    OH, ORING, OA, OB, OKV, OC = 0, 65536, 98304, 131072, 163840, 196608
    h = view(OH, [8, 2048], F32)
    ring = [view(ORING + i * 16384, [16, 512], BF16) for i in range(2)]
    ringd = [view(ORING + i * 16384, [4, 2048], BF16) for i in range(2)]
    hT = view(OA, [16, 1024], BF16)
    L = view(OA, [2048], F32)
    memb = view(OA, [2, 2048], BF16)
    memT = view(OA + 8192, [16, 256], BF16)
    Pn = view(OA + 8192, [2048], BF16)
    PT = [view(OA + 12288 + i * 4096, [16, 128], BF16) for i in range(2)]
    qT = view(OB, [16, 1024], BF16)
    actT = [view(OB + i * 8192, [4, 1024], BF16) for i in range(2)]
    rtmp = [view(OB + 16384 + i * 2048, [512], F32) for i in range(2)]
    kmacc = view(OB, [4, 4], F32)
    kT = view(OKV, [4, 2048], BF16)
    vv = view(OKV + 16384, [16, 512], BF16)
    gtab = view(OKV, [2048], F32)
    btab = view(OKV + 8192, [2048], F32)
    hb = [view(OKV + 16384 + i * 4096, [2048], BF16) for i in range(2)]
    kvf = view(OKV + 24576, [512], F32)
    c = OC
    mkT = view(c, [4, 256], BF16); c += 2048
    mv = view(c, [2, 512], BF16); c += 2048
    ident = view(c, [128], BF16); c += 256
    identf = view(c, [128], F32); c += 512
    ca = c
    hTh = view(ca, [16, 128], BF16); ca += 4096
    hhb = view(ca, [2048], BF16); ca += 4096
    dist = view(ca, [256], F32); ca += 1024
    dist0 = view(ca, [256], F32); ca += 1024
    sinkb = view(ca, [24], F32); ca += 96
    halob = view(ca, [1], F32); ca += 32
    cb = c
    R0 = view(cb, [2048], F32); cb += 8192
    caus = view(cb, [128], F32); cb += 512
    kmT = view(cb, [4, 8], BF16); cb += 64
    gbias = view(cb, [4, 8], F32); cb += 128
    pmask = view(cb, [1], F32); cb += 32
    gm = view(cb, [8], F32); cb += 32
    top8 = view(cb, [8], F32); cb += 32
    bm = view(cb, [8], F32); cb += 32
    c = max(ca, cb)
    stt = view(c, [4, 6], F32); c += 96
    mv2 = view(c, [2], F32); c += 32
    sm = view(c, [16], F32); c += 64
    epsb = view(c, [1], F32); c += 32
    assert c <= ARENA, c
    SM_MX, SM_NEGM, SM_RS, SM_ES, SM_DEN, SM_RDEN, SM_SD, SM_RSTD, SM_NMR = range(9)
    A_TABLE_TOKS = ["hTh", "hhb", "dist", "dist0", "sinkb", "halob"]

    def smc(i):
        return sm[:, i:i + 1]

    def psb(b):
        return ps[:, b, :].bitcast(BF16)

    P.op("pool", lambda e: e.memset(identf, 0.0), writes=["identf"])
    P.op("pool", lambda e: e.affine_select(out=identf, in_=identf, pattern=[[-1, 128]], base=0,
                                           channel_multiplier=1, compare_op=ALU.not_equal, fill=1.0),
         reads=["identf"], writes=["identf"])
    P.op("pool", lambda e: e.tensor_copy(out=ident, in_=identf), reads=["identf"], writes=["ident"])
    P.op("pool", lambda e: e.memset(epsb, EPS), writes=["epsb"])

    def setup_A():
        P.op("pool", lambda e: e.iota(dist, pattern=[[-1, 256]], base=128, channel_multiplier=1,
                                      allow_small_or_imprecise_dtypes=True), writes=["dist"])
        P.op("pool", lambda e: e.affine_select(out=dist, in_=dist, pattern=[[-1, 256]], base=128,
                                               channel_multiplier=1, compare_op=ALU.is_ge, fill=BIG),
             reads=["dist"], writes=["dist"])
        P.op("pool", lambda e: e.affine_select(out=dist, in_=dist, pattern=[[1, 256]], base=-1,
                                               channel_multiplier=-1, compare_op=ALU.is_ge, fill=BIG),
             reads=["dist"], writes=["dist"])
        P.dma("sp", lambda e: e.dma_start(out=halob, in_=halobig), "misc", writes=["halob"])
        P.dma("sp", lambda e: e.dma_start(out=sinkb, in_=sinks.partition_broadcast(128)), "misc", writes=["sinkb"])
        P.op("pool", lambda e: e.tensor_copy(out=dist0, in_=dist), reads=["dist"], writes=["dist0"])
        P.op("pool", lambda e: e.tensor_scalar(out=dist0[:, 0:128], in0=dist0[:, 0:128], scalar1=halob[:, 0:1],
                                               scalar2=None, op0=ALU.max),
             reads=["dist0", "halob"], writes=["dist0"])

    def setup_B():
        P.op("pool", lambda e: e.iota(R0, pattern=[[1, 2048]], base=0, channel_multiplier=0,
                                      allow_small_or_imprecise_dtypes=True), writes=["R0"] + A_TABLE_TOKS)
        P.op("pool", lambda e: e.memset(caus, 0.0), writes=["caus"] + A_TABLE_TOKS)
        P.op("pool", lambda e: e.affine_select(out=caus, in_=caus, pattern=[[-1, 128]], base=0,
                                               channel_multiplier=1, compare_op=ALU.is_ge, fill=NEG),
             reads=["caus"], writes=["caus"])
        P.dma("sp", lambda e: e.dma_start(out=pmask, in_=pastmask), "misc", writes=["pmask"] + A_TABLE_TOKS)
        P.op("pool", lambda e: e.memset(gbias, NEG), writes=["gbias"] + A_TABLE_TOKS)
        for jl in range(4):
            P.op("pool", lambda e, jl=jl: e.memset(gbias[:, jl, 0:4 + jl], 0.0), reads=["gbias"], writes=["gbias"])
            P.op("pool", lambda e, jl=jl: e.tensor_scalar(out=gbias[:, jl, 0:4], in0=gbias[:, jl, 0:4],
                                                          scalar1=pmask[:, 0:1], scalar2=None, op0=ALU.add),
                 reads=["pmask", "gbias"], writes=["gbias"])

    ring_state = {"n": 0}

    def load_unit_cols(W, c0):
        i = ring_state["n"] % 2
        ring_state["n"] += 1
        src = W.rearrange("(c p) f -> p c f", p=128)[:, :, c0:c0 + 512]
        P.dma("pool", lambda e: e.dma_start(out=ring[i], in_=src), "ring%d" % i, writes=[("ring", i)])
        return i

    def load_unit_rows(W, r0):
        i = ring_state["n"] % 2
        ring_state["n"] += 1
        src = W[r0:r0 + 512, :].rearrange("(c p) d -> p c d", p=128)
        P.dma("pool", lambda e: e.dma_start(out=ringd[i], in_=src), "ring%d" % i, writes=[("ring", i)])
        return i

    bank_rr = {"n": 0}

    def mm_group(out_ap, pairs, reads, writes):
        n = len(pairs)
        fns = []
        for i, (l, r) in enumerate(pairs):
            fns.append(lambda e, l=l, r=r, i=i: e.matmul(out=out_ap, lhsT=l, rhs=r, start=(i == 0), stop=(i == n - 1)))
        P.op("pe", fns, reads=reads, writes=writes)

    def transposes(srcs, reads, writes):
        fns = [lambda e, d=d, s=s: e.transpose(out=d, in_=s, identity=ident) for d, s in srcs]
        P.op("pe", fns, reads=list(reads) + ["ident"], writes=writes)

    pb6 = ps[:, 6:8, :].bitcast(BF16).rearrange("p a b -> p (a b)")

    def to_hT(src_ap, src_tok, dst, dst_cols, dst_tok):
        srcs = [(pb6[:, cc * 128:(cc + 1) * 128], src_ap[:, cc * 128:(cc + 1) * 128]) for cc in range(16)]
        transposes(srcs, reads=[src_tok], writes=[("ps", 6), ("ps", 7)])
        P.op("act", lambda e: e.activation(out=dst[:, :, dst_cols], in_=pb6.rearrange("p (a b) -> p a b", b=128),
                                           func=AF.Copy),
             reads=[("ps", 6), ("ps", 7)], writes=[dst_tok])

    hT_all = [("hT", t) for t in range(NT)]

    def evac_act(out_ap, in_ap, scale, reads, writes):
        P.op("act", lambda e: e.activation(out=out_ap, in_=in_ap, func=AF.Copy, scale=scale), reads=reads, writes=writes)

    def next_bank(lo=0, hi=6):
        b = lo + bank_rr["n"] % (hi - lo)
        bank_rr["n"] += 1
        return b

    def proj_feat(slot, j, dst, dst_slot, col_off, scale, tokname):
        for th in range(2):
            b = next_bank()
            pairs = [(ring[slot][:, cc, j * 128:(j + 1) * 128], hT[:, cc, th * 512:(th + 1) * 512]) for cc in range(16)]
            mm_group(ps[:, b, :], pairs, reads=[("ring", slot)] + hT_all[th * 4:th * 4 + 4], writes=[("ps", b)])
            evac_act(dst[:, dst_slot, col_off + th * 512: col_off + (th + 1) * 512], ps[:, b, :], scale,
                     reads=[("ps", b)], writes=[(tokname, dst_slot, 4 * th + k) for k in range(4)])

    cnt = {"ot": 0, "pt": 0, "s": 0}
    KV_ALIAS = ["gtab", "btab", ("hb", 0), ("hb", 1), "kvf"]

    def softmax_pv(isA, h_slot, t, nk, s_psum, sink_col, v_aps, s_banks):
        src = L[:, 0:nk] if s_psum is None else s_psum
        src_reads = ["L"] if s_psum is None else [("ps", b) for b in s_banks]
        P.op("dve", lambda e: e.reduce_max(out=smc(SM_MX), in_=src, axis=AX.X), reads=src_reads, writes=["mx"])
        if sink_col is not None:
            P.op("dve", lambda e: e.tensor_scalar(out=smc(SM_NEGM), in0=smc(SM_MX), scalar1=sinkb[:, sink_col:sink_col + 1],
                                                  scalar2=-1.0, op0=ALU.max, op1=ALU.mult),
                 reads=["mx", "sinkb"], writes=["negm"])
        else:
            P.op("dve", lambda e: e.tensor_scalar(out=smc(SM_NEGM), in0=smc(SM_MX), scalar1=-1.0, scalar2=None,
                                                  op0=ALU.mult), reads=["mx"], writes=["negm"])
        P.op("act", lambda e: e.activation(out=L[:, 0:nk], in_=src, func=AF.Exp, bias=smc(SM_NEGM), scale=1.0,
                                           accum_out=smc(SM_RS)),
             reads=src_reads + ["negm"], writes=["L", "rs"])
        if sink_col is not None:
            P.op("act", lambda e: e.activation(out=smc(SM_ES), in_=sinkb[:, sink_col:sink_col + 1], func=AF.Exp,
                                               bias=smc(SM_NEGM), scale=1.0), reads=["negm", "sinkb"], writes=["es"])
            P.op("dve", lambda e: e.tensor_tensor(out=smc(SM_DEN), in0=smc(SM_RS), in1=smc(SM_ES), op=ALU.add),
                 reads=["rs", "es"], writes=["den"])
            P.op("dve", lambda e: e.reciprocal(out=smc(SM_RDEN), in_=smc(SM_DEN)), reads=["den"], writes=["rden"])
        else:
            P.op("dve", lambda e: e.reciprocal(out=smc(SM_RDEN), in_=smc(SM_RS)), reads=["rs"], writes=["rden"])
        P.op("dve", lambda e: e.tensor_scalar(out=Pn[:, 0:nk], in0=L[:, 0:nk], scalar1=smc(SM_RDEN), scalar2=None,
                                              op0=ALU.mult), reads=["L", "rden"], writes=["Pn"])
        nb = nk // 128
        pi = cnt["pt"] % 2
        cnt["pt"] += 1
        if nb <= 8:
            pbank = [6 + pi]
            pdst = psb(6 + pi)
        else:
            pbank = [6, 7]
            pdst = pb6
        srcs = [(pdst[:, k * 128:(k + 1) * 128], Pn[:, k * 128:(k + 1) * 128]) for k in range(nb)]
        transposes(srcs, reads=["Pn"], writes=[("ps", b) for b in pbank])
        P.op("act", lambda e: e.activation(out=PT[pi][:, 0:nb, :],
                                           in_=pdst[:, 0:nb * 128].rearrange("p (a b) -> p a b", b=128), func=AF.Copy),
             reads=[("ps", b) for b in pbank], writes=[("PT", pi)])
        ob = (4 + cnt["ot"] % 2) if isA else 4
        cnt["ot"] += 1
        pairs = [(v_aps[k][0], PT[pi][:, k, :]) for k in range(nb)]
        vreads = sorted(set(v_aps[k][1] for k in range(nb)), key=str)
        mm_group(ps[:, ob, 0:128], pairs, reads=[("PT", pi)] + vreads, writes=[("ps", ob)])
        P.op("act", lambda e: e.activation(out=qT[:, h_slot, t * 128:(t + 1) * 128], in_=ps[:, ob, 0:128], func=AF.Copy),
             reads=[("ps", ob)], writes=[("qT", h_slot, t)])

    def load_gb(layer, i):
        P.dma("sp", lambda e: e.dma_start(out=gtab, in_=lng_all[layer, i:i + 1, :].partition_broadcast(128)), "misc",
              writes=["gtab"])
        P.dma("sp", lambda e: e.dma_start(out=btab, in_=lnb_all[layer, i:i + 1, :].partition_broadcast(128)), "misc",
              writes=["btab"])

    hout_v = hout.rearrange("(n p) d -> p n d", p=128)

    def layer_norm(t, store, make_hT, halo_send):
        for q4 in range(4):
            P.op("dve", lambda e, q4=q4: e.bn_stats(out=stt[:, q4, :], in_=h[:, t, q4 * 512:(q4 + 1) * 512]),
                 reads=[("h", t)], writes=[("stt", q4)])
        P.op("dve", lambda e: e.bn_aggr(out=mv2, in_=stt.rearrange("p a b -> p (a b)")),
             reads=[("stt", q4) for q4 in range(4)], writes=["mv2"])
        P.op("act", lambda e: e.activation(out=smc(SM_SD), in_=mv2[:, 1:2], func=AF.Sqrt, bias=epsb[:, 0:1], scale=1.0),
             reads=["mv2", "epsb"], writes=["sd"])
        P.op("dve", lambda e: e.reciprocal(out=smc(SM_RSTD), in_=smc(SM_SD)), reads=["sd"], writes=["rstd"])
        P.op("dve", lambda e: e.scalar_tensor_tensor(out=smc(SM_NMR), in0=mv2[:, 0:1], scalar=-1.0, in1=smc(SM_RSTD),
                                                     op0=ALU.mult, op1=ALU.mult), reads=["mv2", "rstd"], writes=["nmr"])
        P.op("act", lambda e: e.activation(out=h[:, t, :], in_=h[:, t, :], func=AF.Identity, scale=smc(SM_RSTD),
                                           bias=smc(SM_NMR)), reads=[("h", t), "rstd", "nmr"], writes=[("h", t)])
        P.op("pool", lambda e: e.tensor_tensor(out=h[:, t, :], in0=h[:, t, :], in1=gtab, op=ALU.mult),
             reads=[("h", t), "gtab"], writes=[("h", t)])
        P.op("dve", lambda e: e.tensor_tensor(out=h[:, t, :], in0=h[:, t, :], in1=btab, op=ALU.add),
             reads=[("h", t), "btab"], writes=[("h", t)])
        if store:
            P.dma("sp", lambda e: e.dma_start(out=hout_v[:, t, :], in_=h[:, t, :]), "hstore%d" % (t % 4), reads=[("h", t)])
        if make_hT:
            i = t % 2
            P.op("act", lambda e: e.activation(out=hb[i], in_=h[:, t, :], func=AF.Copy),
                 reads=[("h", t)], writes=[("hb", i)])
            to_hT(hb[i], ("hb", i), hT, slice(t * 128, (t + 1) * 128), ("hT", t))
            if halo_send and t == NT - 1:
                P.dma("sp", lambda e: e.dma_start(out=halo_in.ap(), in_=h[:, t, :]), "halo", reads=[("h", t)],
                      writes=["halo_in"])
                P.special("pool", lambda e: e.collective_compute(
                    "AllGather", ALU.bypass, replica_groups=groups, ins=[halo_in.ap().opt()],
                    outs=[halo_out.ap().opt()]), "halo", reads=["halo_in"], writes=["halo_out"])

    xin_v = xin.rearrange("(n p) d -> p n d", p=128)
    for t in range(NT):
        P.dma("sp", lambda e, t=t: e.dma_start(out=h[:, t, :], in_=xin_v[:, t, :]), "hload%d" % (t % 4), writes=[("h", t)])
    for t in range(NT):
        i = t % 2
        P.op("act", lambda e, t=t, i=i: e.activation(out=hb[i], in_=h[:, t, :], func=AF.Copy),
             reads=[("h", t)], writes=[("hb", i)])
        to_hT(hb[i], ("hb", i), hT, slice(t * 128, (t + 1) * 128), ("hT", t))
    setup_A()

    for layer in range(nlayers):
        isA = layer < 2
        last = layer == nlayers - 1
        w_mkv = w_mem_kv[layer]
        w_o = w_o_all[layer]
        w_up = w_up_all[layer]
        w_dn = w_dn_all[layer]
        if layer > 0:
            P.barrier()
        if isA:
            w_qkv = w_in_a[layer]
            if layer == 0:
                P.dma("pool", lambda e: e.dma_start(out=hhb, in_=xhalo), "misc2", writes=["hhb"])
            else:
                P.dma("pool", lambda e: e.dma_start(out=hhb, in_=halo_out.ap()[0:128, :]), "misc2", reads=["halo_out"],
                      writes=["hhb"])
            to_hT(hhb, "hhb", hTh, slice(0, 128), "hTh")
            units = [("q", 0), ("q", 1), ("q", 2), ("k", 3), ("v", 4), ("qm", 5)]
        else:
            w_qkv = w_q_b[layer - 2]
            if layer == 2:
                setup_B()
            units = [("q", 0), ("q", 1), ("q", 2), ("qm", 3)]
            kparts = [(kout_ap[0:128, :], kT[:, :, 0:1024]), (kin_ap, kT[:, :, 1024:2048])]
            for si, di in kparts:
                P.dma("pool", lambda e, si=si, di=di: e.dma_start(out=di, in_=si.rearrange("p (a b) -> p a b", b=1024)),
                      "misc2", reads=["kout", "vout", "kmout", "kin", "vin", "kmin"], writes=["kT"] + KV_ALIAS)
            vparts = [(vout_ap[0:128, :], vv[:, 0:8, :]), (vin_ap, vv[:, 8:16, :])]
            for si, di in vparts:
                P.dma("pool", lambda e, si=si, di=di: e.dma_start(out=di, in_=si.rearrange("p (a b) -> p a b", b=512)),
                      "misc2", reads=["kout", "vout", "kmout", "kin", "vin", "kmin"], writes=["vv"] + KV_ALIAS)
            mparts = [(kmout_ap[0:128, 0:16], kmT[:, :, 0:4]), (kmin_ap[:, 0:16], kmT[:, :, 4:8])]
            for si, di in mparts:
                P.dma("pool", lambda e, si=si, di=di: e.dma_start(out=di, in_=si.rearrange("p (a b) -> p a b", b=4)),
                      "misc2", reads=["kout", "vout", "kmout", "kin", "vin", "kmin"], writes=["kmT"] + (A_TABLE_TOKS if layer == 2 else []))
        for typ, u in units:
            slot = load_unit_cols(w_qkv, u * 512)
            if typ in ("q", "qm"):
                base = (u * 4) if typ == "q" else 12
                for j in range(4):
                    proj_feat(slot, j, qT, base + j, 0, SC, "qT")
            elif typ == "k":
                for j in range(4):
                    proj_feat(slot, j, kT, j, 128, 1.0, "kTt")
                    b = next_bank()
                    pairs = [(ring[slot][:, cc, j * 128:(j + 1) * 128], hTh[:, cc, :]) for cc in range(16)]
                    mm_group(ps[:, b, 0:128], pairs, reads=[("ring", slot), "hTh"], writes=[("ps", b)])
                    evac_act(kT[:, j, 0:128], ps[:, b, 0:128], 1.0, reads=[("ps", b)], writes=[("kTh", j)])
            elif typ == "v":
                for t in range(-1, NT):
                    b = next_bank()
                    if t < 0:
                        pairs = [(hTh[:, cc, :], ring[slot][:, cc, :]) for cc in range(16)]
                        rd = ["hTh"]
                    else:
                        pairs = [(hT[:, cc, t * 128:(t + 1) * 128], ring[slot][:, cc, :]) for cc in range(16)]
                        rd = [("hT", t)]
                    mm_group(ps[:, b, :], pairs, reads=[("ring", slot)] + rd, writes=[("ps", b)])
                    evac_act(vv[:, t + 1, :], ps[:, b, :], 1.0, reads=[("ps", b)], writes=[("vvt", t + 1)])
        P.barrier()
        P.dma("pool", lambda e: e.dma_start(out=memb, in_=mem.rearrange("(n p) d -> p n d", p=128)), "misc2",
              writes=["memb"] + hT_all)
        for mt in range(2):
            srcs = [(pb6[:, cc * 128:(cc + 1) * 128], memb[:, mt, cc * 128:(cc + 1) * 128]) for cc in range(16)]
            transposes(srcs, reads=["memb"], writes=[("ps", 6), ("ps", 7)])
            P.op("act", lambda e, mt=mt: e.activation(out=memT[:, :, mt * 128:(mt + 1) * 128],
                                                      in_=pb6.rearrange("p (a b) -> p a b", b=128), func=AF.Copy),
                 reads=[("ps", 6), ("ps", 7)], writes=["memT"])
        slot = load_unit_cols(w_mkv, 0)
        for j in range(4):
            b = next_bank()
            pairs = [(ring[slot][:, cc, j * 128:(j + 1) * 128], memT[:, cc, :]) for cc in range(16)]
            mm_group(ps[:, b, 0:256], pairs, reads=[("ring", slot), "memT"], writes=[("ps", b)])
            evac_act(mkT[:, j, :], ps[:, b, 0:256], 1.0, reads=[("ps", b)], writes=["mkT"])
        slot = load_unit_cols(w_mkv, 512)
        for mt in range(2):
            b = next_bank()
            pairs = [(memT[:, cc, mt * 128:(mt + 1) * 128], ring[slot][:, cc, :]) for cc in range(16)]
            mm_group(ps[:, b, :], pairs, reads=[("ring", slot), "memT"], writes=[("ps", b)])
            evac_act(mv[:, mt, :], ps[:, b, :], 1.0, reads=[("ps", b)], writes=["mv"])
        P.barrier()
        for hh in range(12):
            kvh = hh // 3
            for t in range(NT):
                qap = qT[:, hh, t * 128:(t + 1) * 128]
                if isA:
                    sb = cnt["s"] % 4
                    cnt["s"] += 1
                    mm_group(ps[:, sb, 0:256], [(qap, kT[:, kvh, t * 128:t * 128 + 256])],
                             reads=[("qT", hh, t), ("kTh", kvh)] + [("kTt", kvh, k) for k in range(NT)],
                             writes=[("ps", sb)])
                    dtab = dist0 if t == 0 else dist
                    P.op("dve", lambda e, dtab=dtab, sb=sb, hh=hh: e.scalar_tensor_tensor(
                        out=L[:, 0:256], in0=dtab, scalar=-SLOPES[hh], in1=ps[:, sb, 0:256], op0=ALU.mult, op1=ALU.add),
                        reads=[("ps", sb), "dist", "dist0"], writes=["L"])
                    v_aps = [(vv[:, t + k, kvh * 128:(kvh + 1) * 128], ("vvt", t + k)) for k in range(2)]
                    softmax_pv(True, hh, t, 256, None, layer * 12 + hh, v_aps, None)
                else:
                    jl = t // 2
                    nk = 1024 + (t + 1) * 128
                    nmm = (nk + 511) // 512
                    fns = []
                    for m in range(nmm):
                        w = min(512, nk - m * 512)
                        fns.append(lambda e, m=m, w=w, qap=qap, kvh=kvh: e.matmul(
                            out=ps[:, m, 0:w], lhsT=qap, rhs=kT[:, kvh, m * 512:m * 512 + w], start=True, stop=True))
                    fns.append(lambda e, qap=qap, kvh=kvh: e.matmul(out=ps[:, 5, 0:8], lhsT=qap, rhs=kmT[:, kvh, :],
                                                                    start=True, stop=True))
                    P.op("pe", fns, reads=[("qT", hh, t), "kT", "kmT"],
                         writes=[("ps", m) for m in range(nmm)] + [("ps", 5)])
                    P.op("dve", lambda e, jl=jl: e.tensor_tensor(out=gm, in0=ps[:, 5, 0:8], in1=gbias[:, jl, :], op=ALU.add),
                         reads=[("ps", 5), "gbias"], writes=["gm"])
                    P.op("dve", lambda e: e.max(out=top8, in_=gm), reads=["gm"], writes=["top8"])
                    P.op("dve", lambda e: e.tensor_scalar(out=bm, in0=gm, scalar1=top8[:, 2:3], scalar2=NEG,
                                                          op0=ALU.is_lt, op1=ALU.mult), reads=["gm", "top8"], writes=["bm"])
                    P.op("dve", lambda e, jl=jl: e.tensor_tensor(out=bm, in0=bm, in1=gbias[:, jl, :], op=ALU.add),
                         reads=["bm", "gbias"], writes=["bm"])
                    sall = ps[:, 0:4, :].rearrange("p a b -> p (a b)")
                    P.op("dve", lambda e, nk=nk, hh=hh, sall=sall: e.scalar_tensor_tensor(
                        out=L[:, 0:nk], in0=R0[:, 0:nk], scalar=SLOPES[hh], in1=sall[:, 0:nk], op0=ALU.mult, op1=ALU.add),
                        reads=[("ps", m) for m in range(nmm)] + ["R0"], writes=["L"])
                    for n in range(4 + jl):
                        P.op("act", lambda e, n=n: e.activation(out=L[:, n * 256:(n + 1) * 256],
                                                                in_=L[:, n * 256:(n + 1) * 256], func=AF.Identity,
                                                                bias=bm[:, n:n + 1], scale=1.0),
                             reads=["L", "bm"], writes=["L"])
                    P.op("pool", lambda e, nk=nk: e.tensor_tensor(out=L[:, nk - 128:nk], in0=L[:, nk - 128:nk], in1=caus,
                                                                  op=ALU.add), reads=["L", "caus"], writes=["L"])
                    v_aps = [(vv[:, k, kvh * 128:(kvh + 1) * 128], "vv") for k in range(nk // 128)]
                    softmax_pv(False, hh, t, nk, None, None, v_aps, None)
        for j in range(4):
            for t in range(NT):
                sb = cnt["s"] % 4
                cnt["s"] += 1
                mm_group(ps[:, sb, 0:256], [(qT[:, 12 + j, t * 128:(t + 1) * 128], mkT[:, j, :])],
                         reads=[("qT", 12 + j, t), "mkT"], writes=[("ps", sb)])
                v_aps = [(mv[:, k, j * 128:(j + 1) * 128], "mv") for k in range(2)]
                softmax_pv(isA, 12 + j, t, 256, ps[:, sb, 0:256], None, v_aps, [sb])
        P.barrier()
        load_gb(layer, 0)
        for dq in range(4):
            slot = load_unit_cols(w_o, dq * 512)
            for t in range(NT):
                b = next_bank(0, 6)
                pairs = [(qT[:, cc, t * 128:(t + 1) * 128], ring[slot][:, cc, :]) for cc in range(16)]
                mm_group(ps[:, b, :], pairs, reads=[("ring", slot)] + [("qT", s, t) for s in range(16)],
                         writes=[("ps", b)])
                P.op("dve", lambda e, t=t, dq=dq, b=b: e.scalar_tensor_tensor(
                    out=h[:, t, dq * 512:(dq + 1) * 512], in0=h[:, t, dq * 512:(dq + 1) * 512], scalar=ALPHA,
                    in1=ps[:, b, :], op0=ALU.mult, op1=ALU.add), reads=[("ps", b), ("h", t)], writes=[("h", t)])
        NG = 16
        plan = [("up", 0)]
        for g in range(1, NG):
            plan += [("up", g), ("dn", g - 1)]
        plan += [("dn", NG - 1)]
        plan_slot = {}
        plan_pos = {"n": 0}

        def plan_issue():
            k = plan_pos["n"]
            if k >= len(plan):
                return
            plan_pos["n"] += 1
            typ, g = plan[k]
            plan_slot[(typ, g)] = load_unit_cols(w_up, g * 512) if typ == "up" else load_unit_rows(w_dn, g * 512)

        plan_issue()
        plan_issue()
        P.barrier()
        for t in range(NT):
            layer_norm(t, False, True, False)
        P.barrier()
        load_gb(layer, 1)
        upb = {"n": 0}
        dnb = {"n": 0}
        rt = {"n": 0}

        def ffn_up(g):
            slot = plan_slot[("up", g)]
            ai = g % 2
            for j in range(4):
                for th in range(2):
                    b = upb["n"] % 4
                    upb["n"] += 1
                    pairs = [(ring[slot][:, cc, j * 128:(j + 1) * 128], hT[:, cc, th * 512:(th + 1) * 512])
                             for cc in range(16)]
                    mm_group(ps[:, b, :], pairs, reads=[("ring", slot)] + hT_all[th * 4:th * 4 + 4], writes=[("ps", b)])
                    ri = rt["n"] % 2
                    rt["n"] += 1
                    P.op("act", lambda e, b=b, ri=ri: e.activation(out=rtmp[ri], in_=ps[:, b, :], func=AF.Relu),
                         reads=[("ps", b)], writes=[("rtmp", ri)])
                    P.op("pool", lambda e, ri=ri, ai=ai, j=j, th=th: e.tensor_tensor(
                        out=actT[ai][:, j, th * 512:(th + 1) * 512], in0=rtmp[ri], in1=rtmp[ri], op=ALU.mult),
                        reads=[("rtmp", ri)], writes=[("actT", ai, j, th)])

        def ffn_down(g):
            slot = plan_slot[("dn", g)]
            ai = g % 2
            for t in range(NT):
                for dq in range(4):
                    b = 4 + dnb["n"] % 4
                    dnb["n"] += 1
                    pairs = [(actT[ai][:, j, t * 128:(t + 1) * 128], ringd[slot][:, j, dq * 512:(dq + 1) * 512])
                             for j in range(4)]
                    mm_group(ps[:, b, :], pairs, reads=[("ring", slot)] + [("actT", ai, j, t // 4) for j in range(4)],
                             writes=[("ps", b)])
                    hs = h[:, t, dq * 512:(dq + 1) * 512]
                    if g == 0:
                        P.op("dve", lambda e, hs=hs, b=b: e.scalar_tensor_tensor(out=hs, in0=hs, scalar=ALPHA,
                                                                                 in1=ps[:, b, :], op0=ALU.mult, op1=ALU.add),
                             reads=[("ps", b), ("h", t)], writes=[("h", t)])
                    else:
                        P.op("dve", lambda e, hs=hs, b=b: e.tensor_tensor(out=hs, in0=hs, in1=ps[:, b, :], op=ALU.add),
                             reads=[("ps", b), ("h", t)], writes=[("h", t)])

        for g in range(NG):
            ffn_up(g)
            plan_issue()
            if g > 0:
                ffn_down(g - 1)
                plan_issue()
        ffn_down(NG - 1)
        P.barrier()
        for t in range(NT):
            layer_norm(t, last, not last, layer == 0 and nlayers > 1)
        if layer == 1 and nlayers > 2:
            P.barrier()
            for u in range(2):
                slot = load_unit_cols(w_kvs, u * 512)
                if u == 0:
                    for j in range(4):
                        for th in range(2):
                            b = next_bank()
                            pairs = [(ring[slot][:, cc, j * 128:(j + 1) * 128], hT[:, cc, th * 512:(th + 1) * 512])
                                     for cc in range(16)]
                            mm_group(ps[:, b, :], pairs, reads=[("ring", slot)] + hT_all[th * 4:th * 4 + 4],
                                     writes=[("ps", b)])
                            P.op("act", lambda e, b=b: e.activation(out=kvf, in_=ps[:, b, :], func=AF.Copy),
                                 reads=[("ps", b)], writes=["kvf"])
                            co = j * 1024 + th * 512
                            P.dma("sp", lambda e, co=co: e.dma_start(out=kin_ap[:, co:co + 512], in_=kvf),
                                  "kvst", reads=["kvf"], writes=["kin"])
                            P.op("dve", lambda e, j=j, th=th: e.tensor_reduce(
                                out=kmacc[:, j, 2 * th:2 * th + 2], in_=kvf.rearrange("p (a b) -> p a b", b=256),
                                axis=AX.X, op=ALU.add), reads=["kvf"], writes=["kmacc"])
                    P.op("dve", lambda e: e.tensor_scalar(out=kmacc, in0=kmacc, scalar1=1.0 / 256.0, scalar2=None,
                                                          op0=ALU.mult), reads=["kmacc"], writes=["kmacc"])
                else:
                    for t in range(NT):
                        b = next_bank()
                        pairs = [(hT[:, cc, t * 128:(t + 1) * 128], ring[slot][:, cc, :]) for cc in range(16)]
                        mm_group(ps[:, b, :], pairs, reads=[("ring", slot), ("hT", t)], writes=[("ps", b)])
                        P.op("act", lambda e, b=b: e.activation(out=kvf, in_=ps[:, b, :], func=AF.Copy),
                             reads=[("ps", b)], writes=["kvf"])
                        co = t * 512
                        P.dma("sp", lambda e, co=co: e.dma_start(out=vin_ap[:, co:co + 512], in_=kvf), "kvst",
                              reads=["kvf"], writes=["vin"])
            P.op("dve", lambda e: e.memset(kvf[:, 0:256], 0.0), writes=["kvf"])
            P.op("dve", lambda e: e.tensor_copy(out=kvf[:, 0:16].rearrange("p (a b) -> p a b", b=4), in_=kmacc),
                 reads=["kmacc", "kvf"], writes=["kvf"])
            P.dma("sp", lambda e: e.dma_start(out=kmin_ap, in_=kvf[:, 0:256]), "kvst2", reads=["kvf"], writes=["kmin"])
            for nm, ti, to in (("kx", kin, kout), ("vx", vin, vout), ("kmx", kmin, kmout)):
                P.special("pool", lambda e, ti=ti, to=to: e.collective_compute(
                    "AllGather", ALU.bypass, replica_groups=groups, ins=[ti.ap().opt()], outs=[to.ap().opt()]),
                    nm, reads=[ti.name], writes=[to.name])
    P.emit()
    return nc


_CACHE = {}
_DBG = None


def _prep(x, mem, w_in_a, sinks_a, w_q_b, w_kv_shared, w_mem_kv, w_o, w_up, w_down, ln_g, ln_b, ncores=8):
    f = lambda a: np.ascontiguousarray(np.asarray(a, dtype=np.float32))
    x = f(x); mem = f(mem)
    shared = {"w_in_a": f(w_in_a), "sinks": f(sinks_a).reshape(1, 24), "w_q_b": f(w_q_b), "w_kvs": f(w_kv_shared),
              "w_mem_kv": f(w_mem_kv), "w_o": f(w_o), "w_up": f(w_up), "w_dn": f(w_down), "lng": f(ln_g), "lnb": f(ln_b)}
    zeros_halo = np.zeros((128, 2048), np.float32)
    in_maps = []
    for c in range(ncores):
        b, hf = c // 2, c % 2
        m = dict(shared)
        m["x_own"] = np.ascontiguousarray(x[b, hf * 1024:(hf + 1) * 1024])
        m["x_halo"] = np.ascontiguousarray(x[b, 896:1024]) if hf == 1 else zeros_halo
        m["halobig"] = np.full((128, 1), 0.0 if hf == 1 else BIG, np.float32)
        m["pastmask"] = np.full((128, 1), 0.0 if hf == 1 else NEG, np.float32)
        m["mem"] = np.ascontiguousarray(mem[b])
        in_maps.append(m)
    return in_maps


def kernel(x, mem, w_in_a, sinks_a, w_q_b, w_kv_shared, w_mem_kv, w_o, w_up, w_down, ln_g, ln_b):
    ncores = 8
    if "nc" not in _CACHE:
        _CACHE["nc"] = build_fused(4, ncores)
    nc = _CACHE["nc"]
    in_maps = _prep(x, mem, w_in_a, sinks_a, w_q_b, w_kv_shared, w_mem_kv, w_o, w_up, w_down, ln_g, ln_b, ncores)
    res = run_bass_kernel_spmd(nc, in_maps, core_ids=list(range(ncores)))
    out = np.empty((4, 2048, 2048), np.float32)
    for c in range(ncores):
        b, hf = c // 2, c % 2
        out[b, hf * 1024:(hf + 1) * 1024] = res.results[c]["hout"]
    return out
```

```python
import numpy as np
import concourse.bass as bass
import concourse.mybir as mybir
from concourse.bass_utils import run_bass_kernel_spmd

F32 = mybir.dt.float32
BF16 = mybir.dt.bfloat16
U8 = mybir.dt.uint8
AF = mybir.ActivationFunctionType
ALU = mybir.AluOpType
AX = mybir.AxisListType

NEG = -30000.0
BIG = 1.0e6
SC = 128.0 ** -0.5
ALPHA = 8.0 ** 0.25
EPS = 1e-5
SLOPES = [2.0 ** (-8.0 * (i + 1) / 12.0) for i in range(12)]
NT = 8
TOK = 1024


class Prog:
    def __init__(self, nc):
        self.nc = nc
        self.ops = {e: [] for e in ("pe", "act", "dve", "pool", "sp")}
        self.cnt = {}
        self.seen = {e: {} for e in self.ops}
        self.lastw = {}
        self.readers = {}
        self.ninst = {e: 0 for e in self.ops}
        self.marks = []

    def _deps(self, eng, reads, writes):
        need = {}

        def add(d):
            if d is None:
                return
            s, v = d
            if v > need.get(s, 0):
                need[s] = v
        for t in reads:
            add(self.lastw.get(t))
        for t in writes:
            add(self.lastw.get(t))
            for s, v in self.readers.get(t, {}).items():
                add((s, v))
        waits = []
        for s, v in need.items():
            if self.seen[eng].get(s, 0) < v:
                self.seen[eng][s] = v
                waits.append((s, v))
        return waits

    def _commit(self, stamp, reads, writes):
        s, v = stamp
        for t in reads:
            self.readers.setdefault(t, {})[s] = v
        for t in writes:
            self.lastw[t] = stamp
            self.readers[t] = {}

    def op(self, eng, fns, reads=(), writes=()):
        waits = self._deps(eng, reads, writes)
        self.cnt[eng] = self.cnt.get(eng, 0) + 1
        self._commit((eng, self.cnt[eng]), reads, writes)
        if not isinstance(fns, (list, tuple)):
            fns = [fns]
        self.ninst[eng] += len(fns)
        self.ops[eng].append((waits, list(fns), eng, 1))

    def dma(self, queue, fn, sem, reads=(), writes=()):
        waits = self._deps(queue, reads, writes)
        key = "d_" + sem
        prev = self.cnt.get(key, 0)
        if prev > 0 and self.seen[queue].get(key, 0) < prev:
            self.seen[queue][key] = prev
            waits.append((key, prev))
        self.cnt[key] = prev + 16
        self._commit((key, self.cnt[key]), reads, writes)
        self.ops[queue].append((waits, [fn], key, 16))

    def special(self, queue, fn, sem, reads=(), writes=()):
        waits = self._deps(queue, reads, writes)
        key = "c_" + sem
        assert key not in self.cnt
        self.cnt[key] = 1
        self._commit((key, 1), reads, writes)
        self.ops[queue].append((waits, [fn], key, None))

    def mark(self, name):
        self.marks.append((name, dict(self.ninst)))

    def barrier(self, engs=("pe", "act", "dve", "sp")):
        for e in engs:
            waits = []
            for s, v in self.cnt.items():
                if self.seen[e].get(s, 0) < v:
                    self.seen[e][s] = v
                    waits.append((s, v))
            if waits:
                self.ops[e].append((waits, [], None, 0))

    def emit(self, final_waits_engine="sp"):
        nc = self.nc
        fw = [(s, v) for s, v in self.cnt.items()]
        self.ops[final_waits_engine].append((fw, [], None, 0))
        import contextlib
        with contextlib.ExitStack() as st:
            sems = {}
            for s in self.cnt:
                sems[s] = st.enter_context(nc.semaphore("s_" + s))
            block = st.enter_context(nc.Block())
            engmap = {"pe": block.tensor, "act": block.scalar, "dve": block.vector,
                      "pool": block.gpsimd, "sp": block.sync}
            for ename, deco in engmap.items():
                ops = self.ops[ename]

                def body(e, ops=ops):
                    for waits, fns, incsem, incv in ops:
                        for s, v in waits:
                            e.wait_ge(sems[s], v)
                        ins = None
                        for f in fns:
                            ins = f(e)
                        if ins is not None and incsem is not None:
                            if incv is None:
                                ins.then_inc(sems[incsem])
                            else:
                                ins.then_inc(sems[incsem], incv)
                deco(body)


def build_fused(nlayers=4, ncores=8):
    nc = bass.Bass("TRN2", target_bir_lowering=False)
    P = Prog(nc)
    groups = [[2 * i, 2 * i + 1] for i in range(ncores // 2)]

    def din(name, shape):
        return nc.dram_tensor(name, shape, F32, kind="ExternalInput").ap()

    xin = din("x_own", [TOK, 2048])
    xhalo = din("x_halo", [128, 2048])
    halobig = din("halobig", [128, 1])
    pastmask = din("pastmask", [128, 1])
    mem = din("mem", [256, 2048])
    w_in_a = din("w_in_a", [2, 2048, 3072])
    sinks = din("sinks", [1, 24])
    w_q_b = din("w_q_b", [2, 2048, 2048])
    w_kvs = din("w_kvs", [2048, 1024])
    w_mem_kv = din("w_mem_kv", [4, 2048, 1024])
    w_o_all = din("w_o", [4, 2048, 2048])
    w_up_all = din("w_up", [4, 2048, 8192])
    w_dn_all = din("w_dn", [4, 8192, 2048])
    lng_all = din("lng", [4, 2, 2048])
    lnb_all = din("lnb", [4, 2, 2048])
    hout = nc.dram_tensor("hout", [TOK, 2048], F32, kind="ExternalOutput").ap()
    halo_in = nc.dram_tensor("halo_in", [128, 2048], F32)
    halo_out = nc.dram_tensor("halo_out", [256, 2048], F32)
    kin = nc.dram_tensor("kin", [128, 4096], F32)
    kout = nc.dram_tensor("kout", [256, 4096], F32)
    vin = nc.dram_tensor("vin", [128, 4096], F32)
    vout = nc.dram_tensor("vout", [256, 4096], F32)
    kmin = nc.dram_tensor("kmin", [128, 256], F32)
    kmout = nc.dram_tensor("kmout", [256, 256], F32)
    kin_ap, kout_ap, vin_ap, vout_ap, kmin_ap, kmout_ap = (kin.ap(), kout.ap(), vin.ap(), vout.ap(), kmin.ap(),
                                                            kmout.ap())

    ARENA = 212480
    arena = nc.alloc_sbuf_tensor("arena", [128, ARENA], U8)
    ps = nc.alloc_psum_tensor("ps", [128, 8, 512], F32)

    def view(off, shape, dt):
        n = 1
        for s in shape:
            n *= s
        nb = n * (4 if dt == F32 else 2)
        a = arena[:, off:off + nb].bitcast(dt)
        if len(shape) == 2:
            return a.rearrange("p (a b) -> p a b", b=shape[1])
        return a

    OH, ORING, OA, OB, OKV, OC = 0, 65536, 98304, 131072, 163840, 196608
    h = view(OH, [8, 2048], F32)
    ring = [view(ORING + i * 16384, [16, 512], BF16) for i in range(2)]
    ringd = [view(ORING + i * 16384, [4, 2048], BF16) for i in range(2)]
    hT = view(OA, [16, 1024], BF16)
    L = view(OA, [2048], F32)
    memb = view(OA, [2, 2048], BF16)
    memT = view(OA + 8192, [16, 256], BF16)
    Pnb = [view(OA + 8192, [2048], BF16), view(OA + 20480, [2048], BF16)]
    PT = [view(OA + 12288 + i * 4096, [16, 128], BF16) for i in range(2)]
    qT = view(OB, [16, 1024], BF16)
    actT = [view(OB + i * 8192, [4, 1024], BF16) for i in range(2)]
    rtmp = [view(OB + 16384 + i * 2048, [512], F32) for i in range(2)]
    kmacc = view(OB, [4, 4], F32)
    kT = view(OKV, [4, 2048], BF16)
    vv = view(OKV + 16384, [16, 512], BF16)
    gtab = view(OKV, [2048], F32)
    btab = view(OKV + 8192, [2048], F32)
    hb = [view(OKV + 16384 + i * 4096, [2048], BF16) for i in range(2)]
    kvf = view(OKV + 24576, [512], F32)
    c = OC
    mkT = view(c, [4, 256], BF16); c += 2048
    mv = view(c, [2, 512], BF16); c += 2048
    ident = view(c, [128], BF16); c += 256
    identf = view(c, [128], F32); c += 512
    ca = c
    hTh = view(ca, [16, 128], BF16); ca += 4096
    hhb = view(ca, [2048], BF16); ca += 4096
    dist = view(ca, [256], F32); ca += 1024
    dist0 = view(ca, [256], F32); ca += 1024
    sinkb = view(ca, [24], F32); ca += 96
    halob = view(ca, [1], F32); ca += 32
    cb = c
    R0 = view(cb, [2048], F32); cb += 8192
    caus = view(cb, [128], F32); cb += 512
    kmT = view(cb, [4, 8], BF16); cb += 64
    gbias = view(cb, [4, 8], F32); cb += 128
    pmask = view(cb, [1], F32); cb += 32
    gm = view(cb, [8], F32); cb += 32
    top8 = view(cb, [8], F32); cb += 32
    bm = view(cb, [8], F32); cb += 32
    c = max(ca, cb)
    stt = view(c, [4, 6], F32); c += 96
    mv2 = view(c, [2], F32); c += 32
    sm = view(c, [16], F32); c += 64
    epsb = view(c, [1], F32); c += 32
    assert c <= ARENA, c
    SM_MX, SM_NEGM, SM_RS, SM_ES, SM_DEN, SM_RDEN, SM_SD, SM_RSTD, SM_NMR = range(9)
    A_TABLE_TOKS = ["hTh", "hhb", "dist", "dist0", "sinkb", "halob"]

    def smc(i):
        return sm[:, i:i + 1]

    def psb(b):
        return ps[:, b, :].bitcast(BF16)

    P.op("pool", lambda e: e.memset(identf, 0.0), writes=["identf"])
    P.op("pool", lambda e: e.affine_select(out=identf, in_=identf, pattern=[[-1, 128]], base=0,
                                           channel_multiplier=1, compare_op=ALU.not_equal, fill=1.0),
         reads=["identf"], writes=["identf"])
    P.op("pool", lambda e: e.tensor_copy(out=ident, in_=identf), reads=["identf"], writes=["ident"])
    P.op("pool", lambda e: e.memset(epsb, EPS), writes=["epsb"])

    def setup_A():
        P.op("pool", lambda e: e.iota(dist, pattern=[[-1, 256]], base=128, channel_multiplier=1,
                                      allow_small_or_imprecise_dtypes=True), writes=["dist"])
        P.op("pool", lambda e: e.affine_select(out=dist, in_=dist, pattern=[[-1, 256]], base=128,
                                               channel_multiplier=1, compare_op=ALU.is_ge, fill=BIG),
             reads=["dist"], writes=["dist"])
        P.op("pool", lambda e: e.affine_select(out=dist, in_=dist, pattern=[[1, 256]], base=-1,
                                               channel_multiplier=-1, compare_op=ALU.is_ge, fill=BIG),
             reads=["dist"], writes=["dist"])
        P.dma("sp", lambda e: e.dma_start(out=halob, in_=halobig), "misc", writes=["halob"])
        P.dma("sp", lambda e: e.dma_start(out=sinkb, in_=sinks.partition_broadcast(128)), "misc", writes=["sinkb"])
        P.op("pool", lambda e: e.tensor_copy(out=dist0, in_=dist), reads=["dist"], writes=["dist0"])
        P.op("pool", lambda e: e.tensor_scalar(out=dist0[:, 0:128], in0=dist0[:, 0:128], scalar1=halob[:, 0:1],
                                               scalar2=None, op0=ALU.max),
             reads=["dist0", "halob"], writes=["dist0"])

    def setup_B():
        P.op("pool", lambda e: e.iota(R0, pattern=[[1, 2048]], base=0, channel_multiplier=0,
                                      allow_small_or_imprecise_dtypes=True), writes=["R0"] + A_TABLE_TOKS)
        P.op("pool", lambda e: e.memset(caus, 0.0), writes=["caus"] + A_TABLE_TOKS)
        P.op("pool", lambda e: e.affine_select(out=caus, in_=caus, pattern=[[-1, 128]], base=0,
                                               channel_multiplier=1, compare_op=ALU.is_ge, fill=NEG),
             reads=["caus"], writes=["caus"])
        P.dma("sp", lambda e: e.dma_start(out=pmask, in_=pastmask), "misc", writes=["pmask"] + A_TABLE_TOKS)
        P.op("pool", lambda e: e.memset(gbias, NEG), writes=["gbias"] + A_TABLE_TOKS)
        for jl in range(4):
            P.op("pool", lambda e, jl=jl: e.memset(gbias[:, jl, 0:4 + jl], 0.0), reads=["gbias"], writes=["gbias"])
            P.op("pool", lambda e, jl=jl: e.tensor_scalar(out=gbias[:, jl, 0:4], in0=gbias[:, jl, 0:4],
                                                          scalar1=pmask[:, 0:1], scalar2=None, op0=ALU.add),
                 reads=["pmask", "gbias"], writes=["gbias"])

    ring_state = {"n": 0}

    def load_unit_cols(W, c0):
        i = ring_state["n"] % 2
        ring_state["n"] += 1
        src = W.rearrange("(c p) f -> p c f", p=128)[:, :, c0:c0 + 512]
        P.dma("pool", lambda e: e.dma_start(out=ring[i], in_=src), "ring%d" % i, writes=[("ring", i)])
        return i

    def load_unit_rows(W, r0):
        i = ring_state["n"] % 2
        ring_state["n"] += 1
        src = W[r0:r0 + 512, :].rearrange("(c p) d -> p c d", p=128)
        P.dma("pool", lambda e: e.dma_start(out=ringd[i], in_=src), "ring%d" % i, writes=[("ring", i)])
        return i

    bank_rr = {"n": 0}

    def mm_group(out_ap, pairs, reads, writes):
        n = len(pairs)
        fns = []
        for i, (l, r) in enumerate(pairs):
            fns.append(lambda e, l=l, r=r, i=i: e.matmul(out=out_ap, lhsT=l, rhs=r, start=(i == 0), stop=(i == n - 1)))
        P.op("pe", fns, reads=reads, writes=writes)

    def transposes(srcs, reads, writes):
        fns = [lambda e, d=d, s=s: e.transpose(out=d, in_=s, identity=ident) for d, s in srcs]
        P.op("pe", fns, reads=list(reads) + ["ident"], writes=writes)

    pb6 = ps[:, 6:8, :].bitcast(BF16).rearrange("p a b -> p (a b)")

    def to_hT(src_ap, src_tok, dst, dst_cols, dst_tok):
        srcs = [(pb6[:, cc * 128:(cc + 1) * 128], src_ap[:, cc * 128:(cc + 1) * 128]) for cc in range(16)]
        transposes(srcs, reads=[src_tok], writes=[("ps", 6), ("ps", 7)])
        P.op("act", lambda e: e.activation(out=dst[:, :, dst_cols], in_=pb6.rearrange("p (a b) -> p a b", b=128),
                                           func=AF.Copy),
             reads=[("ps", 6), ("ps", 7)], writes=[dst_tok])

    hT_all = [("hT", t) for t in range(NT)]

    def evac_act(out_ap, in_ap, scale, reads, writes):
        P.op("act", lambda e: e.activation(out=out_ap, in_=in_ap, func=AF.Copy, scale=scale), reads=reads, writes=writes)

    def next_bank(lo=0, hi=6):
        b = lo + bank_rr["n"] % (hi - lo)
        bank_rr["n"] += 1
        return b

    def proj_feat(slot, j, dst, dst_slot, col_off, scale, tokname):
        for th in range(2):
            b = next_bank()
            pairs = [(ring[slot][:, cc, j * 128:(j + 1) * 128], hT[:, cc, th * 512:(th + 1) * 512]) for cc in range(16)]
            mm_group(ps[:, b, :], pairs, reads=[("ring", slot)] + hT_all[th * 4:th * 4 + 4], writes=[("ps", b)])
            evac_act(dst[:, dst_slot, col_off + th * 512: col_off + (th + 1) * 512], ps[:, b, :], scale,
                     reads=[("ps", b)], writes=[(tokname, dst_slot, 4 * th + k) for k in range(4)])

    cnt = {"ot": 0, "pt": 0, "s": 0}
    pend = {"y": None}
    KV_ALIAS = ["gtab", "btab", ("hb", 0), ("hb", 1), "kvf"]

    def softmax_pv(isA, h_slot, t, nk, s_psum, sink_col, v_aps, s_banks):
        src = L[:, 0:nk] if s_psum is None else s_psum
        src_reads = ["L"] if s_psum is None else [("ps", b) for b in s_banks]
        P.op("dve", lambda e: e.reduce_max(out=smc(SM_MX), in_=src, axis=AX.X), reads=src_reads, writes=["mx"])
        if sink_col is not None:
            P.op("dve", lambda e: e.tensor_scalar(out=smc(SM_NEGM), in0=smc(SM_MX), scalar1=sinkb[:, sink_col:sink_col + 1],
                                                  scalar2=-1.0, op0=ALU.max, op1=ALU.mult),
                 reads=["mx", "sinkb"], writes=["negm"])
        else:
            P.op("dve", lambda e: e.tensor_scalar(out=smc(SM_NEGM), in0=smc(SM_MX), scalar1=-1.0, scalar2=None,
                                                  op0=ALU.mult), reads=["mx"], writes=["negm"])
        P.op("act", lambda e: e.activation(out=L[:, 0:nk], in_=src, func=AF.Exp, bias=smc(SM_NEGM), scale=1.0,
                                           accum_out=smc(SM_RS)),
             reads=src_reads + ["negm"], writes=["L", "rs"])
        if sink_col is not None:
            P.op("act", lambda e: e.activation(out=smc(SM_ES), in_=sinkb[:, sink_col:sink_col + 1], func=AF.Exp,
                                               bias=smc(SM_NEGM), scale=1.0), reads=["negm", "sinkb"], writes=["es"])
            P.op("dve", lambda e: e.tensor_tensor(out=smc(SM_DEN), in0=smc(SM_RS), in1=smc(SM_ES), op=ALU.add),
                 reads=["rs", "es"], writes=["den"])
            P.op("dve", lambda e: e.reciprocal(out=smc(SM_RDEN), in_=smc(SM_DEN)), reads=["den"], writes=["rden"])
        else:
            P.op("dve", lambda e: e.reciprocal(out=smc(SM_RDEN), in_=smc(SM_RS)), reads=["rs"], writes=["rden"])
        pi = cnt["pt"] % 2
        cnt["pt"] += 1
        Pn = Pnb[pi]
        P.op("dve", lambda e: e.tensor_scalar(out=Pn[:, 0:nk], in0=L[:, 0:nk], scalar1=smc(SM_RDEN), scalar2=None,
                                              op0=ALU.mult), reads=["L", "rden"], writes=[("Pn", pi)])
        yprev = pend["y"]

        def y_part():
            softmax_y(isA, h_slot, t, nk, v_aps, pi)
        pend["y"] = y_part
        if yprev is not None:
            yprev()

    def flush_y():
        if pend["y"] is not None:
            pend["y"]()
            pend["y"] = None

    def softmax_y(isA, h_slot, t, nk, v_aps, pi):
        Pn = Pnb[pi]
        nb = nk // 128
        if nb <= 8:
            pbank = [6 + pi]
            pdst = psb(6 + pi)
        else:
            pbank = [6, 7]
            pdst = pb6
        srcs = [(pdst[:, k * 128:(k + 1) * 128], Pn[:, k * 128:(k + 1) * 128]) for k in range(nb)]
        transposes(srcs, reads=[("Pn", pi)], writes=[("ps", b) for b in pbank])
        P.op("act", lambda e: e.activation(out=PT[pi][:, 0:nb, :],
                                           in_=pdst[:, 0:nb * 128].rearrange("p (a b) -> p a b", b=128), func=AF.Copy),
             reads=[("ps", b) for b in pbank], writes=[("PT", pi)])
        ob = (4 + cnt["ot"] % 2) if isA else 4
        cnt["ot"] += 1
        pairs = [(v_aps[k][0], PT[pi][:, k, :]) for k in range(nb)]
        vreads = sorted(set(v_aps[k][1] for k in range(nb)), key=str)
        mm_group(ps[:, ob, 0:128], pairs, reads=[("PT", pi)] + vreads, writes=[("ps", ob)])
        P.op("act", lambda e: e.activation(out=qT[:, h_slot, t * 128:(t + 1) * 128], in_=ps[:, ob, 0:128], func=AF.Copy),
             reads=[("ps", ob)], writes=[("qT", h_slot, t)])

    def load_gb(layer, i):
        P.dma("sp", lambda e: e.dma_start(out=gtab, in_=lng_all[layer, i:i + 1, :].partition_broadcast(128)), "misc",
              writes=["gtab"])
        P.dma("sp", lambda e: e.dma_start(out=btab, in_=lnb_all[layer, i:i + 1, :].partition_broadcast(128)), "misc",
              writes=["btab"])

    hout_v = hout.rearrange("(n p) d -> p n d", p=128)

    def layer_norm(t, store, make_hT, halo_send):
        for q4 in range(4):
            P.op("dve", lambda e, q4=q4: e.bn_stats(out=stt[:, q4, :], in_=h[:, t, q4 * 512:(q4 + 1) * 512]),
                 reads=[("h", t)], writes=[("stt", q4)])
        P.op("dve", lambda e: e.bn_aggr(out=mv2, in_=stt.rearrange("p a b -> p (a b)")),
             reads=[("stt", q4) for q4 in range(4)], writes=["mv2"])
        P.op("act", lambda e: e.activation(out=smc(SM_SD), in_=mv2[:, 1:2], func=AF.Sqrt, bias=epsb[:, 0:1], scale=1.0),
             reads=["mv2", "epsb"], writes=["sd"])
        P.op("dve", lambda e: e.reciprocal(out=smc(SM_RSTD), in_=smc(SM_SD)), reads=["sd"], writes=["rstd"])
        P.op("dve", lambda e: e.scalar_tensor_tensor(out=smc(SM_NMR), in0=mv2[:, 0:1], scalar=-1.0, in1=smc(SM_RSTD),
                                                     op0=ALU.mult, op1=ALU.mult), reads=["mv2", "rstd"], writes=["nmr"])
        P.op("act", lambda e: e.activation(out=h[:, t, :], in_=h[:, t, :], func=AF.Identity, scale=smc(SM_RSTD),
                                           bias=smc(SM_NMR)), reads=[("h", t), "rstd", "nmr"], writes=[("h", t)])
        P.op("pool", lambda e: e.tensor_tensor(out=h[:, t, :], in0=h[:, t, :], in1=gtab, op=ALU.mult),
             reads=[("h", t), "gtab"], writes=[("h", t)])
        P.op("dve", lambda e: e.tensor_tensor(out=h[:, t, :], in0=h[:, t, :], in1=btab, op=ALU.add),
             reads=[("h", t), "btab"], writes=[("h", t)])
        if store:
            P.dma("sp", lambda e: e.dma_start(out=hout_v[:, t, :], in_=h[:, t, :]), "hstore%d" % (t % 4), reads=[("h", t)])
        if make_hT:
            i = t % 2
            P.op("act", lambda e: e.activation(out=hb[i], in_=h[:, t, :], func=AF.Copy),
                 reads=[("h", t)], writes=[("hb", i)])
            to_hT(hb[i], ("hb", i), hT, slice(t * 128, (t + 1) * 128), ("hT", t))
            if halo_send and t == NT - 1:
                P.dma("sp", lambda e: e.dma_start(out=halo_in.ap(), in_=h[:, t, :]), "halo", reads=[("h", t)],
                      writes=["halo_in"])
                P.special("pool", lambda e: e.collective_compute(
                    "AllGather", ALU.bypass, replica_groups=groups, ins=[halo_in.ap().opt()],
                    outs=[halo_out.ap().opt()]), "halo", reads=["halo_in"], writes=["halo_out"])

    xin_v = xin.rearrange("(n p) d -> p n d", p=128)
    for t in range(NT):
        P.dma("sp", lambda e, t=t: e.dma_start(out=h[:, t, :], in_=xin_v[:, t, :]), "hload%d" % (t % 4), writes=[("h", t)])
    for t in range(NT):
        i = t % 2
        P.op("act", lambda e, t=t, i=i: e.activation(out=hb[i], in_=h[:, t, :], func=AF.Copy),
             reads=[("h", t)], writes=[("hb", i)])
        to_hT(hb[i], ("hb", i), hT, slice(t * 128, (t + 1) * 128), ("hT", t))
    setup_A()

    for layer in range(nlayers):
        isA = layer < 2
        last = layer == nlayers - 1
        w_mkv = w_mem_kv[layer]
        w_o = w_o_all[layer]
        w_up = w_up_all[layer]
        w_dn = w_dn_all[layer]
        if layer > 0:
            P.barrier()
        if isA:
            w_qkv = w_in_a[layer]
            if layer == 0:
                P.dma("pool", lambda e: e.dma_start(out=hhb, in_=xhalo), "misc2", writes=["hhb"])
            else:
                P.dma("pool", lambda e: e.dma_start(out=hhb, in_=halo_out.ap()[0:128, :]), "misc2", reads=["halo_out"],
                      writes=["hhb"])
            to_hT(hhb, "hhb", hTh, slice(0, 128), "hTh")
            units = [("q", 0), ("q", 1), ("q", 2), ("k", 3), ("v", 4), ("qm", 5)]
        else:
            w_qkv = w_q_b[layer - 2]
            if layer == 2:
                setup_B()
            units = [("q", 0), ("q", 1), ("q", 2), ("qm", 3)]
            kparts = [(kout_ap[0:128, :], kT[:, :, 0:1024]), (kin_ap, kT[:, :, 1024:2048])]
            for si, di in kparts:
                P.dma("pool", lambda e, si=si, di=di: e.dma_start(out=di, in_=si.rearrange("p (a b) -> p a b", b=1024)),
                      "misc2", reads=["kout", "vout", "kmout", "kin", "vin", "kmin"], writes=["kT"] + KV_ALIAS)
            vparts = [(vout_ap[0:128, :], vv[:, 0:8, :]), (vin_ap, vv[:, 8:16, :])]
            for si, di in vparts:
                P.dma("pool", lambda e, si=si, di=di: e.dma_start(out=di, in_=si.rearrange("p (a b) -> p a b", b=512)),
                      "misc2", reads=["kout", "vout", "kmout", "kin", "vin", "kmin"], writes=["vv"] + KV_ALIAS)
            mparts = [(kmout_ap[0:128, 0:16], kmT[:, :, 0:4]), (kmin_ap[:, 0:16], kmT[:, :, 4:8])]
            for si, di in mparts:
                P.dma("pool", lambda e, si=si, di=di: e.dma_start(out=di, in_=si.rearrange("p (a b) -> p a b", b=4)),
                      "misc2", reads=["kout", "vout", "kmout", "kin", "vin", "kmin"], writes=["kmT"] + (A_TABLE_TOKS if layer == 2 else []))
        P.mark("L%d proj" % layer)
        for typ, u in units:
            slot = load_unit_cols(w_qkv, u * 512)
            if typ in ("q", "qm"):
                base = (u * 4) if typ == "q" else 12
                for j in range(4):
                    proj_feat(slot, j, qT, base + j, 0, SC, "qT")
            elif typ == "k":
                for j in range(4):
                    proj_feat(slot, j, kT, j, 128, 1.0, "kTt")
                    b = next_bank()
                    pairs = [(ring[slot][:, cc, j * 128:(j + 1) * 128], hTh[:, cc, :]) for cc in range(16)]
                    mm_group(ps[:, b, 0:128], pairs, reads=[("ring", slot), "hTh"], writes=[("ps", b)])
                    evac_act(kT[:, j, 0:128], ps[:, b, 0:128], 1.0, reads=[("ps", b)], writes=[("kTh", j)])
            elif typ == "v":
                for t in range(-1, NT):
                    b = next_bank()
                    if t < 0:
                        pairs = [(hTh[:, cc, :], ring[slot][:, cc, :]) for cc in range(16)]
                        rd = ["hTh"]
                    else:
                        pairs = [(hT[:, cc, t * 128:(t + 1) * 128], ring[slot][:, cc, :]) for cc in range(16)]
                        rd = [("hT", t)]
                    mm_group(ps[:, b, :], pairs, reads=[("ring", slot)] + rd, writes=[("ps", b)])
                    evac_act(vv[:, t + 1, :], ps[:, b, :], 1.0, reads=[("ps", b)], writes=[("vvt", t + 1)])
        P.mark("L%d memkv" % layer)
        P.barrier()
        P.dma("pool", lambda e: e.dma_start(out=memb, in_=mem.rearrange("(n p) d -> p n d", p=128)), "misc2",
              writes=["memb"] + hT_all)
        for mt in range(2):
            srcs = [(pb6[:, cc * 128:(cc + 1) * 128], memb[:, mt, cc * 128:(cc + 1) * 128]) for cc in range(16)]
            transposes(srcs, reads=["memb"], writes=[("ps", 6), ("ps", 7)])
            P.op("act", lambda e, mt=mt: e.activation(out=memT[:, :, mt * 128:(mt + 1) * 128],
                                                      in_=pb6.rearrange("p (a b) -> p a b", b=128), func=AF.Copy),
                 reads=[("ps", 6), ("ps", 7)], writes=["memT"])
        slot = load_unit_cols(w_mkv, 0)
        for j in range(4):
            b = next_bank()
            pairs = [(ring[slot][:, cc, j * 128:(j + 1) * 128], memT[:, cc, :]) for cc in range(16)]
            mm_group(ps[:, b, 0:256], pairs, reads=[("ring", slot), "memT"], writes=[("ps", b)])
            evac_act(mkT[:, j, :], ps[:, b, 0:256], 1.0, reads=[("ps", b)], writes=["mkT"])
        slot = load_unit_cols(w_mkv, 512)
        for mt in range(2):
            b = next_bank()
            pairs = [(memT[:, cc, mt * 128:(mt + 1) * 128], ring[slot][:, cc, :]) for cc in range(16)]
            mm_group(ps[:, b, :], pairs, reads=[("ring", slot), "memT"], writes=[("ps", b)])
            evac_act(mv[:, mt, :], ps[:, b, :], 1.0, reads=[("ps", b)], writes=["mv"])
        P.barrier()
        P.mark("L%d att" % layer)
        for hh in range(12):
            kvh = hh // 3
            for t in range(NT):
                qap = qT[:, hh, t * 128:(t + 1) * 128]
                if isA:
                    sb = cnt["s"] % 4
                    cnt["s"] += 1
                    mm_group(ps[:, sb, 0:256], [(qap, kT[:, kvh, t * 128:t * 128 + 256])],
                             reads=[("qT", hh, t), ("kTh", kvh)] + [("kTt", kvh, k) for k in range(NT)],
                             writes=[("ps", sb)])
                    dtab = dist0 if t == 0 else dist
                    P.op("dve", lambda e, dtab=dtab, sb=sb, hh=hh: e.scalar_tensor_tensor(
                        out=L[:, 0:256], in0=dtab, scalar=-SLOPES[hh], in1=ps[:, sb, 0:256], op0=ALU.mult, op1=ALU.add),
                        reads=[("ps", sb), "dist", "dist0"], writes=["L"])
                    v_aps = [(vv[:, t + k, kvh * 128:(kvh + 1) * 128], ("vvt", t + k)) for k in range(2)]
                    softmax_pv(True, hh, t, 256, None, layer * 12 + hh, v_aps, None)
                else:
                    jl = t // 2
                    nk = 1024 + (t + 1) * 128
                    nmm = (nk + 511) // 512
                    fns = []
                    for m in range(nmm):
                        w = min(512, nk - m * 512)
                        fns.append(lambda e, m=m, w=w, qap=qap, kvh=kvh: e.matmul(
                            out=ps[:, m, 0:w], lhsT=qap, rhs=kT[:, kvh, m * 512:m * 512 + w], start=True, stop=True))
                    fns.append(lambda e, qap=qap, kvh=kvh: e.matmul(out=ps[:, 5, 0:8], lhsT=qap, rhs=kmT[:, kvh, :],
                                                                    start=True, stop=True))
                    P.op("pe", fns, reads=[("qT", hh, t), "kT", "kmT"],
                         writes=[("ps", m) for m in range(nmm)] + [("ps", 5)])
                    P.op("dve", lambda e, jl=jl: e.tensor_tensor(out=gm, in0=ps[:, 5, 0:8], in1=gbias[:, jl, :], op=ALU.add),
                         reads=[("ps", 5), "gbias"], writes=["gm"])
                    P.op("dve", lambda e: e.max(out=top8, in_=gm), reads=["gm"], writes=["top8"])
                    P.op("dve", lambda e: e.tensor_scalar(out=bm, in0=gm, scalar1=top8[:, 2:3], scalar2=NEG,
                                                          op0=ALU.is_lt, op1=ALU.mult), reads=["gm", "top8"], writes=["bm"])
                    P.op("dve", lambda e, jl=jl: e.tensor_tensor(out=bm, in0=bm, in1=gbias[:, jl, :], op=ALU.add),
                         reads=["bm", "gbias"], writes=["bm"])
                    sall = ps[:, 0:4, :].rearrange("p a b -> p (a b)")
                    P.op("dve", lambda e, nk=nk, hh=hh, sall=sall: e.scalar_tensor_tensor(
                        out=L[:, 0:nk], in0=R0[:, 0:nk], scalar=SLOPES[hh], in1=sall[:, 0:nk], op0=ALU.mult, op1=ALU.add),
                        reads=[("ps", m) for m in range(nmm)] + ["R0"], writes=["L"])
                    nblk = 4 + jl
                    L3 = L[:, 0:nblk * 256].rearrange("p (a b) -> p a b", b=256)
                    bmb = bm[:, 0:nblk].unsqueeze(2).to_broadcast([128, nblk, 256])
                    P.op("pool", lambda e, L3=L3, bmb=bmb: e.tensor_tensor(out=L3, in0=L3, in1=bmb, op=ALU.add),
                         reads=["L", "bm"], writes=["L"])
                    P.op("pool", lambda e, nk=nk: e.tensor_tensor(out=L[:, nk - 128:nk], in0=L[:, nk - 128:nk], in1=caus,
                                                                  op=ALU.add), reads=["L", "caus"], writes=["L"])
                    v_aps = [(vv[:, k, kvh * 128:(kvh + 1) * 128], "vv") for k in range(nk // 128)]
                    softmax_pv(False, hh, t, nk, None, None, v_aps, None)
        for j in range(4):
            for t in range(NT):
                sb = cnt["s"] % 4
                cnt["s"] += 1
                mm_group(ps[:, sb, 0:256], [(qT[:, 12 + j, t * 128:(t + 1) * 128], mkT[:, j, :])],
                         reads=[("qT", 12 + j, t), "mkT"], writes=[("ps", sb)])
                v_aps = [(mv[:, k, j * 128:(j + 1) * 128], "mv") for k in range(2)]
                softmax_pv(isA, 12 + j, t, 256, ps[:, sb, 0:256], None, v_aps, [sb])
        flush_y()
        P.mark("L%d wo" % layer)
        P.barrier()
        load_gb(layer, 0)
        for dq in range(4):
            slot = load_unit_cols(w_o, dq * 512)
            for t in range(NT):
                b = next_bank(0, 6)
                pairs = [(qT[:, cc, t * 128:(t + 1) * 128], ring[slot][:, cc, :]) for cc in range(16)]
                mm_group(ps[:, b, :], pairs, reads=[("ring", slot)] + [("qT", s, t) for s in range(16)],
                         writes=[("ps", b)])
                P.op("dve", lambda e, t=t, dq=dq, b=b: e.scalar_tensor_tensor(
                    out=h[:, t, dq * 512:(dq + 1) * 512], in0=h[:, t, dq * 512:(dq + 1) * 512], scalar=ALPHA,
                    in1=ps[:, b, :], op0=ALU.mult, op1=ALU.add), reads=[("ps", b), ("h", t)], writes=[("h", t)])
        NG = 16
        plan = [("up", 0)]
        for g in range(1, NG):
            plan += [("up", g), ("dn", g - 1)]
        plan += [("dn", NG - 1)]
        plan_slot = {}
        plan_pos = {"n": 0}

        def plan_issue():
            k = plan_pos["n"]
            if k >= len(plan):
                return
            plan_pos["n"] += 1
            typ, g = plan[k]
            plan_slot[(typ, g)] = load_unit_cols(w_up, g * 512) if typ == "up" else load_unit_rows(w_dn, g * 512)

        plan_issue()
        plan_issue()
        P.mark("L%d ln1" % layer)
        P.barrier()
        for t in range(NT):
            layer_norm(t, False, True, False)
        P.mark("L%d ffn" % layer)
        P.barrier()
        load_gb(layer, 1)
        upb = {"n": 0}
        dnb = {"n": 0}
        rt = {"n": 0}

        def ffn_up(g):
            slot = plan_slot[("up", g)]
            ai = g % 2
            for j in range(4):
                for th in range(2):
                    b = upb["n"] % 4
                    upb["n"] += 1
                    pairs = [(ring[slot][:, cc, j * 128:(j + 1) * 128], hT[:, cc, th * 512:(th + 1) * 512])
                             for cc in range(16)]
                    mm_group(ps[:, b, :], pairs, reads=[("ring", slot)] + hT_all[th * 4:th * 4 + 4], writes=[("ps", b)])
                    ri = rt["n"] % 2
                    rt["n"] += 1
                    P.op("act", lambda e, b=b, ri=ri: e.activation(out=rtmp[ri], in_=ps[:, b, :], func=AF.Relu),
                         reads=[("ps", b)], writes=[("rtmp", ri)])
                    P.op("pool", lambda e, ri=ri, ai=ai, j=j, th=th: e.tensor_tensor(
                        out=actT[ai][:, j, th * 512:(th + 1) * 512], in0=rtmp[ri], in1=rtmp[ri], op=ALU.mult),
                        reads=[("rtmp", ri)], writes=[("actT", ai, j, th)])

        def ffn_down(g):
            slot = plan_slot[("dn", g)]
            ai = g % 2
            for t in range(NT):
                for dq in range(4):
                    b = 4 + dnb["n"] % 4
                    dnb["n"] += 1
                    pairs = [(actT[ai][:, j, t * 128:(t + 1) * 128], ringd[slot][:, j, dq * 512:(dq + 1) * 512])
                             for j in range(4)]
                    mm_group(ps[:, b, :], pairs, reads=[("ring", slot)] + [("actT", ai, j, t // 4) for j in range(4)],
                             writes=[("ps", b)])
                    hs = h[:, t, dq * 512:(dq + 1) * 512]
                    if g == 0:
                        P.op("dve", lambda e, hs=hs, b=b: e.scalar_tensor_tensor(out=hs, in0=hs, scalar=ALPHA,
                                                                                 in1=ps[:, b, :], op0=ALU.mult, op1=ALU.add),
                             reads=[("ps", b), ("h", t)], writes=[("h", t)])
                    else:
                        P.op("dve", lambda e, hs=hs, b=b: e.tensor_tensor(out=hs, in0=hs, in1=ps[:, b, :], op=ALU.add),
                             reads=[("ps", b), ("h", t)], writes=[("h", t)])

        for g in range(NG):
            ffn_up(g)
            plan_issue()
            if g > 0:
                ffn_down(g - 1)
                plan_issue()
        ffn_down(NG - 1)
        P.mark("L%d ln2" % layer)
        P.barrier()
        for t in range(NT):
            layer_norm(t, last, not last, layer == 0 and nlayers > 1)
        if layer == 1 and nlayers > 2:
            P.barrier()
            for u in range(2):
                slot = load_unit_cols(w_kvs, u * 512)
                if u == 0:
                    for j in range(4):
                        for th in range(2):
                            b = next_bank()
                            pairs = [(ring[slot][:, cc, j * 128:(j + 1) * 128], hT[:, cc, th * 512:(th + 1) * 512])
                                     for cc in range(16)]
                            mm_group(ps[:, b, :], pairs, reads=[("ring", slot)] + hT_all[th * 4:th * 4 + 4],
                                     writes=[("ps", b)])
                            P.op("act", lambda e, b=b: e.activation(out=kvf, in_=ps[:, b, :], func=AF.Copy),
                                 reads=[("ps", b)], writes=["kvf"])
                            co = j * 1024 + th * 512
                            P.dma("sp", lambda e, co=co: e.dma_start(out=kin_ap[:, co:co + 512], in_=kvf),
                                  "kvst", reads=["kvf"], writes=["kin"])
                            P.op("dve", lambda e, j=j, th=th: e.tensor_reduce(
                                out=kmacc[:, j, 2 * th:2 * th + 2], in_=kvf.rearrange("p (a b) -> p a b", b=256),
                                axis=AX.X, op=ALU.add), reads=["kvf"], writes=["kmacc"])
                    P.op("dve", lambda e: e.tensor_scalar(out=kmacc, in0=kmacc, scalar1=1.0 / 256.0, scalar2=None,
                                                          op0=ALU.mult), reads=["kmacc"], writes=["kmacc"])
                else:
                    for t in range(NT):
                        b = next_bank()
                        pairs = [(hT[:, cc, t * 128:(t + 1) * 128], ring[slot][:, cc, :]) for cc in range(16)]
                        mm_group(ps[:, b, :], pairs, reads=[("ring", slot), ("hT", t)], writes=[("ps", b)])
                        P.op("act", lambda e, b=b: e.activation(out=kvf, in_=ps[:, b, :], func=AF.Copy),
                             reads=[("ps", b)], writes=["kvf"])
                        co = t * 512
                        P.dma("sp", lambda e, co=co: e.dma_start(out=vin_ap[:, co:co + 512], in_=kvf), "kvst",
                              reads=["kvf"], writes=["vin"])
            P.op("dve", lambda e: e.memset(kvf[:, 0:256], 0.0), writes=["kvf"])
            P.op("dve", lambda e: e.tensor_copy(out=kvf[:, 0:16].rearrange("p (a b) -> p a b", b=4), in_=kmacc),
                 reads=["kmacc", "kvf"], writes=["kvf"])
            P.dma("sp", lambda e: e.dma_start(out=kmin_ap, in_=kvf[:, 0:256]), "kvst2", reads=["kvf"], writes=["kmin"])
            for nm, ti, to in (("kx", kin, kout), ("vx", vin, vout), ("kmx", kmin, kmout)):
                P.special("pool", lambda e, ti=ti, to=to: e.collective_compute(
                    "AllGather", ALU.bypass, replica_groups=groups, ins=[ti.ap().opt()], outs=[to.ap().opt()]),
                    nm, reads=[ti.name], writes=[to.name])
    P.mark("end")
    P.emit()
    nc._marks = P.marks
    return nc


_CACHE = {}
_DBG = None


def _prep(x, mem, w_in_a, sinks_a, w_q_b, w_kv_shared, w_mem_kv, w_o, w_up, w_down, ln_g, ln_b, ncores=8):
    f = lambda a: np.ascontiguousarray(np.asarray(a, dtype=np.float32))
    x = f(x); mem = f(mem)
    shared = {"w_in_a": f(w_in_a), "sinks": f(sinks_a).reshape(1, 24), "w_q_b": f(w_q_b), "w_kvs": f(w_kv_shared),
              "w_mem_kv": f(w_mem_kv), "w_o": f(w_o), "w_up": f(w_up), "w_dn": f(w_down), "lng": f(ln_g), "lnb": f(ln_b)}
    zeros_halo = np.zeros((128, 2048), np.float32)
    in_maps = []
    for c in range(ncores):
        b, hf = c // 2, c % 2
        m = dict(shared)
        m["x_own"] = np.ascontiguousarray(x[b, hf * 1024:(hf + 1) * 1024])
        m["x_halo"] = np.ascontiguousarray(x[b, 896:1024]) if hf == 1 else zeros_halo
        m["halobig"] = np.full((128, 1), 0.0 if hf == 1 else BIG, np.float32)
        m["pastmask"] = np.full((128, 1), 0.0 if hf == 1 else NEG, np.float32)
        m["mem"] = np.ascontiguousarray(mem[b])
        in_maps.append(m)
    return in_maps


def kernel(x, mem, w_in_a, sinks_a, w_q_b, w_kv_shared, w_mem_kv, w_o, w_up, w_down, ln_g, ln_b):
    ncores = 8
    if "nc" not in _CACHE:
        _CACHE["nc"] = build_fused(4, ncores)
    nc = _CACHE["nc"]
    in_maps = _prep(x, mem, w_in_a, sinks_a, w_q_b, w_kv_shared, w_mem_kv, w_o, w_up, w_down, ln_g, ln_b, ncores)
    res = run_bass_kernel_spmd(nc, in_maps, core_ids=list(range(ncores)))
    out = np.empty((4, 2048, 2048), np.float32)
    for c in range(ncores):
        b, hf = c // 2, c % 2
        out[b, hf * 1024:(hf + 1) * 1024] = res.results[c]["hout"]
    return out
```

```python
import numpy as np
import concourse.bass as bass
import concourse.mybir as mybir
from concourse.bass_utils import run_bass_kernel_spmd

F32 = mybir.dt.float32
BF16 = mybir.dt.bfloat16
U8 = mybir.dt.uint8
AF = mybir.ActivationFunctionType
ALU = mybir.AluOpType
AX = mybir.AxisListType

NEG = -30000.0
BIG = 1.0e6
SC = 128.0 ** -0.5
ALPHA = 8.0 ** 0.25
EPS = 1e-5
SLOPES = [2.0 ** (-8.0 * (i + 1) / 12.0) for i in range(12)]
NT = 8
TOK = 1024


class Prog:
    def __init__(self, nc):
        self.nc = nc
        self.ops = {e: [] for e in ("pe", "act", "dve", "pool", "sp")}
        self.cnt = {}
        self.seen = {e: {} for e in self.ops}
        self.lastw = {}
        self.readers = {}
        self.ninst = {e: 0 for e in self.ops}
        self.marks = []

    def _deps(self, eng, reads, writes):
        need = {}

        def add(d):
            if d is None:
                return
            s, v = d
            if v > need.get(s, 0):
                need[s] = v
        for t in reads:
            add(self.lastw.get(t))
        for t in writes:
            add(self.lastw.get(t))
            for s, v in self.readers.get(t, {}).items():
                add((s, v))
        waits = []
        for s, v in need.items():
            if self.seen[eng].get(s, 0) < v:
                self.seen[eng][s] = v
                waits.append((s, v))
        return waits

    def _commit(self, stamp, reads, writes):
        s, v = stamp
        for t in reads:
            self.readers.setdefault(t, {})[s] = v
        for t in writes:
            self.lastw[t] = stamp
            self.readers[t] = {}

    def op(self, eng, fns, reads=(), writes=()):
        waits = self._deps(eng, reads, writes)
        self.cnt[eng] = self.cnt.get(eng, 0) + 1
        self._commit((eng, self.cnt[eng]), reads, writes)
        if not isinstance(fns, (list, tuple)):
            fns = [fns]
        self.ninst[eng] += len(fns)
        self.ops[eng].append((waits, list(fns), eng, 1))

    def dma(self, queue, fn, sem, reads=(), writes=()):
        waits = self._deps(queue, reads, writes)
        key = "d_" + sem
        prev = self.cnt.get(key, 0)
        if prev > 0 and self.seen[queue].get(key, 0) < prev:
            self.seen[queue][key] = prev
            waits.append((key, prev))
        self.cnt[key] = prev + 16
        self._commit((key, self.cnt[key]), reads, writes)
        self.ops[queue].append((waits, [fn], key, 16))

    def special(self, queue, fn, sem, reads=(), writes=()):
        waits = self._deps(queue, reads, writes)
        key = "c_" + sem
        assert key not in self.cnt
        self.cnt[key] = 1
        self._commit((key, 1), reads, writes)
        self.ops[queue].append((waits, [fn], key, None))

    def mark(self, name):
        self.marks.append((name, dict(self.ninst)))

    def barrier(self, engs=("pe", "act", "dve", "sp")):
        for e in engs:
            waits = []
            for s, v in self.cnt.items():
                if self.seen[e].get(s, 0) < v:
                    self.seen[e][s] = v
                    waits.append((s, v))
            if waits:
                self.ops[e].append((waits, [], None, 0))

    def emit(self, final_waits_engine="sp"):
        nc = self.nc
        fw = [(s, v) for s, v in self.cnt.items()]
        self.ops[final_waits_engine].append((fw, [], None, 0))
        import contextlib
        with contextlib.ExitStack() as st:
            sems = {}
            for s in self.cnt:
                sems[s] = st.enter_context(nc.semaphore("s_" + s))
            block = st.enter_context(nc.Block())
            engmap = {"pe": block.tensor, "act": block.scalar, "dve": block.vector,
                      "pool": block.gpsimd, "sp": block.sync}
            for ename, deco in engmap.items():
                ops = self.ops[ename]

                def body(e, ops=ops):
                    for waits, fns, incsem, incv in ops:
                        for s, v in waits:
                            e.wait_ge(sems[s], v)
                        ins = None
                        for f in fns:
                            ins = f(e)
                        if ins is not None and incsem is not None:
                            if incv is None:
                                ins.then_inc(sems[incsem])
                            else:
                                ins.then_inc(sems[incsem], incv)
                deco(body)


def build_fused(nlayers=4, ncores=8):
    nc = bass.Bass("TRN2", target_bir_lowering=False)
    P = Prog(nc)
    groups = [[2 * i, 2 * i + 1] for i in range(ncores // 2)]

    def din(name, shape):
        return nc.dram_tensor(name, shape, F32, kind="ExternalInput").ap()

    xin = din("x_own", [TOK, 2048])
    xhalo = din("x_halo", [128, 2048])
    halobig = din("halobig", [128, 1])
    pastmask = din("pastmask", [128, 1])
    mem = din("mem", [256, 2048])
    w_in_a = din("w_in_a", [2, 2048, 3072])
    sinks = din("sinks", [1, 24])
    w_q_b = din("w_q_b", [2, 2048, 2048])
    w_kvs = din("w_kvs", [2048, 1024])
    w_mem_kv = din("w_mem_kv", [4, 2048, 1024])
    w_o_all = din("w_o", [4, 2048, 2048])
    w_up_all = din("w_up", [4, 2048, 8192])
    w_dn_all = din("w_dn", [4, 8192, 2048])
    lng_all = din("lng", [4, 2, 2048])
    lnb_all = din("lnb", [4, 2, 2048])
    hout = nc.dram_tensor("hout", [TOK, 2048], F32, kind="ExternalOutput").ap()
    halo_in = nc.dram_tensor("halo_in", [128, 2048], F32)
    halo_out = nc.dram_tensor("halo_out", [256, 2048], F32)
    kin = nc.dram_tensor("kin", [128, 4096], F32)
    kout = nc.dram_tensor("kout", [256, 4096], F32)
    vin = nc.dram_tensor("vin", [128, 4096], F32)
    vout = nc.dram_tensor("vout", [256, 4096], F32)
    kmin = nc.dram_tensor("kmin", [128, 256], F32)
    kmout = nc.dram_tensor("kmout", [256, 256], F32)
    kin_ap, kout_ap, vin_ap, vout_ap, kmin_ap, kmout_ap = (kin.ap(), kout.ap(), vin.ap(), vout.ap(), kmin.ap(),
                                                            kmout.ap())

    ARENA = 212480
    arena = nc.alloc_sbuf_tensor("arena", [128, ARENA], U8)
    ps = nc.alloc_psum_tensor("ps", [128, 8, 512], F32)

    def view(off, shape, dt):
        n = 1
        for s in shape:
            n *= s
        nb = n * (4 if dt == F32 else 2)
        a = arena[:, off:off + nb].bitcast(dt)
        if len(shape) == 2:
            return a.rearrange("p (a b) -> p a b", b=shape[1])
        return a

    OH, ORING, OA, OB, OKV, OC = 0, 65536, 98304, 131072, 163840, 196608
    h = view(OH, [8, 2048], F32)
    ring = [view(ORING + i * 16384, [16, 512], BF16) for i in range(2)]
    ringd = [view(ORING + i * 16384, [4, 2048], BF16) for i in range(2)]
    hT = view(OA, [16, 1024], BF16)
    Lb = [view(OA, [2048], F32), view(OA + 24576, [2048], F32)]
    memb = view(OA, [2, 2048], BF16)
    memT = view(OA + 8192, [16, 256], BF16)
    Pnb = [view(OA + 8192, [2048], BF16), view(OA + 20480, [2048], BF16)]
    PT = [view(OA + 12288 + i * 4096, [16, 128], BF16) for i in range(2)]
    qT = view(OB, [16, 1024], BF16)
    actT = [view(OB + i * 8192, [4, 1024], BF16) for i in range(2)]
    rtmp = [view(OB + 16384 + i * 2048, [512], F32) for i in range(2)]
    kmacc = view(OB, [4, 4], F32)
    kT = view(OKV, [4, 2048], BF16)
    vv = view(OKV + 16384, [16, 512], BF16)
    gtab = view(OKV, [2048], F32)
    btab = view(OKV + 8192, [2048], F32)
    hb = [view(OKV + 16384 + i * 4096, [2048], BF16) for i in range(2)]
    kvf = view(OKV + 24576, [512], F32)
    c = OC
    mkT = view(c, [4, 256], BF16); c += 2048
    mv = view(c, [2, 512], BF16); c += 2048
    ident = view(c, [128], BF16); c += 256
    identf = view(c, [128], F32); c += 512
    ca = c
    hTh = view(ca, [16, 128], BF16); ca += 4096
    hhb = view(ca, [2048], BF16); ca += 4096
    dist = view(ca, [256], F32); ca += 1024
    dist0 = view(ca, [256], F32); ca += 1024
    sinkb = view(ca, [24], F32); ca += 96
    halob = view(ca, [1], F32); ca += 32
    cb = c
    R0 = view(cb, [2048], F32); cb += 8192
    caus = view(cb, [128], F32); cb += 512
    kmT = view(cb, [4, 8], BF16); cb += 64
    gbias = view(cb, [4, 8], F32); cb += 128
    pmask = view(cb, [1], F32); cb += 32
    gm = view(cb, [8], F32); cb += 32
    top8 = view(cb, [8], F32); cb += 32
    bm = view(cb, [8], F32); cb += 32
    c = max(ca, cb)
    stt = view(c, [4, 6], F32); c += 96
    mv2 = view(c, [2], F32); c += 32
    sm = view(c, [16], F32); c += 64
    epsb = view(c, [1], F32); c += 32
    assert c <= ARENA, c
    SM_MX, SM_NEGM, SM_RS, SM_ES, SM_DEN, SM_RDEN, SM_SD, SM_RSTD, SM_NMR = range(9)
    A_TABLE_TOKS = ["hTh", "hhb", "dist", "dist0", "sinkb", "halob"]

    def smc(i):
        return sm[:, i:i + 1]

    def psb(b):
        return ps[:, b, :].bitcast(BF16)

    P.op("pool", lambda e: e.memset(identf, 0.0), writes=["identf"])
    P.op("pool", lambda e: e.affine_select(out=identf, in_=identf, pattern=[[-1, 128]], base=0,
                                           channel_multiplier=1, compare_op=ALU.not_equal, fill=1.0),
         reads=["identf"], writes=["identf"])
    P.op("pool", lambda e: e.tensor_copy(out=ident, in_=identf), reads=["identf"], writes=["ident"])
    P.op("pool", lambda e: e.memset(epsb, EPS), writes=["epsb"])

    def setup_A():
        P.op("pool", lambda e: e.iota(dist, pattern=[[-1, 256]], base=128, channel_multiplier=1,
                                      allow_small_or_imprecise_dtypes=True), writes=["dist"])
        P.op("pool", lambda e: e.affine_select(out=dist, in_=dist, pattern=[[-1, 256]], base=128,
                                               channel_multiplier=1, compare_op=ALU.is_ge, fill=BIG),
             reads=["dist"], writes=["dist"])
        P.op("pool", lambda e: e.affine_select(out=dist, in_=dist, pattern=[[1, 256]], base=-1,
                                               channel_multiplier=-1, compare_op=ALU.is_ge, fill=BIG),
             reads=["dist"], writes=["dist"])
        P.dma("sp", lambda e: e.dma_start(out=halob, in_=halobig), "misc", writes=["halob"])
        P.dma("sp", lambda e: e.dma_start(out=sinkb, in_=sinks.partition_broadcast(128)), "misc", writes=["sinkb"])
        P.op("pool", lambda e: e.tensor_copy(out=dist0, in_=dist), reads=["dist"], writes=["dist0"])
        P.op("pool", lambda e: e.tensor_scalar(out=dist0[:, 0:128], in0=dist0[:, 0:128], scalar1=halob[:, 0:1],
                                               scalar2=None, op0=ALU.max),
             reads=["dist0", "halob"], writes=["dist0"])

    def setup_B():
        P.op("pool", lambda e: e.iota(R0, pattern=[[1, 2048]], base=0, channel_multiplier=0,
                                      allow_small_or_imprecise_dtypes=True), writes=["R0"] + A_TABLE_TOKS)
        P.op("pool", lambda e: e.memset(caus, 0.0), writes=["caus"] + A_TABLE_TOKS)
        P.op("pool", lambda e: e.affine_select(out=caus, in_=caus, pattern=[[-1, 128]], base=0,
                                               channel_multiplier=1, compare_op=ALU.is_ge, fill=NEG),
             reads=["caus"], writes=["caus"])
        P.dma("sp", lambda e: e.dma_start(out=pmask, in_=pastmask), "misc", writes=["pmask"] + A_TABLE_TOKS)
        P.op("pool", lambda e: e.memset(gbias, NEG), writes=["gbias"] + A_TABLE_TOKS)
        for jl in range(4):
            P.op("pool", lambda e, jl=jl: e.memset(gbias[:, jl, 0:4 + jl], 0.0), reads=["gbias"], writes=["gbias"])
            P.op("pool", lambda e, jl=jl: e.tensor_scalar(out=gbias[:, jl, 0:4], in0=gbias[:, jl, 0:4],
                                                          scalar1=pmask[:, 0:1], scalar2=None, op0=ALU.add),
                 reads=["pmask", "gbias"], writes=["gbias"])

    ring_state = {"n": 0}

    def load_unit_cols(W, c0):
        i = ring_state["n"] % 2
        ring_state["n"] += 1
        src = W.rearrange("(c p) f -> p c f", p=128)[:, :, c0:c0 + 512]
        P.dma("pool", lambda e: e.dma_start(out=ring[i], in_=src), "ring%d" % i, writes=[("ring", i)])
        return i

    def load_unit_rows(W, r0):
        i = ring_state["n"] % 2
        ring_state["n"] += 1
        src = W[r0:r0 + 512, :].rearrange("(c p) d -> p c d", p=128)
        P.dma("pool", lambda e: e.dma_start(out=ringd[i], in_=src), "ring%d" % i, writes=[("ring", i)])
        return i

    bank_rr = {"n": 0}

    def mm_group(out_ap, pairs, reads, writes):
        n = len(pairs)
        fns = []
        for i, (l, r) in enumerate(pairs):
            fns.append(lambda e, l=l, r=r, i=i: e.matmul(out=out_ap, lhsT=l, rhs=r, start=(i == 0), stop=(i == n - 1)))
        P.op("pe", fns, reads=reads, writes=writes)

    def transposes(srcs, reads, writes):
        fns = [lambda e, d=d, s=s: e.transpose(out=d, in_=s, identity=ident) for d, s in srcs]
        P.op("pe", fns, reads=list(reads) + ["ident"], writes=writes)

    pb6 = ps[:, 6:8, :].bitcast(BF16).rearrange("p a b -> p (a b)")

    def to_hT(src_ap, src_tok, dst, dst_cols, dst_tok):
        srcs = [(pb6[:, cc * 128:(cc + 1) * 128], src_ap[:, cc * 128:(cc + 1) * 128]) for cc in range(16)]
        transposes(srcs, reads=[src_tok], writes=[("ps", 6), ("ps", 7)])
        P.op("act", lambda e: e.activation(out=dst[:, :, dst_cols], in_=pb6.rearrange("p (a b) -> p a b", b=128),
                                           func=AF.Copy),
             reads=[("ps", 6), ("ps", 7)], writes=[dst_tok])

    hT_all = [("hT", t) for t in range(NT)]

    def evac_act(out_ap, in_ap, scale, reads, writes):
        P.op("act", lambda e: e.activation(out=out_ap, in_=in_ap, func=AF.Copy, scale=scale), reads=reads, writes=writes)

    def next_bank(lo=0, hi=6):
        b = lo + bank_rr["n"] % (hi - lo)
        bank_rr["n"] += 1
        return b

    def proj_feat(slot, j, dst, dst_slot, col_off, scale, tokname):
        for th in range(2):
            b = next_bank()
            pairs = [(ring[slot][:, cc, j * 128:(j + 1) * 128], hT[:, cc, th * 512:(th + 1) * 512]) for cc in range(16)]
            mm_group(ps[:, b, :], pairs, reads=[("ring", slot)] + hT_all[th * 4:th * 4 + 4], writes=[("ps", b)])
            evac_act(dst[:, dst_slot, col_off + th * 512: col_off + (th + 1) * 512], ps[:, b, :], scale,
                     reads=[("ps", b)], writes=[(tokname, dst_slot, 4 * th + k) for k in range(4)])

    cnt = {"ot": 0, "pt": 0, "s": 0, "l": 0}
    pend = {"y": None, "x2": None}
    KV_ALIAS = ["gtab", "btab", ("hb", 0), ("hb", 1), "kvf"]

    def softmax_pv(isA, h_slot, t, nk, s_psum, sink_col, v_aps, s_banks, li):
        L = Lb[li]
        ltok = ("L", li)

        def x2_part():
            src = L[:, 0:nk] if s_psum is None else s_psum
            src_reads = [ltok] if s_psum is None else [("ps", b) for b in s_banks]
            P.op("dve", lambda e: e.reduce_max(out=smc(SM_MX), in_=src, axis=AX.X), reads=src_reads, writes=["mx"])
            if sink_col is not None:
                P.op("dve", lambda e: e.tensor_scalar(out=smc(SM_NEGM), in0=smc(SM_MX),
                                                      scalar1=sinkb[:, sink_col:sink_col + 1],
                                                      scalar2=-1.0, op0=ALU.max, op1=ALU.mult),
                     reads=["mx", "sinkb"], writes=["negm"])
            else:
                P.op("dve", lambda e: e.tensor_scalar(out=smc(SM_NEGM), in0=smc(SM_MX), scalar1=-1.0, scalar2=None,
                                                      op0=ALU.mult), reads=["mx"], writes=["negm"])
            P.op("act", lambda e: e.activation(out=L[:, 0:nk], in_=src, func=AF.Exp, bias=smc(SM_NEGM), scale=1.0,
                                               accum_out=smc(SM_RS)),
                 reads=src_reads + ["negm"], writes=[ltok, "rs"])
            if sink_col is not None:
                P.op("act", lambda e: e.activation(out=smc(SM_ES), in_=sinkb[:, sink_col:sink_col + 1], func=AF.Exp,
                                                   bias=smc(SM_NEGM), scale=1.0), reads=["negm", "sinkb"], writes=["es"])
                P.op("dve", lambda e: e.tensor_tensor(out=smc(SM_DEN), in0=smc(SM_RS), in1=smc(SM_ES), op=ALU.add),
                     reads=["rs", "es"], writes=["den"])
                P.op("dve", lambda e: e.reciprocal(out=smc(SM_RDEN), in_=smc(SM_DEN)), reads=["den"], writes=["rden"])
            else:
                P.op("dve", lambda e: e.reciprocal(out=smc(SM_RDEN), in_=smc(SM_RS)), reads=["rs"], writes=["rden"])
            pi = cnt["pt"] % 2
            cnt["pt"] += 1
            Pn = Pnb[pi]
            P.op("act", lambda e: e.activation(out=Pn[:, 0:nk], in_=L[:, 0:nk], func=AF.Identity,
                                               scale=smc(SM_RDEN), bias=0.0),
                 reads=[ltok, "rden"], writes=[("Pn", pi)])

            def y_part():
                softmax_y(isA, h_slot, t, nk, v_aps, pi)
            return y_part

        xprev = pend["x2"]
        pend["x2"] = x2_part
        if xprev is not None:
            ynew = xprev()
            if pend["y"] is not None:
                pend["y"]()
            pend["y"] = ynew

    def flush_y():
        ylast = None
        if pend["x2"] is not None:
            ylast = pend["x2"]()
            pend["x2"] = None
        if pend["y"] is not None:
            pend["y"]()
            pend["y"] = None
        if ylast is not None:
            ylast()

    def softmax_y(isA, h_slot, t, nk, v_aps, pi):
        Pn = Pnb[pi]
        nb = nk // 128
        if nb <= 8:
            pbank = [6 + pi]
            pdst = psb(6 + pi)
        else:
            pbank = [6, 7]
            pdst = pb6
        srcs = [(pdst[:, k * 128:(k + 1) * 128], Pn[:, k * 128:(k + 1) * 128]) for k in range(nb)]
        transposes(srcs, reads=[("Pn", pi)], writes=[("ps", b) for b in pbank])
        P.op("act", lambda e: e.activation(out=PT[pi][:, 0:nb, :],
                                           in_=pdst[:, 0:nb * 128].rearrange("p (a b) -> p a b", b=128), func=AF.Copy),
             reads=[("ps", b) for b in pbank], writes=[("PT", pi)])
        ob = (4 + cnt["ot"] % 2) if isA else 4
        cnt["ot"] += 1
        pairs = [(v_aps[k][0], PT[pi][:, k, :]) for k in range(nb)]
        vreads = sorted(set(v_aps[k][1] for k in range(nb)), key=str)
        mm_group(ps[:, ob, 0:128], pairs, reads=[("PT", pi)] + vreads, writes=[("ps", ob)])
        P.op("act", lambda e: e.activation(out=qT[:, h_slot, t * 128:(t + 1) * 128], in_=ps[:, ob, 0:128], func=AF.Copy),
             reads=[("ps", ob)], writes=[("qT", h_slot, t)])

    def load_gb(layer, i):
        P.dma("sp", lambda e: e.dma_start(out=gtab, in_=lng_all[layer, i:i + 1, :].partition_broadcast(128)), "misc",
              writes=["gtab"])
        P.dma("sp", lambda e: e.dma_start(out=btab, in_=lnb_all[layer, i:i + 1, :].partition_broadcast(128)), "misc",
              writes=["btab"])

    hout_v = hout.rearrange("(n p) d -> p n d", p=128)

    def layer_norm(t, store, make_hT, halo_send):
        for q4 in range(4):
            P.op("dve", lambda e, q4=q4: e.bn_stats(out=stt[:, q4, :], in_=h[:, t, q4 * 512:(q4 + 1) * 512]),
                 reads=[("h", t)], writes=[("stt", q4)])
        P.op("dve", lambda e: e.bn_aggr(out=mv2, in_=stt.rearrange("p a b -> p (a b)")),
             reads=[("stt", q4) for q4 in range(4)], writes=["mv2"])
        P.op("act", lambda e: e.activation(out=smc(SM_SD), in_=mv2[:, 1:2], func=AF.Sqrt, bias=epsb[:, 0:1], scale=1.0),
             reads=["mv2", "epsb"], writes=["sd"])
        P.op("dve", lambda e: e.reciprocal(out=smc(SM_RSTD), in_=smc(SM_SD)), reads=["sd"], writes=["rstd"])
        P.op("dve", lambda e: e.scalar_tensor_tensor(out=smc(SM_NMR), in0=mv2[:, 0:1], scalar=-1.0, in1=smc(SM_RSTD),
                                                     op0=ALU.mult, op1=ALU.mult), reads=["mv2", "rstd"], writes=["nmr"])
        P.op("act", lambda e: e.activation(out=h[:, t, :], in_=h[:, t, :], func=AF.Identity, scale=smc(SM_RSTD),
                                           bias=smc(SM_NMR)), reads=[("h", t), "rstd", "nmr"], writes=[("h", t)])
        P.op("pool", lambda e: e.tensor_tensor(out=h[:, t, :], in0=h[:, t, :], in1=gtab, op=ALU.mult),
             reads=[("h", t), "gtab"], writes=[("h", t)])
        P.op("dve", lambda e: e.tensor_tensor(out=h[:, t, :], in0=h[:, t, :], in1=btab, op=ALU.add),
             reads=[("h", t), "btab"], writes=[("h", t)])
        if store:
            P.dma("sp", lambda e: e.dma_start(out=hout_v[:, t, :], in_=h[:, t, :]), "hstore%d" % (t % 4), reads=[("h", t)])
        if make_hT:
            i = t % 2
            P.op("act", lambda e: e.activation(out=hb[i], in_=h[:, t, :], func=AF.Copy),
                 reads=[("h", t)], writes=[("hb", i)])
            to_hT(hb[i], ("hb", i), hT, slice(t * 128, (t + 1) * 128), ("hT", t))
            if halo_send and t == NT - 1:
                P.dma("sp", lambda e: e.dma_start(out=halo_in.ap(), in_=h[:, t, :]), "halo", reads=[("h", t)],
                      writes=["halo_in"])
                P.special("pool", lambda e: e.collective_compute(
                    "AllGather", ALU.bypass, replica_groups=groups, ins=[halo_in.ap().opt()],
                    outs=[halo_out.ap().opt()]), "halo", reads=["halo_in"], writes=["halo_out"])

    xin_v = xin.rearrange("(n p) d -> p n d", p=128)
    for t in range(NT):
        P.dma("sp", lambda e, t=t: e.dma_start(out=h[:, t, :], in_=xin_v[:, t, :]), "hload%d" % (t % 4), writes=[("h", t)])
    for t in range(NT):
        i = t % 2
        P.op("act", lambda e, t=t, i=i: e.activation(out=hb[i], in_=h[:, t, :], func=AF.Copy),
             reads=[("h", t)], writes=[("hb", i)])
        to_hT(hb[i], ("hb", i), hT, slice(t * 128, (t + 1) * 128), ("hT", t))
    setup_A()

    for layer in range(nlayers):
        isA = layer < 2
        last = layer == nlayers - 1
        w_mkv = w_mem_kv[layer]
        w_o = w_o_all[layer]
        w_up = w_up_all[layer]
        w_dn = w_dn_all[layer]
        if layer > 0:
            P.barrier()
        if isA:
            w_qkv = w_in_a[layer]
            if layer == 0:
                P.dma("pool", lambda e: e.dma_start(out=hhb, in_=xhalo), "misc2", writes=["hhb"])
            else:
                P.dma("pool", lambda e: e.dma_start(out=hhb, in_=halo_out.ap()[0:128, :]), "misc2", reads=["halo_out"],
                      writes=["hhb"])
            to_hT(hhb, "hhb", hTh, slice(0, 128), "hTh")
            units = [("q", 0), ("q", 1), ("q", 2), ("k", 3), ("v", 4), ("qm", 5)]
        else:
            w_qkv = w_q_b[layer - 2]
            if layer == 2:
                setup_B()
            units = [("q", 0), ("q", 1), ("q", 2), ("qm", 3)]
            kparts = [(kout_ap[0:128, :], kT[:, :, 0:1024]), (kin_ap, kT[:, :, 1024:2048])]
            for si, di in kparts:
                P.dma("pool", lambda e, si=si, di=di: e.dma_start(out=di, in_=si.rearrange("p (a b) -> p a b", b=1024)),
                      "misc2", reads=["kout", "vout", "kmout", "kin", "vin", "kmin"], writes=["kT"] + KV_ALIAS)
            vparts = [(vout_ap[0:128, :], vv[:, 0:8, :]), (vin_ap, vv[:, 8:16, :])]
            for si, di in vparts:
                P.dma("pool", lambda e, si=si, di=di: e.dma_start(out=di, in_=si.rearrange("p (a b) -> p a b", b=512)),
                      "misc2", reads=["kout", "vout", "kmout", "kin", "vin", "kmin"], writes=["vv"] + KV_ALIAS)
            mparts = [(kmout_ap[0:128, 0:16], kmT[:, :, 0:4]), (kmin_ap[:, 0:16], kmT[:, :, 4:8])]
            for si, di in mparts:
                P.dma("pool", lambda e, si=si, di=di: e.dma_start(out=di, in_=si.rearrange("p (a b) -> p a b", b=4)),
                      "misc2", reads=["kout", "vout", "kmout", "kin", "vin", "kmin"], writes=["kmT"] + (A_TABLE_TOKS if layer == 2 else []))
        P.mark("L%d proj" % layer)
        for typ, u in units:
            slot = load_unit_cols(w_qkv, u * 512)
            if typ in ("q", "qm"):
                base = (u * 4) if typ == "q" else 12
                for j in range(4):
                    proj_feat(slot, j, qT, base + j, 0, SC, "qT")
            elif typ == "k":
                for j in range(4):
                    proj_feat(slot, j, kT, j, 128, 1.0, "kTt")
                    b = next_bank()
                    pairs = [(ring[slot][:, cc, j * 128:(j + 1) * 128], hTh[:, cc, :]) for cc in range(16)]
                    mm_group(ps[:, b, 0:128], pairs, reads=[("ring", slot), "hTh"], writes=[("ps", b)])
                    evac_act(kT[:, j, 0:128], ps[:, b, 0:128], 1.0, reads=[("ps", b)], writes=[("kTh", j)])
            elif typ == "v":
                for t in range(-1, NT):
                    b = next_bank()
                    if t < 0:
                        pairs = [(hTh[:, cc, :], ring[slot][:, cc, :]) for cc in range(16)]
                        rd = ["hTh"]
                    else:
                        pairs = [(hT[:, cc, t * 128:(t + 1) * 128], ring[slot][:, cc, :]) for cc in range(16)]
                        rd = [("hT", t)]
                    mm_group(ps[:, b, :], pairs, reads=[("ring", slot)] + rd, writes=[("ps", b)])
                    evac_act(vv[:, t + 1, :], ps[:, b, :], 1.0, reads=[("ps", b)], writes=[("vvt", t + 1)])
        P.mark("L%d memkv" % layer)
        P.barrier()
        P.dma("pool", lambda e: e.dma_start(out=memb, in_=mem.rearrange("(n p) d -> p n d", p=128)), "misc2",
              writes=["memb"] + hT_all)
        for mt in range(2):
            srcs = [(pb6[:, cc * 128:(cc + 1) * 128], memb[:, mt, cc * 128:(cc + 1) * 128]) for cc in range(16)]
            transposes(srcs, reads=["memb"], writes=[("ps", 6), ("ps", 7)])
            P.op("act", lambda e, mt=mt: e.activation(out=memT[:, :, mt * 128:(mt + 1) * 128],
                                                      in_=pb6.rearrange("p (a b) -> p a b", b=128), func=AF.Copy),
                 reads=[("ps", 6), ("ps", 7)], writes=["memT"])
        slot = load_unit_cols(w_mkv, 0)
        for j in range(4):
            b = next_bank()
            pairs = [(ring[slot][:, cc, j * 128:(j + 1) * 128], memT[:, cc, :]) for cc in range(16)]
            mm_group(ps[:, b, 0:256], pairs, reads=[("ring", slot), "memT"], writes=[("ps", b)])
            evac_act(mkT[:, j, :], ps[:, b, 0:256], 1.0, reads=[("ps", b)], writes=["mkT"])
        slot = load_unit_cols(w_mkv, 512)
        for mt in range(2):
            b = next_bank()
            pairs = [(memT[:, cc, mt * 128:(mt + 1) * 128], ring[slot][:, cc, :]) for cc in range(16)]
            mm_group(ps[:, b, :], pairs, reads=[("ring", slot), "memT"], writes=[("ps", b)])
            evac_act(mv[:, mt, :], ps[:, b, :], 1.0, reads=[("ps", b)], writes=["mv"])
        P.barrier()
        P.mark("L%d att" % layer)
        for hh in range(12):
            kvh = hh // 3
            for t in range(NT):
                qap = qT[:, hh, t * 128:(t + 1) * 128]
                if isA:
                    sb = cnt["s"] % 4
                    cnt["s"] += 1
                    mm_group(ps[:, sb, 0:256], [(qap, kT[:, kvh, t * 128:t * 128 + 256])],
                             reads=[("qT", hh, t), ("kTh", kvh)] + [("kTt", kvh, k) for k in range(NT)],
                             writes=[("ps", sb)])
                    dtab = dist0 if t == 0 else dist
                    li = cnt["l"] % 2
                    cnt["l"] += 1
                    L = Lb[li]
                    P.op("dve", lambda e, dtab=dtab, sb=sb, hh=hh, L=L: e.scalar_tensor_tensor(
                        out=L[:, 0:256], in0=dtab, scalar=-SLOPES[hh], in1=ps[:, sb, 0:256], op0=ALU.mult, op1=ALU.add),
                        reads=[("ps", sb), "dist", "dist0"], writes=[("L", li)])
                    v_aps = [(vv[:, t + k, kvh * 128:(kvh + 1) * 128], ("vvt", t + k)) for k in range(2)]
                    softmax_pv(True, hh, t, 256, None, layer * 12 + hh, v_aps, None, li)
                else:
                    jl = t // 2
                    nk = 1024 + (t + 1) * 128
                    nmm = (nk + 511) // 512
                    fns = []
                    for m in range(nmm):
                        w = min(512, nk - m * 512)
                        fns.append(lambda e, m=m, w=w, qap=qap, kvh=kvh: e.matmul(
                            out=ps[:, m, 0:w], lhsT=qap, rhs=kT[:, kvh, m * 512:m * 512 + w], start=True, stop=True))
                    fns.append(lambda e, qap=qap, kvh=kvh: e.matmul(out=ps[:, 5, 0:8], lhsT=qap, rhs=kmT[:, kvh, :],
                                                                    start=True, stop=True))
                    P.op("pe", fns, reads=[("qT", hh, t), "kT", "kmT"],
                         writes=[("ps", m) for m in range(nmm)] + [("ps", 5)])
                    P.op("dve", lambda e, jl=jl: e.tensor_tensor(out=gm, in0=ps[:, 5, 0:8], in1=gbias[:, jl, :], op=ALU.add),
                         reads=[("ps", 5), "gbias"], writes=["gm"])
                    P.op("dve", lambda e: e.max(out=top8, in_=gm), reads=["gm"], writes=["top8"])
                    P.op("dve", lambda e: e.tensor_scalar(out=bm, in0=gm, scalar1=top8[:, 2:3], scalar2=NEG,
                                                          op0=ALU.is_lt, op1=ALU.mult), reads=["gm", "top8"], writes=["bm"])
                    P.op("dve", lambda e, jl=jl: e.tensor_tensor(out=bm, in0=bm, in1=gbias[:, jl, :], op=ALU.add),
                         reads=["bm", "gbias"], writes=["bm"])
                    sall = ps[:, 0:4, :].rearrange("p a b -> p (a b)")
                    li = cnt["l"] % 2
                    cnt["l"] += 1
                    L = Lb[li]
                    P.op("dve", lambda e, nk=nk, hh=hh, sall=sall, L=L: e.scalar_tensor_tensor(
                        out=L[:, 0:nk], in0=R0[:, 0:nk], scalar=SLOPES[hh], in1=sall[:, 0:nk], op0=ALU.mult, op1=ALU.add),
                        reads=[("ps", m) for m in range(nmm)] + ["R0"], writes=[("L", li)])
                    nblk = 4 + jl
                    L3 = L[:, 0:nblk * 256].rearrange("p (a b) -> p a b", b=256)
                    bmb = bm[:, 0:nblk].unsqueeze(2).to_broadcast([128, nblk, 256])
                    P.op("pool", lambda e, L3=L3, bmb=bmb: e.tensor_tensor(out=L3, in0=L3, in1=bmb, op=ALU.add),
                         reads=[("L", li), "bm"], writes=[("L", li)])
                    P.op("pool", lambda e, nk=nk, L=L: e.tensor_tensor(out=L[:, nk - 128:nk], in0=L[:, nk - 128:nk],
                                                                       in1=caus, op=ALU.add),
                         reads=[("L", li), "caus"], writes=[("L", li)])
                    v_aps = [(vv[:, k, kvh * 128:(kvh + 1) * 128], "vv") for k in range(nk // 128)]
                    softmax_pv(False, hh, t, nk, None, None, v_aps, None, li)
        for j in range(4):
            for t in range(NT):
                sb = cnt["s"] % 4
                cnt["s"] += 1
                mm_group(ps[:, sb, 0:256], [(qT[:, 12 + j, t * 128:(t + 1) * 128], mkT[:, j, :])],
                         reads=[("qT", 12 + j, t), "mkT"], writes=[("ps", sb)])
                v_aps = [(mv[:, k, j * 128:(j + 1) * 128], "mv") for k in range(2)]
                li = cnt["l"] % 2
                cnt["l"] += 1
                softmax_pv(isA, 12 + j, t, 256, ps[:, sb, 0:256], None, v_aps, [sb], li)
        flush_y()
        P.mark("L%d wo" % layer)
        P.barrier()
        load_gb(layer, 0)
        for dq in range(4):
            slot = load_unit_cols(w_o, dq * 512)
            for t in range(NT):
                b = next_bank(0, 6)
                pairs = [(qT[:, cc, t * 128:(t + 1) * 128], ring[slot][:, cc, :]) for cc in range(16)]
                mm_group(ps[:, b, :], pairs, reads=[("ring", slot)] + [("qT", s, t) for s in range(16)],
                         writes=[("ps", b)])
                P.op("dve", lambda e, t=t, dq=dq, b=b: e.scalar_tensor_tensor(
                    out=h[:, t, dq * 512:(dq + 1) * 512], in0=h[:, t, dq * 512:(dq + 1) * 512], scalar=ALPHA,
                    in1=ps[:, b, :], op0=ALU.mult, op1=ALU.add), reads=[("ps", b), ("h", t)], writes=[("h", t)])
        NG = 16
        plan = [("up", 0)]
        for g in range(1, NG):
            plan += [("up", g), ("dn", g - 1)]
        plan += [("dn", NG - 1)]
        plan_slot = {}
        plan_pos = {"n": 0}

        def plan_issue():
            k = plan_pos["n"]
            if k >= len(plan):
                return
            plan_pos["n"] += 1
            typ, g = plan[k]
            plan_slot[(typ, g)] = load_unit_cols(w_up, g * 512) if typ == "up" else load_unit_rows(w_dn, g * 512)

        plan_issue()
        plan_issue()
        P.mark("L%d ln1" % layer)
        P.barrier()
        for t in range(NT):
            layer_norm(t, False, True, False)
        P.mark("L%d ffn" % layer)
        P.barrier()
        load_gb(layer, 1)
        upb = {"n": 0}
        dnb = {"n": 0}
        rt = {"n": 0}

        def ffn_up(g):
            slot = plan_slot[("up", g)]
            ai = g % 2
            for j in range(4):
                for th in range(2):
                    b = upb["n"] % 4
                    upb["n"] += 1
                    pairs = [(ring[slot][:, cc, j * 128:(j + 1) * 128], hT[:, cc, th * 512:(th + 1) * 512])
                             for cc in range(16)]
                    mm_group(ps[:, b, :], pairs, reads=[("ring", slot)] + hT_all[th * 4:th * 4 + 4], writes=[("ps", b)])
                    ri = rt["n"] % 2
                    rt["n"] += 1
                    P.op("act", lambda e, b=b, ri=ri: e.activation(out=rtmp[ri], in_=ps[:, b, :], func=AF.Relu),
                         reads=[("ps", b)], writes=[("rtmp", ri)])
                    P.op("pool", lambda e, ri=ri, ai=ai, j=j, th=th: e.tensor_tensor(
                        out=actT[ai][:, j, th * 512:(th + 1) * 512], in0=rtmp[ri], in1=rtmp[ri], op=ALU.mult),
                        reads=[("rtmp", ri)], writes=[("actT", ai, j, th)])

        def ffn_down(g):
            slot = plan_slot[("dn", g)]
            ai = g % 2
            for t in range(NT):
                for dq in range(4):
                    b = 4 + dnb["n"] % 4
                    dnb["n"] += 1
                    pairs = [(actT[ai][:, j, t * 128:(t + 1) * 128], ringd[slot][:, j, dq * 512:(dq + 1) * 512])
                             for j in range(4)]
                    mm_group(ps[:, b, :], pairs, reads=[("ring", slot)] + [("actT", ai, j, t // 4) for j in range(4)],
                             writes=[("ps", b)])
                    hs = h[:, t, dq * 512:(dq + 1) * 512]
                    if g == 0:
                        P.op("dve", lambda e, hs=hs, b=b: e.scalar_tensor_tensor(out=hs, in0=hs, scalar=ALPHA,
                                                                                 in1=ps[:, b, :], op0=ALU.mult, op1=ALU.add),
                             reads=[("ps", b), ("h", t)], writes=[("h", t)])
                    else:
                        P.op("dve", lambda e, hs=hs, b=b: e.tensor_tensor(out=hs, in0=hs, in1=ps[:, b, :], op=ALU.add),
                             reads=[("ps", b), ("h", t)], writes=[("h", t)])

        for g in range(NG):
            ffn_up(g)
            plan_issue()
            if g > 0:
                ffn_down(g - 1)
                plan_issue()
        ffn_down(NG - 1)
        P.mark("L%d ln2" % layer)
        P.barrier()
        for t in range(NT):
            layer_norm(t, last, not last, layer == 0 and nlayers > 1)
        if layer == 1 and nlayers > 2:
            P.barrier()
            for u in range(2):
                slot = load_unit_cols(w_kvs, u * 512)
                if u == 0:
                    for j in range(4):
                        for th in range(2):
                            b = next_bank()
                            pairs = [(ring[slot][:, cc, j * 128:(j + 1) * 128], hT[:, cc, th * 512:(th + 1) * 512])
                                     for cc in range(16)]
                            mm_group(ps[:, b, :], pairs, reads=[("ring", slot)] + hT_all[th * 4:th * 4 + 4],
                                     writes=[("ps", b)])
                            P.op("act", lambda e, b=b: e.activation(out=kvf, in_=ps[:, b, :], func=AF.Copy),
                                 reads=[("ps", b)], writes=["kvf"])
                            co = j * 1024 + th * 512
                            P.dma("sp", lambda e, co=co: e.dma_start(out=kin_ap[:, co:co + 512], in_=kvf),
                                  "kvst", reads=["kvf"], writes=["kin"])
                            P.op("dve", lambda e, j=j, th=th: e.tensor_reduce(
                                out=kmacc[:, j, 2 * th:2 * th + 2], in_=kvf.rearrange("p (a b) -> p a b", b=256),
                                axis=AX.X, op=ALU.add), reads=["kvf"], writes=["kmacc"])
                    P.op("dve", lambda e: e.tensor_scalar(out=kmacc, in0=kmacc, scalar1=1.0 / 256.0, scalar2=None,
                                                          op0=ALU.mult), reads=["kmacc"], writes=["kmacc"])
                else:
                    for t in range(NT):
                        b = next_bank()
                        pairs = [(hT[:, cc, t * 128:(t + 1) * 128], ring[slot][:, cc, :]) for cc in range(16)]
                        mm_group(ps[:, b, :], pairs, reads=[("ring", slot), ("hT", t)], writes=[("ps", b)])
                        P.op("act", lambda e, b=b: e.activation(out=kvf, in_=ps[:, b, :], func=AF.Copy),
                             reads=[("ps", b)], writes=["kvf"])
                        co = t * 512
                        P.dma("sp", lambda e, co=co: e.dma_start(out=vin_ap[:, co:co + 512], in_=kvf), "kvst",
                              reads=["kvf"], writes=["vin"])
            P.op("dve", lambda e: e.memset(kvf[:, 0:256], 0.0), writes=["kvf"])
            P.op("dve", lambda e: e.tensor_copy(out=kvf[:, 0:16].rearrange("p (a b) -> p a b", b=4), in_=kmacc),
                 reads=["kmacc", "kvf"], writes=["kvf"])
            P.dma("sp", lambda e: e.dma_start(out=kmin_ap, in_=kvf[:, 0:256]), "kvst2", reads=["kvf"], writes=["kmin"])
            for nm, ti, to in (("kx", kin, kout), ("vx", vin, vout), ("kmx", kmin, kmout)):
                P.special("pool", lambda e, ti=ti, to=to: e.collective_compute(
                    "AllGather", ALU.bypass, replica_groups=groups, ins=[ti.ap().opt()], outs=[to.ap().opt()]),
                    nm, reads=[ti.name], writes=[to.name])
    P.mark("end")
    P.emit()
    nc._marks = P.marks
    return nc


_CACHE = {}
_DBG = None


def _prep(x, mem, w_in_a, sinks_a, w_q_b, w_kv_shared, w_mem_kv, w_o, w_up, w_down, ln_g, ln_b, ncores=8):
    f = lambda a: np.ascontiguousarray(np.asarray(a, dtype=np.float32))
    x = f(x); mem = f(mem)
    shared = {"w_in_a": f(w_in_a), "sinks": f(sinks_a).reshape(1, 24), "w_q_b": f(w_q_b), "w_kvs": f(w_kv_shared),
              "w_mem_kv": f(w_mem_kv), "w_o": f(w_o), "w_up": f(w_up), "w_dn": f(w_down), "lng": f(ln_g), "lnb": f(ln_b)}
    zeros_halo = np.zeros((128, 2048), np.float32)
    in_maps = []
    for c in range(ncores):
        b, hf = c // 2, c % 2
        m = dict(shared)
        m["x_own"] = np.ascontiguousarray(x[b, hf * 1024:(hf + 1) * 1024])
        m["x_halo"] = np.ascontiguousarray(x[b, 896:1024]) if hf == 1 else zeros_halo
        m["halobig"] = np.full((128, 1), 0.0 if hf == 1 else BIG, np.float32)
        m["pastmask"] = np.full((128, 1), 0.0 if hf == 1 else NEG, np.float32)
        m["mem"] = np.ascontiguousarray(mem[b])
        in_maps.append(m)
    return in_maps


def kernel(x, mem, w_in_a, sinks_a, w_q_b, w_kv_shared, w_mem_kv, w_o, w_up, w_down, ln_g, ln_b):
    ncores = 8
    if "nc" not in _CACHE:
        _CACHE["nc"] = build_fused(4, ncores)
    nc = _CACHE["nc"]
    in_maps = _prep(x, mem, w_in_a, sinks_a, w_q_b, w_kv_shared, w_mem_kv, w_o, w_up, w_down, ln_g, ln_b, ncores)
    res = run_bass_kernel_spmd(nc, in_maps, core_ids=list(range(ncores)))
    out = np.empty((4, 2048, 2048), np.float32)
    for c in range(ncores):
        b, hf = c // 2, c % 2
        out[b, hf * 1024:(hf + 1) * 1024] = res.results[c]["hout"]
    return out
```

```python
import numpy as np
import concourse.bass as bass
import concourse.mybir as mybir
from concourse.bass_utils import run_bass_kernel_spmd

F32 = mybir.dt.float32
BF16 = mybir.dt.bfloat16
U8 = mybir.dt.uint8
AF = mybir.ActivationFunctionType
ALU = mybir.AluOpType
AX = mybir.AxisListType

NEG = -30000.0
BIG = 1.0e6
SC = 128.0 ** -0.5
ALPHA = 8.0 ** 0.25
EPS = 1e-5
SLOPES = [2.0 ** (-8.0 * (i + 1) / 12.0) for i in range(12)]
NT = 8
TOK = 1024


class Prog:
    def __init__(self, nc):
        self.nc = nc
        self.ops = {e: [] for e in ("pe", "act", "dve", "pool", "sp")}
        self.cnt = {}
        self.seen = {e: {} for e in self.ops}
        self.lastw = {}
        self.readers = {}
        self.ninst = {e: 0 for e in self.ops}
        self.marks = []

    def _deps(self, eng, reads, writes):
        need = {}

        def add(d):
            if d is None:
                return
            s, v = d
            if v > need.get(s, 0):
                need[s] = v
        for t in reads:
            add(self.lastw.get(t))
        for t in writes:
            add(self.lastw.get(t))
            for s, v in self.readers.get(t, {}).items():
                add((s, v))
        waits = []
        for s, v in need.items():
            if self.seen[eng].get(s, 0) < v:
                self.seen[eng][s] = v
                waits.append((s, v))
        return waits

    def _commit(self, stamp, reads, writes):
        s, v = stamp
        for t in reads:
            self.readers.setdefault(t, {})[s] = v
        for t in writes:
            self.lastw[t] = stamp
            self.readers[t] = {}

    def op(self, eng, fns, reads=(), writes=()):
        waits = self._deps(eng, reads, writes)
        self.cnt[eng] = self.cnt.get(eng, 0) + 1
        self._commit((eng, self.cnt[eng]), reads, writes)
        if not isinstance(fns, (list, tuple)):
            fns = [fns]
        self.ninst[eng] += len(fns)
        self.ops[eng].append((waits, list(fns), eng, 1))

    def dma(self, queue, fn, sem, reads=(), writes=()):
        waits = self._deps(queue, reads, writes)
        key = "d_" + sem
        prev = self.cnt.get(key, 0)
        if prev > 0 and self.seen[queue].get(key, 0) < prev:
            self.seen[queue][key] = prev
            waits.append((key, prev))
        self.cnt[key] = prev + 16
        self._commit((key, self.cnt[key]), reads, writes)
        self.ops[queue].append((waits, [fn], key, 16))

    def special(self, queue, fn, sem, reads=(), writes=()):
        waits = self._deps(queue, reads, writes)
        key = "c_" + sem
        assert key not in self.cnt
        self.cnt[key] = 1
        self._commit((key, 1), reads, writes)
        self.ops[queue].append((waits, [fn], key, None))

    def mark(self, name):
        self.marks.append((name, dict(self.ninst)))

    def barrier(self, engs=("pe", "act", "dve", "sp")):
        for e in engs:
            waits = []
            for s, v in self.cnt.items():
                if self.seen[e].get(s, 0) < v:
                    self.seen[e][s] = v
                    waits.append((s, v))
            if waits:
                self.ops[e].append((waits, [], None, 0))

    def emit(self, final_waits_engine="sp"):
        nc = self.nc
        fw = [(s, v) for s, v in self.cnt.items()]
        self.ops[final_waits_engine].append((fw, [], None, 0))
        import contextlib
        with contextlib.ExitStack() as st:
            sems = {}
            for s in self.cnt:
                sems[s] = st.enter_context(nc.semaphore("s_" + s))
            block = st.enter_context(nc.Block())
            engmap = {"pe": block.tensor, "act": block.scalar, "dve": block.vector,
                      "pool": block.gpsimd, "sp": block.sync}
            for ename, deco in engmap.items():
                ops = self.ops[ename]

                def body(e, ops=ops):
                    for waits, fns, incsem, incv in ops:
                        for s, v in waits:
                            e.wait_ge(sems[s], v)
                        ins = None
                        for f in fns:
                            ins = f(e)
                        if ins is not None and incsem is not None:
                            if incv is None:
                                ins.then_inc(sems[incsem])
                            else:
                                ins.then_inc(sems[incsem], incv)
                deco(body)


def build_fused(nlayers=4, ncores=8):
    nc = bass.Bass("TRN2", target_bir_lowering=False)
    P = Prog(nc)
    groups = [[2 * i, 2 * i + 1] for i in range(ncores // 2)]

    def din(name, shape):
        return nc.dram_tensor(name, shape, F32, kind="ExternalInput").ap()

    xin = din("x_own", [TOK, 2048])
    xhalo = din("x_halo", [128, 2048])
    halobig = din("halobig", [128, 1])
    pastmask = din("pastmask", [128, 1])
    mem = din("mem", [256, 2048])
    w_in_a = din("w_in_a", [2, 2048, 3072])
    sinks = din("sinks", [1, 24])
    w_q_b = din("w_q_b", [2, 2048, 2048])
    w_kvs = din("w_kvs", [2048, 1024])
    w_mem_kv = din("w_mem_kv", [4, 2048, 1024])
    w_o_all = din("w_o", [4, 2048, 2048])
    w_up_all = din("w_up", [4, 2048, 8192])
    w_dn_all = din("w_dn", [4, 8192, 2048])
    lng_all = din("lng", [4, 2, 2048])
    lnb_all = din("lnb", [4, 2, 2048])
    hout = nc.dram_tensor("hout", [TOK, 2048], F32, kind="ExternalOutput").ap()
    halo_in = nc.dram_tensor("halo_in", [128, 2048], F32)
    halo_out = nc.dram_tensor("halo_out", [256, 2048], F32)
    kin = nc.dram_tensor("kin", [128, 4096], F32)
    kout = nc.dram_tensor("kout", [256, 4096], F32)
    vin = nc.dram_tensor("vin", [128, 4096], F32)
    vout = nc.dram_tensor("vout", [256, 4096], F32)
    kmin = nc.dram_tensor("kmin", [128, 256], F32)
    kmout = nc.dram_tensor("kmout", [256, 256], F32)
    kin_ap, kout_ap, vin_ap, vout_ap, kmin_ap, kmout_ap = (kin.ap(), kout.ap(), vin.ap(), vout.ap(), kmin.ap(),
                                                            kmout.ap())

    ARENA = 212480
    arena = nc.alloc_sbuf_tensor("arena", [128, ARENA], U8)
    ps = nc.alloc_psum_tensor("ps", [128, 8, 512], F32)

    def view(off, shape, dt):
        n = 1
        for s in shape:
            n *= s
        nb = n * (4 if dt == F32 else 2)
        a = arena[:, off:off + nb].bitcast(dt)
        if len(shape) == 2:
            return a.rearrange("p (a b) -> p a b", b=shape[1])
        return a

    OH, ORING, OA, OB, OKV, OC = 0, 65536, 98304, 131072, 163840, 196608
    h = view(OH, [8, 2048], F32)
    ring = [view(ORING + i * 16384, [16, 512], BF16) for i in range(2)]
    ringd = [view(ORING + i * 16384, [4, 2048], BF16) for i in range(2)]
    hT = view(OA, [16, 1024], BF16)
    Lb = [view(OA, [2048], F32), view(OA + 24576, [2048], F32)]
    memb = view(OA, [2, 2048], BF16)
    memT = view(OA + 8192, [16, 256], BF16)
    Pnb = [view(OA + 8192, [2048], BF16), view(OA + 20480, [2048], BF16)]
    PT = [view(OA + 12288 + i * 4096, [16, 128], BF16) for i in range(2)]
    qT = view(OB, [16, 1024], BF16)
    actT = [view(OB + i * 8192, [4, 1024], BF16) for i in range(2)]
    rtmp = [view(OB + 16384 + i * 2048, [512], F32) for i in range(2)]
    kmacc = view(OB, [4, 4], F32)
    kT = view(OKV, [4, 2048], BF16)
    vv = view(OKV + 16384, [16, 512], BF16)
    gtab = view(OKV, [2048], F32)
    btab = view(OKV + 8192, [2048], F32)
    hb = [view(OKV + 16384 + i * 4096, [2048], BF16) for i in range(2)]
    kvf = view(OKV + 24576, [512], F32)
    c = OC
    mkT = view(c, [4, 256], BF16); c += 2048
    mv = view(c, [2, 512], BF16); c += 2048
    ident = view(c, [128], BF16); c += 256
    identf = view(c, [128], F32); c += 512
    ca = c
    hTh = view(ca, [16, 128], BF16); ca += 4096
    hhb = view(ca, [2048], BF16); ca += 4096
    dist = view(ca, [256], F32); ca += 1024
    dist0 = view(ca, [256], F32); ca += 1024
    sinkb = view(ca, [24], F32); ca += 96
    halob = view(ca, [1], F32); ca += 32
    cb = c
    R0 = view(cb, [2048], F32); cb += 8192
    caus = view(cb, [128], F32); cb += 512
    kmT = view(cb, [4, 8], BF16); cb += 64
    gbias = view(cb, [4, 8], F32); cb += 128
    pmask = view(cb, [1], F32); cb += 32
    gm = view(cb, [8], F32); cb += 32
    top8 = view(cb, [8], F32); cb += 32
    bm = view(cb, [8], F32); cb += 32
    c = max(ca, cb)
    stt = view(c, [4, 6], F32); c += 96
    mv2 = view(c, [2], F32); c += 32
    sm = view(c, [16], F32); c += 64
    epsb = view(c, [1], F32); c += 32
    assert c <= ARENA, c
    SM_MX, SM_NEGM, SM_RS, SM_ES, SM_DEN, SM_RDEN, SM_SD, SM_RSTD, SM_NMR = range(9)
    A_TABLE_TOKS = ["hTh", "hhb", "dist", "dist0", "sinkb", "halob"]

    def smc(i):
        return sm[:, i:i + 1]

    def psb(b):
        return ps[:, b, :].bitcast(BF16)

    P.op("pool", lambda e: e.memset(identf, 0.0), writes=["identf"])
    P.op("pool", lambda e: e.affine_select(out=identf, in_=identf, pattern=[[-1, 128]], base=0,
                                           channel_multiplier=1, compare_op=ALU.not_equal, fill=1.0),
         reads=["identf"], writes=["identf"])
    P.op("pool", lambda e: e.tensor_copy(out=ident, in_=identf), reads=["identf"], writes=["ident"])
    P.op("pool", lambda e: e.memset(epsb, EPS), writes=["epsb"])

    def setup_A():
        P.op("pool", lambda e: e.iota(dist, pattern=[[-1, 256]], base=128, channel_multiplier=1,
                                      allow_small_or_imprecise_dtypes=True), writes=["dist"])
        P.op("pool", lambda e: e.affine_select(out=dist, in_=dist, pattern=[[-1, 256]], base=128,
                                               channel_multiplier=1, compare_op=ALU.is_ge, fill=BIG),
             reads=["dist"], writes=["dist"])
        P.op("pool", lambda e: e.affine_select(out=dist, in_=dist, pattern=[[1, 256]], base=-1,
                                               channel_multiplier=-1, compare_op=ALU.is_ge, fill=BIG),
             reads=["dist"], writes=["dist"])
        P.dma("sp", lambda e: e.dma_start(out=halob, in_=halobig), "misc", writes=["halob"])
        P.dma("sp", lambda e: e.dma_start(out=sinkb, in_=sinks.partition_broadcast(128)), "misc", writes=["sinkb"])
        P.op("pool", lambda e: e.tensor_copy(out=dist0, in_=dist), reads=["dist"], writes=["dist0"])
        P.op("pool", lambda e: e.tensor_scalar(out=dist0[:, 0:128], in0=dist0[:, 0:128], scalar1=halob[:, 0:1],
                                               scalar2=None, op0=ALU.max),
             reads=["dist0", "halob"], writes=["dist0"])

    def setup_B():
        P.op("pool", lambda e: e.iota(R0, pattern=[[1, 2048]], base=0, channel_multiplier=0,
                                      allow_small_or_imprecise_dtypes=True), writes=["R0"] + A_TABLE_TOKS)
        P.op("pool", lambda e: e.memset(caus, 0.0), writes=["caus"] + A_TABLE_TOKS)
        P.op("pool", lambda e: e.affine_select(out=caus, in_=caus, pattern=[[-1, 128]], base=0,
                                               channel_multiplier=1, compare_op=ALU.is_ge, fill=NEG),
             reads=["caus"], writes=["caus"])
        P.dma("sp", lambda e: e.dma_start(out=pmask, in_=pastmask), "misc", writes=["pmask"] + A_TABLE_TOKS)
        P.op("pool", lambda e: e.memset(gbias, NEG), writes=["gbias"] + A_TABLE_TOKS)
        for jl in range(4):
            P.op("pool", lambda e, jl=jl: e.memset(gbias[:, jl, 0:4 + jl], 0.0), reads=["gbias"], writes=["gbias"])
            P.op("pool", lambda e, jl=jl: e.tensor_scalar(out=gbias[:, jl, 0:4], in0=gbias[:, jl, 0:4],
                                                          scalar1=pmask[:, 0:1], scalar2=None, op0=ALU.add),
                 reads=["pmask", "gbias"], writes=["gbias"])

    ring_state = {"n": 0}

    def load_unit_cols(W, c0):
        i = ring_state["n"] % 2
        ring_state["n"] += 1
        src = W.rearrange("(c p) f -> p c f", p=128)[:, :, c0:c0 + 512]
        P.dma("pool", lambda e: e.dma_start(out=ring[i], in_=src), "ring%d" % i, writes=[("ring", i)])
        return i

    def load_unit_rows(W, r0):
        i = ring_state["n"] % 2
        ring_state["n"] += 1
        src = W[r0:r0 + 512, :].rearrange("(c p) d -> p c d", p=128)
        P.dma("pool", lambda e: e.dma_start(out=ringd[i], in_=src), "ring%d" % i, writes=[("ring", i)])
        return i

    bank_rr = {"n": 0}

    def mm_group(out_ap, pairs, reads, writes):
        n = len(pairs)
        fns = []
        for i, (l, r) in enumerate(pairs):
            fns.append(lambda e, l=l, r=r, i=i: e.matmul(out=out_ap, lhsT=l, rhs=r, start=(i == 0), stop=(i == n - 1)))
        P.op("pe", fns, reads=reads, writes=writes)

    def transposes(srcs, reads, writes):
        fns = [lambda e, d=d, s=s: e.transpose(out=d, in_=s, identity=ident) for d, s in srcs]
        P.op("pe", fns, reads=list(reads) + ["ident"], writes=writes)

    pb6 = ps[:, 6:8, :].bitcast(BF16).rearrange("p a b -> p (a b)")

    def to_hT(src_ap, src_tok, dst, dst_cols, dst_tok):
        srcs = [(pb6[:, cc * 128:(cc + 1) * 128], src_ap[:, cc * 128:(cc + 1) * 128]) for cc in range(16)]
        transposes(srcs, reads=[src_tok], writes=[("ps", 6), ("ps", 7)])
        P.op("act", lambda e: e.activation(out=dst[:, :, dst_cols], in_=pb6.rearrange("p (a b) -> p a b", b=128),
                                           func=AF.Copy),
             reads=[("ps", 6), ("ps", 7)], writes=[dst_tok])

    hT_all = [("hT", t) for t in range(NT)]

    def evac_act(out_ap, in_ap, scale, reads, writes):
        P.op("act", lambda e: e.activation(out=out_ap, in_=in_ap, func=AF.Copy, scale=scale), reads=reads, writes=writes)

    def next_bank(lo=0, hi=6):
        b = lo + bank_rr["n"] % (hi - lo)
        bank_rr["n"] += 1
        return b

    def proj_feat(slot, j, dst, dst_slot, col_off, scale, tokname):
        for th in range(2):
            b = next_bank()
            pairs = [(ring[slot][:, cc, j * 128:(j + 1) * 128], hT[:, cc, th * 512:(th + 1) * 512]) for cc in range(16)]
            mm_group(ps[:, b, :], pairs, reads=[("ring", slot)] + hT_all[th * 4:th * 4 + 4], writes=[("ps", b)])
            evac_act(dst[:, dst_slot, col_off + th * 512: col_off + (th + 1) * 512], ps[:, b, :], scale,
                     reads=[("ps", b)], writes=[(tokname, dst_slot, 4 * th + k) for k in range(4)])

    cnt = {"ot": 0, "pt": 0, "s": 0, "l": 0}
    pend = {"y": None, "x2": None}
    KV_ALIAS = ["gtab", "btab", ("hb", 0), ("hb", 1), "kvf"]

    def softmax_pv(isA, h_slot, t, nk, s_psum, sink_col, v_aps, s_banks, li):
        L = Lb[li]
        ltok = ("L", li)

        def x2_part():
            src = L[:, 0:nk] if s_psum is None else s_psum
            src_reads = [ltok] if s_psum is None else [("ps", b) for b in s_banks]
            P.op("dve", lambda e: e.reduce_max(out=smc(SM_MX), in_=src, axis=AX.X), reads=src_reads, writes=["mx"])
            if sink_col is not None:
                P.op("dve", lambda e: e.tensor_scalar(out=smc(SM_NEGM), in0=smc(SM_MX),
                                                      scalar1=sinkb[:, sink_col:sink_col + 1],
                                                      scalar2=-1.0, op0=ALU.max, op1=ALU.mult),
                     reads=["mx", "sinkb"], writes=["negm"])
            else:
                P.op("dve", lambda e: e.tensor_scalar(out=smc(SM_NEGM), in0=smc(SM_MX), scalar1=-1.0, scalar2=None,
                                                      op0=ALU.mult), reads=["mx"], writes=["negm"])
            P.op("act", lambda e: e.activation(out=L[:, 0:nk], in_=src, func=AF.Exp, bias=smc(SM_NEGM), scale=1.0,
                                               accum_out=smc(SM_RS)),
                 reads=src_reads + ["negm"], writes=[ltok, "rs"])
            if sink_col is not None:
                P.op("act", lambda e: e.activation(out=smc(SM_ES), in_=sinkb[:, sink_col:sink_col + 1], func=AF.Exp,
                                                   bias=smc(SM_NEGM), scale=1.0), reads=["negm", "sinkb"], writes=["es"])
                P.op("dve", lambda e: e.tensor_tensor(out=smc(SM_DEN), in0=smc(SM_RS), in1=smc(SM_ES), op=ALU.add),
                     reads=["rs", "es"], writes=["den"])
                P.op("dve", lambda e: e.reciprocal(out=smc(SM_RDEN), in_=smc(SM_DEN)), reads=["den"], writes=["rden"])
            else:
                P.op("dve", lambda e: e.reciprocal(out=smc(SM_RDEN), in_=smc(SM_RS)), reads=["rs"], writes=["rden"])
            pi = cnt["pt"] % 2
            cnt["pt"] += 1
            Pn = Pnb[pi]
            P.op("act", lambda e: e.activation(out=Pn[:, 0:nk], in_=L[:, 0:nk], func=AF.Identity,
                                               scale=smc(SM_RDEN), bias=0.0),
                 reads=[ltok, "rden"], writes=[("Pn", pi)])

            def y_part():
                softmax_y(isA, h_slot, t, nk, v_aps, pi)
            return y_part

        xprev = pend["x2"]
        pend["x2"] = x2_part
        if xprev is not None:
            if pend["y"] is not None:
                pend["y"]()
            pend["y"] = xprev()

    def flush_y():
        if pend["y"] is not None:
            pend["y"]()
            pend["y"] = None
        if pend["x2"] is not None:
            ylast = pend["x2"]()
            pend["x2"] = None
            ylast()

    def softmax_y(isA, h_slot, t, nk, v_aps, pi):
        Pn = Pnb[pi]
        nb = nk // 128
        if nb <= 8:
            pbank = [6 + pi]
            pdst = psb(6 + pi)
        else:
            pbank = [6, 7]
            pdst = pb6
        srcs = [(pdst[:, k * 128:(k + 1) * 128], Pn[:, k * 128:(k + 1) * 128]) for k in range(nb)]
        transposes(srcs, reads=[("Pn", pi)], writes=[("ps", b) for b in pbank])
        P.op("act", lambda e: e.activation(out=PT[pi][:, 0:nb, :],
                                           in_=pdst[:, 0:nb * 128].rearrange("p (a b) -> p a b", b=128), func=AF.Copy),
             reads=[("ps", b) for b in pbank], writes=[("PT", pi)])
        ob = (4 + cnt["ot"] % 2) if isA else 4
        cnt["ot"] += 1
        pairs = [(v_aps[k][0], PT[pi][:, k, :]) for k in range(nb)]
        vreads = sorted(set(v_aps[k][1] for k in range(nb)), key=str)
        mm_group(ps[:, ob, 0:128], pairs, reads=[("PT", pi)] + vreads, writes=[("ps", ob)])
        P.op("act", lambda e: e.activation(out=qT[:, h_slot, t * 128:(t + 1) * 128], in_=ps[:, ob, 0:128], func=AF.Copy),
             reads=[("ps", ob)], writes=[("qT", h_slot, t)])

    def load_gb(layer, i):
        P.dma("sp", lambda e: e.dma_start(out=gtab, in_=lng_all[layer, i:i + 1, :].partition_broadcast(128)), "misc",
              writes=["gtab"])
        P.dma("sp", lambda e: e.dma_start(out=btab, in_=lnb_all[layer, i:i + 1, :].partition_broadcast(128)), "misc",
              writes=["btab"])

    hout_v = hout.rearrange("(n p) d -> p n d", p=128)

    def layer_norm(t, store, make_hT, halo_send):
        for q4 in range(4):
            P.op("dve", lambda e, q4=q4: e.bn_stats(out=stt[:, q4, :], in_=h[:, t, q4 * 512:(q4 + 1) * 512]),
                 reads=[("h", t)], writes=[("stt", q4)])
        P.op("dve", lambda e: e.bn_aggr(out=mv2, in_=stt.rearrange("p a b -> p (a b)")),
             reads=[("stt", q4) for q4 in range(4)], writes=["mv2"])
        P.op("act", lambda e: e.activation(out=smc(SM_SD), in_=mv2[:, 1:2], func=AF.Sqrt, bias=epsb[:, 0:1], scale=1.0),
             reads=["mv2", "epsb"], writes=["sd"])
        P.op("dve", lambda e: e.reciprocal(out=smc(SM_RSTD), in_=smc(SM_SD)), reads=["sd"], writes=["rstd"])
        P.op("dve", lambda e: e.scalar_tensor_tensor(out=smc(SM_NMR), in0=mv2[:, 0:1], scalar=-1.0, in1=smc(SM_RSTD),
                                                     op0=ALU.mult, op1=ALU.mult), reads=["mv2", "rstd"], writes=["nmr"])
        P.op("act", lambda e: e.activation(out=h[:, t, :], in_=h[:, t, :], func=AF.Identity, scale=smc(SM_RSTD),
                                           bias=smc(SM_NMR)), reads=[("h", t), "rstd", "nmr"], writes=[("h", t)])
        P.op("pool", lambda e: e.tensor_tensor(out=h[:, t, :], in0=h[:, t, :], in1=gtab, op=ALU.mult),
             reads=[("h", t), "gtab"], writes=[("h", t)])
        P.op("dve", lambda e: e.tensor_tensor(out=h[:, t, :], in0=h[:, t, :], in1=btab, op=ALU.add),
             reads=[("h", t), "btab"], writes=[("h", t)])
        if store:
            P.dma("sp", lambda e: e.dma_start(out=hout_v[:, t, :], in_=h[:, t, :]), "hstore%d" % (t % 4), reads=[("h", t)])
        if make_hT:
            i = t % 2
            P.op("act", lambda e: e.activation(out=hb[i], in_=h[:, t, :], func=AF.Copy),
                 reads=[("h", t)], writes=[("hb", i)])
            to_hT(hb[i], ("hb", i), hT, slice(t * 128, (t + 1) * 128), ("hT", t))
            if halo_send and t == NT - 1:
                P.dma("sp", lambda e: e.dma_start(out=halo_in.ap(), in_=h[:, t, :]), "halo", reads=[("h", t)],
                      writes=["halo_in"])
                P.special("pool", lambda e: e.collective_compute(
                    "AllGather", ALU.bypass, replica_groups=groups, ins=[halo_in.ap().opt()],
                    outs=[halo_out.ap().opt()]), "halo", reads=["halo_in"], writes=["halo_out"])

    xin_v = xin.rearrange("(n p) d -> p n d", p=128)
    for t in range(NT):
        P.dma("sp", lambda e, t=t: e.dma_start(out=h[:, t, :], in_=xin_v[:, t, :]), "hload%d" % (t % 4), writes=[("h", t)])
    for t in range(NT):
        i = t % 2
        P.op("act", lambda e, t=t, i=i: e.activation(out=hb[i], in_=h[:, t, :], func=AF.Copy),
             reads=[("h", t)], writes=[("hb", i)])
        to_hT(hb[i], ("hb", i), hT, slice(t * 128, (t + 1) * 128), ("hT", t))
    setup_A()

    for layer in range(nlayers):
        isA = layer < 2
        last = layer == nlayers - 1
        w_mkv = w_mem_kv[layer]
        w_o = w_o_all[layer]
        w_up = w_up_all[layer]
        w_dn = w_dn_all[layer]
        if layer > 0:
            P.barrier()
        if isA:
            w_qkv = w_in_a[layer]
            if layer == 0:
                P.dma("pool", lambda e: e.dma_start(out=hhb, in_=xhalo), "misc2", writes=["hhb"])
            else:
                P.dma("pool", lambda e: e.dma_start(out=hhb, in_=halo_out.ap()[0:128, :]), "misc2", reads=["halo_out"],
                      writes=["hhb"])
            to_hT(hhb, "hhb", hTh, slice(0, 128), "hTh")
            units = [("q", 0), ("q", 1), ("q", 2), ("k", 3), ("v", 4), ("qm", 5)]
        else:
            w_qkv = w_q_b[layer - 2]
            if layer == 2:
                setup_B()
            units = [("q", 0), ("q", 1), ("q", 2), ("qm", 3)]
            kparts = [(kout_ap[0:128, :], kT[:, :, 0:1024]), (kin_ap, kT[:, :, 1024:2048])]
            for si, di in kparts:
                P.dma("pool", lambda e, si=si, di=di: e.dma_start(out=di, in_=si.rearrange("p (a b) -> p a b", b=1024)),
                      "misc2", reads=["kout", "vout", "kmout", "kin", "vin", "kmin"], writes=["kT"] + KV_ALIAS)
            vparts = [(vout_ap[0:128, :], vv[:, 0:8, :]), (vin_ap, vv[:, 8:16, :])]
            for si, di in vparts:
                P.dma("pool", lambda e, si=si, di=di: e.dma_start(out=di, in_=si.rearrange("p (a b) -> p a b", b=512)),
                      "misc2", reads=["kout", "vout", "kmout", "kin", "vin", "kmin"], writes=["vv"] + KV_ALIAS)
            mparts = [(kmout_ap[0:128, 0:16], kmT[:, :, 0:4]), (kmin_ap[:, 0:16], kmT[:, :, 4:8])]
            for si, di in mparts:
                P.dma("pool", lambda e, si=si, di=di: e.dma_start(out=di, in_=si.rearrange("p (a b) -> p a b", b=4)),
                      "misc2", reads=["kout", "vout", "kmout", "kin", "vin", "kmin"], writes=["kmT"] + (A_TABLE_TOKS if layer == 2 else []))
        P.mark("L%d proj" % layer)
        for typ, u in units:
            slot = load_unit_cols(w_qkv, u * 512)
            if typ in ("q", "qm"):
                base = (u * 4) if typ == "q" else 12
                for j in range(4):
                    proj_feat(slot, j, qT, base + j, 0, SC, "qT")
            elif typ == "k":
                for j in range(4):
                    proj_feat(slot, j, kT, j, 128, 1.0, "kTt")
                    b = next_bank()
                    pairs = [(ring[slot][:, cc, j * 128:(j + 1) * 128], hTh[:, cc, :]) for cc in range(16)]
                    mm_group(ps[:, b, 0:128], pairs, reads=[("ring", slot), "hTh"], writes=[("ps", b)])
                    evac_act(kT[:, j, 0:128], ps[:, b, 0:128], 1.0, reads=[("ps", b)], writes=[("kTh", j)])
            elif typ == "v":
                for t in range(-1, NT):
                    b = next_bank()
                    if t < 0:
                        pairs = [(hTh[:, cc, :], ring[slot][:, cc, :]) for cc in range(16)]
                        rd = ["hTh"]
                    else:
                        pairs = [(hT[:, cc, t * 128:(t + 1) * 128], ring[slot][:, cc, :]) for cc in range(16)]
                        rd = [("hT", t)]
                    mm_group(ps[:, b, :], pairs, reads=[("ring", slot)] + rd, writes=[("ps", b)])
                    evac_act(vv[:, t + 1, :], ps[:, b, :], 1.0, reads=[("ps", b)], writes=[("vvt", t + 1)])
        P.mark("L%d memkv" % layer)
        P.barrier()
        P.dma("pool", lambda e: e.dma_start(out=memb, in_=mem.rearrange("(n p) d -> p n d", p=128)), "misc2",
              writes=["memb"] + hT_all)
        for mt in range(2):
            srcs = [(pb6[:, cc * 128:(cc + 1) * 128], memb[:, mt, cc * 128:(cc + 1) * 128]) for cc in range(16)]
            transposes(srcs, reads=["memb"], writes=[("ps", 6), ("ps", 7)])
            P.op("act", lambda e, mt=mt: e.activation(out=memT[:, :, mt * 128:(mt + 1) * 128],
                                                      in_=pb6.rearrange("p (a b) -> p a b", b=128), func=AF.Copy),
                 reads=[("ps", 6), ("ps", 7)], writes=["memT"])
        slot = load_unit_cols(w_mkv, 0)
        for j in range(4):
            b = next_bank()
            pairs = [(ring[slot][:, cc, j * 128:(j + 1) * 128], memT[:, cc, :]) for cc in range(16)]
            mm_group(ps[:, b, 0:256], pairs, reads=[("ring", slot), "memT"], writes=[("ps", b)])
            evac_act(mkT[:, j, :], ps[:, b, 0:256], 1.0, reads=[("ps", b)], writes=["mkT"])
        slot = load_unit_cols(w_mkv, 512)
        for mt in range(2):
            b = next_bank()
            pairs = [(memT[:, cc, mt * 128:(mt + 1) * 128], ring[slot][:, cc, :]) for cc in range(16)]
            mm_group(ps[:, b, :], pairs, reads=[("ring", slot), "memT"], writes=[("ps", b)])
            evac_act(mv[:, mt, :], ps[:, b, :], 1.0, reads=[("ps", b)], writes=["mv"])
        P.barrier()
        P.mark("L%d att" % layer)
        for hh in range(12):
            kvh = hh // 3
            for t in range(NT):
                qap = qT[:, hh, t * 128:(t + 1) * 128]
                if isA:
                    sb = cnt["s"] % 4
                    cnt["s"] += 1
                    mm_group(ps[:, sb, 0:256], [(qap, kT[:, kvh, t * 128:t * 128 + 256])],
                             reads=[("qT", hh, t), ("kTh", kvh)] + [("kTt", kvh, k) for k in range(NT)],
                             writes=[("ps", sb)])
                    dtab = dist0 if t == 0 else dist
                    li = cnt["l"] % 2
                    cnt["l"] += 1
                    L = Lb[li]
                    P.op("dve", lambda e, dtab=dtab, sb=sb, hh=hh, L=L: e.scalar_tensor_tensor(
                        out=L[:, 0:256], in0=dtab, scalar=-SLOPES[hh], in1=ps[:, sb, 0:256], op0=ALU.mult, op1=ALU.add),
                        reads=[("ps", sb), "dist", "dist0"], writes=[("L", li)])
                    v_aps = [(vv[:, t + k, kvh * 128:(kvh + 1) * 128], ("vvt", t + k)) for k in range(2)]
                    softmax_pv(True, hh, t, 256, None, layer * 12 + hh, v_aps, None, li)
                else:
                    jl = t // 2
                    nk = 1024 + (t + 1) * 128
                    nmm = (nk + 511) // 512
                    fns = []
                    for m in range(nmm):
                        w = min(512, nk - m * 512)
                        fns.append(lambda e, m=m, w=w, qap=qap, kvh=kvh: e.matmul(
                            out=ps[:, m, 0:w], lhsT=qap, rhs=kT[:, kvh, m * 512:m * 512 + w], start=True, stop=True))
                    fns.append(lambda e, qap=qap, kvh=kvh: e.matmul(out=ps[:, 5, 0:8], lhsT=qap, rhs=kmT[:, kvh, :],
                                                                    start=True, stop=True))
                    P.op("pe", fns, reads=[("qT", hh, t), "kT", "kmT"],
                         writes=[("ps", m) for m in range(nmm)] + [("ps", 5)])
                    P.op("dve", lambda e, jl=jl: e.tensor_tensor(out=gm, in0=ps[:, 5, 0:8], in1=gbias[:, jl, :], op=ALU.add),
                         reads=[("ps", 5), "gbias"], writes=["gm"])
                    P.op("dve", lambda e: e.max(out=top8, in_=gm), reads=["gm"], writes=["top8"])
                    P.op("dve", lambda e: e.tensor_scalar(out=bm, in0=gm, scalar1=top8[:, 2:3], scalar2=NEG,
                                                          op0=ALU.is_lt, op1=ALU.mult), reads=["gm", "top8"], writes=["bm"])
                    P.op("dve", lambda e, jl=jl: e.tensor_tensor(out=bm, in0=bm, in1=gbias[:, jl, :], op=ALU.add),
                         reads=["bm", "gbias"], writes=["bm"])
                    sall = ps[:, 0:4, :].rearrange("p a b -> p (a b)")
                    li = cnt["l"] % 2
                    cnt["l"] += 1
                    L = Lb[li]
                    P.op("dve", lambda e, nk=nk, hh=hh, sall=sall, L=L: e.scalar_tensor_tensor(
                        out=L[:, 0:nk], in0=R0[:, 0:nk], scalar=SLOPES[hh], in1=sall[:, 0:nk], op0=ALU.mult, op1=ALU.add),
                        reads=[("ps", m) for m in range(nmm)] + ["R0"], writes=[("L", li)])
                    nblk = 4 + jl
                    L3 = L[:, 0:nblk * 256].rearrange("p (a b) -> p a b", b=256)
                    bmb = bm[:, 0:nblk].unsqueeze(2).to_broadcast([128, nblk, 256])
                    P.op("pool", lambda e, L3=L3, bmb=bmb: e.tensor_tensor(out=L3, in0=L3, in1=bmb, op=ALU.add),
                         reads=[("L", li), "bm"], writes=[("L", li)])
                    P.op("pool", lambda e, nk=nk, L=L: e.tensor_tensor(out=L[:, nk - 128:nk], in0=L[:, nk - 128:nk],
                                                                       in1=caus, op=ALU.add),
                         reads=[("L", li), "caus"], writes=[("L", li)])
                    v_aps = [(vv[:, k, kvh * 128:(kvh + 1) * 128], "vv") for k in range(nk // 128)]
                    softmax_pv(False, hh, t, nk, None, None, v_aps, None, li)
        for j in range(4):
            for t in range(NT):
                sb = cnt["s"] % 4
                cnt["s"] += 1
                mm_group(ps[:, sb, 0:256], [(qT[:, 12 + j, t * 128:(t + 1) * 128], mkT[:, j, :])],
                         reads=[("qT", 12 + j, t), "mkT"], writes=[("ps", sb)])
                v_aps = [(mv[:, k, j * 128:(j + 1) * 128], "mv") for k in range(2)]
                li = cnt["l"] % 2
                cnt["l"] += 1
                softmax_pv(isA, 12 + j, t, 256, ps[:, sb, 0:256], None, v_aps, [sb], li)
        flush_y()
        P.mark("L%d wo" % layer)
        P.barrier()
        load_gb(layer, 0)
        for dq in range(4):
            slot = load_unit_cols(w_o, dq * 512)
            for t in range(NT):
                b = next_bank(0, 6)
                pairs = [(qT[:, cc, t * 128:(t + 1) * 128], ring[slot][:, cc, :]) for cc in range(16)]
                mm_group(ps[:, b, :], pairs, reads=[("ring", slot)] + [("qT", s, t) for s in range(16)],
                         writes=[("ps", b)])
                P.op("dve", lambda e, t=t, dq=dq, b=b: e.scalar_tensor_tensor(
                    out=h[:, t, dq * 512:(dq + 1) * 512], in0=h[:, t, dq * 512:(dq + 1) * 512], scalar=ALPHA,
                    in1=ps[:, b, :], op0=ALU.mult, op1=ALU.add), reads=[("ps", b), ("h", t)], writes=[("h", t)])
        NG = 16
        plan = [("up", 0)]
        for g in range(1, NG):
            plan += [("up", g), ("dn", g - 1)]
        plan += [("dn", NG - 1)]
        plan_slot = {}
        plan_pos = {"n": 0}

        def plan_issue():
            k = plan_pos["n"]
            if k >= len(plan):
                return
            plan_pos["n"] += 1
            typ, g = plan[k]
            plan_slot[(typ, g)] = load_unit_cols(w_up, g * 512) if typ == "up" else load_unit_rows(w_dn, g * 512)

        plan_issue()
        plan_issue()
        P.mark("L%d ln1" % layer)
        P.barrier()
        for t in range(NT):
            layer_norm(t, False, True, False)
        P.mark("L%d ffn" % layer)
        P.barrier()
        load_gb(layer, 1)
        upb = {"n": 0}
        dnb = {"n": 0}
        rt = {"n": 0}

        def ffn_up(g):
            slot = plan_slot[("up", g)]
            ai = g % 2
            for j in range(4):
                for th in range(2):
                    b = upb["n"] % 4
                    upb["n"] += 1
                    pairs = [(ring[slot][:, cc, j * 128:(j + 1) * 128], hT[:, cc, th * 512:(th + 1) * 512])
                             for cc in range(16)]
                    mm_group(ps[:, b, :], pairs, reads=[("ring", slot)] + hT_all[th * 4:th * 4 + 4], writes=[("ps", b)])
                    ri = rt["n"] % 2
                    rt["n"] += 1
                    P.op("act", lambda e, b=b, ri=ri: e.activation(out=rtmp[ri], in_=ps[:, b, :], func=AF.Relu),
                         reads=[("ps", b)], writes=[("rtmp", ri)])
                    P.op("pool", lambda e, ri=ri, ai=ai, j=j, th=th: e.tensor_tensor(
                        out=actT[ai][:, j, th * 512:(th + 1) * 512], in0=rtmp[ri], in1=rtmp[ri], op=ALU.mult),
                        reads=[("rtmp", ri)], writes=[("actT", ai, j, th)])

        def ffn_down(g):
            slot = plan_slot[("dn", g)]
            ai = g % 2
            for t in range(NT):
                for dq in range(4):
                    b = 4 + dnb["n"] % 4
                    dnb["n"] += 1
                    pairs = [(actT[ai][:, j, t * 128:(t + 1) * 128], ringd[slot][:, j, dq * 512:(dq + 1) * 512])
                             for j in range(4)]
                    mm_group(ps[:, b, :], pairs, reads=[("ring", slot)] + [("actT", ai, j, t // 4) for j in range(4)],
                             writes=[("ps", b)])
                    hs = h[:, t, dq * 512:(dq + 1) * 512]
                    if g == 0:
                        P.op("dve", lambda e, hs=hs, b=b: e.scalar_tensor_tensor(out=hs, in0=hs, scalar=ALPHA,
                                                                                 in1=ps[:, b, :], op0=ALU.mult, op1=ALU.add),
                             reads=[("ps", b), ("h", t)], writes=[("h", t)])
                    else:
                        P.op("dve", lambda e, hs=hs, b=b: e.tensor_tensor(out=hs, in0=hs, in1=ps[:, b, :], op=ALU.add),
                             reads=[("ps", b), ("h", t)], writes=[("h", t)])

        for g in range(NG):
            ffn_up(g)
            plan_issue()
            if g > 0:
                ffn_down(g - 1)
                plan_issue()
        ffn_down(NG - 1)
        P.mark("L%d ln2" % layer)
        P.barrier()
        for t in range(NT):
            layer_norm(t, last, not last, layer == 0 and nlayers > 1)
        if layer == 1 and nlayers > 2:
            P.barrier()
            for u in range(2):
                slot = load_unit_cols(w_kvs, u * 512)
                if u == 0:
                    for j in range(4):
                        for th in range(2):
                            b = next_bank()
                            pairs = [(ring[slot][:, cc, j * 128:(j + 1) * 128], hT[:, cc, th * 512:(th + 1) * 512])
                                     for cc in range(16)]
                            mm_group(ps[:, b, :], pairs, reads=[("ring", slot)] + hT_all[th * 4:th * 4 + 4],
                                     writes=[("ps", b)])
                            P.op("act", lambda e, b=b: e.activation(out=kvf, in_=ps[:, b, :], func=AF.Copy),
                                 reads=[("ps", b)], writes=["kvf"])
                            co = j * 1024 + th * 512
                            P.dma("sp", lambda e, co=co: e.dma_start(out=kin_ap[:, co:co + 512], in_=kvf),
                                  "kvst", reads=["kvf"], writes=["kin"])
                            P.op("dve", lambda e, j=j, th=th: e.tensor_reduce(
                                out=kmacc[:, j, 2 * th:2 * th + 2], in_=kvf.rearrange("p (a b) -> p a b", b=256),
                                axis=AX.X, op=ALU.add), reads=["kvf"], writes=["kmacc"])
                    P.op("dve", lambda e: e.tensor_scalar(out=kmacc, in0=kmacc, scalar1=1.0 / 256.0, scalar2=None,
                                                          op0=ALU.mult), reads=["kmacc"], writes=["kmacc"])
                else:
                    for t in range(NT):
                        b = next_bank()
                        pairs = [(hT[:, cc, t * 128:(t + 1) * 128], ring[slot][:, cc, :]) for cc in range(16)]
                        mm_group(ps[:, b, :], pairs, reads=[("ring", slot), ("hT", t)], writes=[("ps", b)])
                        P.op("act", lambda e, b=b: e.activation(out=kvf, in_=ps[:, b, :], func=AF.Copy),
                             reads=[("ps", b)], writes=["kvf"])
                        co = t * 512
                        P.dma("sp", lambda e, co=co: e.dma_start(out=vin_ap[:, co:co + 512], in_=kvf), "kvst",
                              reads=["kvf"], writes=["vin"])
            P.op("dve", lambda e: e.memset(kvf[:, 0:256], 0.0), writes=["kvf"])
            P.op("dve", lambda e: e.tensor_copy(out=kvf[:, 0:16].rearrange("p (a b) -> p a b", b=4), in_=kmacc),
                 reads=["kmacc", "kvf"], writes=["kvf"])
            P.dma("sp", lambda e: e.dma_start(out=kmin_ap, in_=kvf[:, 0:256]), "kvst2", reads=["kvf"], writes=["kmin"])
            for nm, ti, to in (("kx", kin, kout), ("vx", vin, vout), ("kmx", kmin, kmout)):
                P.special("pool", lambda e, ti=ti, to=to: e.collective_compute(
                    "AllGather", ALU.bypass, replica_groups=groups, ins=[ti.ap().opt()], outs=[to.ap().opt()]),
                    nm, reads=[ti.name], writes=[to.name])
    P.mark("end")
    P.emit()
    nc._marks = P.marks
    return nc


_CACHE = {}
_DBG = None


def _prep(x, mem, w_in_a, sinks_a, w_q_b, w_kv_shared, w_mem_kv, w_o, w_up, w_down, ln_g, ln_b, ncores=8):
    f = lambda a: np.ascontiguousarray(np.asarray(a, dtype=np.float32))
    x = f(x); mem = f(mem)
    shared = {"w_in_a": f(w_in_a), "sinks": f(sinks_a).reshape(1, 24), "w_q_b": f(w_q_b), "w_kvs": f(w_kv_shared),
              "w_mem_kv": f(w_mem_kv), "w_o": f(w_o), "w_up": f(w_up), "w_dn": f(w_down), "lng": f(ln_g), "lnb": f(ln_b)}
    zeros_halo = np.zeros((128, 2048), np.float32)
    in_maps = []
    for c in range(ncores):
        b, hf = c // 2, c % 2
        m = dict(shared)
        m["x_own"] = np.ascontiguousarray(x[b, hf * 1024:(hf + 1) * 1024])
        m["x_halo"] = np.ascontiguousarray(x[b, 896:1024]) if hf == 1 else zeros_halo
        m["halobig"] = np.full((128, 1), 0.0 if hf == 1 else BIG, np.float32)
        m["pastmask"] = np.full((128, 1), 0.0 if hf == 1 else NEG, np.float32)
        m["mem"] = np.ascontiguousarray(mem[b])
        in_maps.append(m)
    return in_maps


def kernel(x, mem, w_in_a, sinks_a, w_q_b, w_kv_shared, w_mem_kv, w_o, w_up, w_down, ln_g, ln_b):
    ncores = 8
    if "nc" not in _CACHE:
        _CACHE["nc"] = build_fused(4, ncores)
    nc = _CACHE["nc"]
    in_maps = _prep(x, mem, w_in_a, sinks_a, w_q_b, w_kv_shared, w_mem_kv, w_o, w_up, w_down, ln_g, ln_b, ncores)
    res = run_bass_kernel_spmd(nc, in_maps, core_ids=list(range(ncores)))
    out = np.empty((4, 2048, 2048), np.float32)
    for c in range(ncores):
        b, hf = c // 2, c % 2
        out[b, hf * 1024:(hf + 1) * 1024] = res.results[c]["hout"]
    return out
```

```python
import numpy as np
import concourse.bass as bass
import concourse.mybir as mybir
from concourse.bass_utils import run_bass_kernel_spmd

F32 = mybir.dt.float32
BF16 = mybir.dt.bfloat16
U8 = mybir.dt.uint8
AF = mybir.ActivationFunctionType
ALU = mybir.AluOpType
AX = mybir.AxisListType

NEG = -30000.0
BIG = 1.0e6
SC = 128.0 ** -0.5
ALPHA = 8.0 ** 0.25
EPS = 1e-5
SLOPES = [2.0 ** (-8.0 * (i + 1) / 12.0) for i in range(12)]
NT = 8
TOK = 1024


class Prog:
    def __init__(self, nc):
        self.nc = nc
        self.ops = {e: [] for e in ("pe", "act", "dve", "pool", "sp")}
        self.cnt = {}
        self.seen = {e: {} for e in self.ops}
        self.lastw = {}
        self.readers = {}
        self.ninst = {e: 0 for e in self.ops}
        self.marks = []

    def _deps(self, eng, reads, writes):
        need = {}

        def add(d):
            if d is None:
                return
            s, v = d
            if v > need.get(s, 0):
                need[s] = v
        for t in reads:
            add(self.lastw.get(t))
        for t in writes:
            add(self.lastw.get(t))
            for s, v in self.readers.get(t, {}).items():
                add((s, v))
        waits = []
        for s, v in need.items():
            if self.seen[eng].get(s, 0) < v:
                self.seen[eng][s] = v
                waits.append((s, v))
        return waits

    def _commit(self, stamp, reads, writes):
        s, v = stamp
        for t in reads:
            self.readers.setdefault(t, {})[s] = v
        for t in writes:
            self.lastw[t] = stamp
            self.readers[t] = {}

    def op(self, eng, fns, reads=(), writes=()):
        waits = self._deps(eng, reads, writes)
        self.cnt[eng] = self.cnt.get(eng, 0) + 1
        self._commit((eng, self.cnt[eng]), reads, writes)
        if not isinstance(fns, (list, tuple)):
            fns = [fns]
        self.ninst[eng] += len(fns)
        self.ops[eng].append((waits, list(fns), eng, 1))

    def dma(self, queue, fn, sem, reads=(), writes=()):
        waits = self._deps(queue, reads, writes)
        key = "d_" + sem
        prev = self.cnt.get(key, 0)
        if prev > 0 and self.seen[queue].get(key, 0) < prev:
            self.seen[queue][key] = prev
            waits.append((key, prev))
        self.cnt[key] = prev + 16
        self._commit((key, self.cnt[key]), reads, writes)
        self.ops[queue].append((waits, [fn], key, 16))

    def special(self, queue, fn, sem, reads=(), writes=()):
        waits = self._deps(queue, reads, writes)
        key = "c_" + sem
        assert key not in self.cnt
        self.cnt[key] = 1
        self._commit((key, 1), reads, writes)
        self.ops[queue].append((waits, [fn], key, None))

    def mark(self, name):
        self.marks.append((name, dict(self.ninst)))

    def barrier(self, engs=("pe", "act", "dve", "sp")):
        for e in engs:
            waits = []
            for s, v in self.cnt.items():
                if self.seen[e].get(s, 0) < v:
                    self.seen[e][s] = v
                    waits.append((s, v))
            if waits:
                self.ops[e].append((waits, [], None, 0))

    def emit(self, final_waits_engine="sp"):
        nc = self.nc
        fw = [(s, v) for s, v in self.cnt.items()]
        self.ops[final_waits_engine].append((fw, [], None, 0))
        import contextlib
        with contextlib.ExitStack() as st:
            sems = {}
            for s in self.cnt:
                sems[s] = st.enter_context(nc.semaphore("s_" + s))
            block = st.enter_context(nc.Block())
            engmap = {"pe": block.tensor, "act": block.scalar, "dve": block.vector,
                      "pool": block.gpsimd, "sp": block.sync}
            for ename, deco in engmap.items():
                ops = self.ops[ename]

                def body(e, ops=ops):
                    for waits, fns, incsem, incv in ops:
                        for s, v in waits:
                            e.wait_ge(sems[s], v)
                        ins = None
                        for f in fns:
                            ins = f(e)
                        if ins is not None and incsem is not None:
                            if incv is None:
                                ins.then_inc(sems[incsem])
                            else:
                                ins.then_inc(sems[incsem], incv)
                deco(body)


def build_fused(nlayers=4, ncores=8):
    nc = bass.Bass("TRN2", target_bir_lowering=False)
    P = Prog(nc)
    groups = [[2 * i, 2 * i + 1] for i in range(ncores // 2)]

    def din(name, shape):
        return nc.dram_tensor(name, shape, F32, kind="ExternalInput").ap()

    xin = din("x_own", [TOK, 2048])
    xhalo = din("x_halo", [128, 2048])
    halobig = din("halobig", [128, 1])
    pastmask = din("pastmask", [128, 1])
    mem = din("mem", [256, 2048])
    w_in_a = din("w_in_a", [2, 2048, 3072])
    sinks = din("sinks", [1, 24])
    w_q_b = din("w_q_b", [2, 2048, 2048])
    w_kvs = din("w_kvs", [2048, 1024])
    w_mem_kv = din("w_mem_kv", [4, 2048, 1024])
    w_o_all = din("w_o", [4, 2048, 2048])
    w_up_all = din("w_up", [4, 2048, 8192])
    w_dn_all = din("w_dn", [4, 8192, 2048])
    lng_all = din("lng", [4, 2, 2048])
    lnb_all = din("lnb", [4, 2, 2048])
    hout = nc.dram_tensor("hout", [TOK, 2048], F32, kind="ExternalOutput").ap()
    halo_in = nc.dram_tensor("halo_in", [128, 2048], F32)
    halo_out = nc.dram_tensor("halo_out", [256, 2048], F32)
    kin = nc.dram_tensor("kin", [128, 4096], F32)
    kout = nc.dram_tensor("kout", [256, 4096], F32)
    vin = nc.dram_tensor("vin", [128, 4096], F32)
    vout = nc.dram_tensor("vout", [256, 4096], F32)
    kmin = nc.dram_tensor("kmin", [128, 256], F32)
    kmout = nc.dram_tensor("kmout", [256, 256], F32)
    kin_ap, kout_ap, vin_ap, vout_ap, kmin_ap, kmout_ap = (kin.ap(), kout.ap(), vin.ap(), vout.ap(), kmin.ap(),
                                                            kmout.ap())

    ARENA = 212480
    arena = nc.alloc_sbuf_tensor("arena", [128, ARENA], U8)
    ps = nc.alloc_psum_tensor("ps", [128, 8, 512], F32)

    def view(off, shape, dt):
        n = 1
        for s in shape:
            n *= s
        nb = n * (4 if dt == F32 else 2)
        a = arena[:, off:off + nb].bitcast(dt)
        if len(shape) == 2:
            return a.rearrange("p (a b) -> p a b", b=shape[1])
        return a

    OH, ORING, OA, OB, OKV, OC = 0, 65536, 98304, 131072, 163840, 196608
    h = view(OH, [8, 2048], F32)
    ring = [view(ORING + i * 16384, [16, 512], BF16) for i in range(2)]
    ringd = [view(ORING + i * 16384, [4, 2048], BF16) for i in range(2)]
    hT = view(OA, [16, 1024], BF16)
    Lb = [view(OA, [2048], F32), view(OA + 24576, [2048], F32)]
    memb = view(OA, [2, 2048], BF16)
    memT = view(OA + 8192, [16, 256], BF16)
    Pnb = [view(OA + 8192, [2048], BF16), view(OA + 20480, [2048], BF16)]
    PT = [view(OA + 12288 + i * 4096, [16, 128], BF16) for i in range(2)]
    qT = view(OB, [16, 1024], BF16)
    actT = [view(OB + i * 8192, [4, 1024], BF16) for i in range(2)]
    rtmp = [view(OB + 16384 + i * 2048, [512], F32) for i in range(2)]
    kmacc = view(OB, [4, 4], F32)
    kT = view(OKV, [4, 2048], BF16)
    vv = view(OKV + 16384, [16, 512], BF16)
    gtab = view(OKV, [2048], F32)
    btab = view(OKV + 8192, [2048], F32)
    hb = [view(OKV + 16384 + i * 4096, [2048], BF16) for i in range(2)]
    kvf = view(OKV + 24576, [512], F32)
    c = OC
    mkT = view(c, [4, 256], BF16); c += 2048
    mv = view(c, [2, 512], BF16); c += 2048
    ident = view(c, [128], BF16); c += 256
    identf = view(c, [128], F32); c += 512
    ca = c
    hTh = view(ca, [16, 128], BF16); ca += 4096
    hhb = view(ca, [2048], BF16); ca += 4096
    dist = view(ca, [256], F32); ca += 1024
    dist0 = view(ca, [256], F32); ca += 1024
    sinkb = view(ca, [24], F32); ca += 96
    halob = view(ca, [1], F32); ca += 32
    cb = c
    R0 = view(cb, [2048], F32); cb += 8192
    caus = view(cb, [128], F32); cb += 512
    kmT = view(cb, [4, 8], BF16); cb += 64
    gbias = view(cb, [4, 8], F32); cb += 128
    pmask = view(cb, [1], F32); cb += 32
    gm = view(cb, [8], F32); cb += 32
    top8 = view(cb, [8], F32); cb += 32
    bm = view(cb, [8], F32); cb += 32
    c = max(ca, cb)
    stt2 = [view(c, [4, 6], F32), view(c + 96, [4, 6], F32)]; c += 192
    mv2b = [view(c, [2], F32), view(c + 32, [2], F32)]; c += 64
    sm = view(c, [16], F32); c += 64
    epsb = view(c, [1], F32); c += 32
    assert c <= ARENA, c
    SM_MX, SM_NEGM, SM_RS, SM_ES, SM_DEN, SM_RDEN, SM_SD, SM_RSTD, SM_NMR = range(9)
    A_TABLE_TOKS = ["hTh", "hhb", "dist", "dist0", "sinkb", "halob"]

    def smc(i):
        return sm[:, i:i + 1]

    def psb(b):
        return ps[:, b, :].bitcast(BF16)

    P.op("pool", lambda e: e.memset(identf, 0.0), writes=["identf"])
    P.op("pool", lambda e: e.affine_select(out=identf, in_=identf, pattern=[[-1, 128]], base=0,
                                           channel_multiplier=1, compare_op=ALU.not_equal, fill=1.0),
         reads=["identf"], writes=["identf"])
    P.op("pool", lambda e: e.tensor_copy(out=ident, in_=identf), reads=["identf"], writes=["ident"])
    P.op("pool", lambda e: e.memset(epsb, EPS), writes=["epsb"])

    def setup_A():
        P.op("pool", lambda e: e.iota(dist, pattern=[[-1, 256]], base=128, channel_multiplier=1,
                                      allow_small_or_imprecise_dtypes=True), writes=["dist"])
        P.op("pool", lambda e: e.affine_select(out=dist, in_=dist, pattern=[[-1, 256]], base=128,
                                               channel_multiplier=1, compare_op=ALU.is_ge, fill=BIG),
             reads=["dist"], writes=["dist"])
        P.op("pool", lambda e: e.affine_select(out=dist, in_=dist, pattern=[[1, 256]], base=-1,
                                               channel_multiplier=-1, compare_op=ALU.is_ge, fill=BIG),
             reads=["dist"], writes=["dist"])
        P.dma("sp", lambda e: e.dma_start(out=halob, in_=halobig), "misc", writes=["halob"])
        P.dma("sp", lambda e: e.dma_start(out=sinkb, in_=sinks.partition_broadcast(128)), "misc", writes=["sinkb"])
        P.op("pool", lambda e: e.tensor_copy(out=dist0, in_=dist), reads=["dist"], writes=["dist0"])
        P.op("pool", lambda e: e.tensor_scalar(out=dist0[:, 0:128], in0=dist0[:, 0:128], scalar1=halob[:, 0:1],
                                               scalar2=None, op0=ALU.max),
             reads=["dist0", "halob"], writes=["dist0"])

    def setup_B():
        P.op("pool", lambda e: e.iota(R0, pattern=[[1, 2048]], base=0, channel_multiplier=0,
                                      allow_small_or_imprecise_dtypes=True), writes=["R0"] + A_TABLE_TOKS)
        P.op("pool", lambda e: e.memset(caus, 0.0), writes=["caus"] + A_TABLE_TOKS)
        P.op("pool", lambda e: e.affine_select(out=caus, in_=caus, pattern=[[-1, 128]], base=0,
                                               channel_multiplier=1, compare_op=ALU.is_ge, fill=NEG),
             reads=["caus"], writes=["caus"])
        P.dma("sp", lambda e: e.dma_start(out=pmask, in_=pastmask), "misc", writes=["pmask"] + A_TABLE_TOKS)
        P.op("pool", lambda e: e.memset(gbias, NEG), writes=["gbias"] + A_TABLE_TOKS)
        for jl in range(4):
            P.op("pool", lambda e, jl=jl: e.memset(gbias[:, jl, 0:4 + jl], 0.0), reads=["gbias"], writes=["gbias"])
            P.op("pool", lambda e, jl=jl: e.tensor_scalar(out=gbias[:, jl, 0:4], in0=gbias[:, jl, 0:4],
                                                          scalar1=pmask[:, 0:1], scalar2=None, op0=ALU.add),
                 reads=["pmask", "gbias"], writes=["gbias"])

    ring_state = {"n": 0}

    def load_unit_cols(W, c0):
        i = ring_state["n"] % 2
        ring_state["n"] += 1
        src = W.rearrange("(c p) f -> p c f", p=128)[:, :, c0:c0 + 512]
        P.dma("pool", lambda e: e.dma_start(out=ring[i], in_=src), "ring%d" % i, writes=[("ring", i)])
        return i

    def load_unit_rows(W, r0):
        i = ring_state["n"] % 2
        ring_state["n"] += 1
        src = W[r0:r0 + 512, :].rearrange("(c p) d -> p c d", p=128)
        P.dma("pool", lambda e: e.dma_start(out=ringd[i], in_=src), "ring%d" % i, writes=[("ring", i)])
        return i

    bank_rr = {"n": 0}

    def mm_group(out_ap, pairs, reads, writes):
        n = len(pairs)
        fns = []
        for i, (l, r) in enumerate(pairs):
            fns.append(lambda e, l=l, r=r, i=i: e.matmul(out=out_ap, lhsT=l, rhs=r, start=(i == 0), stop=(i == n - 1)))
        P.op("pe", fns, reads=reads, writes=writes)

    def transposes(srcs, reads, writes):
        fns = [lambda e, d=d, s=s: e.transpose(out=d, in_=s, identity=ident) for d, s in srcs]
        P.op("pe", fns, reads=list(reads) + ["ident"], writes=writes)

    pb6 = ps[:, 6:8, :].bitcast(BF16).rearrange("p a b -> p (a b)")

    def to_hT(src_ap, src_tok, dst, dst_cols, dst_tok):
        srcs = [(pb6[:, cc * 128:(cc + 1) * 128], src_ap[:, cc * 128:(cc + 1) * 128]) for cc in range(16)]
        transposes(srcs, reads=[src_tok], writes=[("ps", 6), ("ps", 7)])
        P.op("act", lambda e: e.activation(out=dst[:, :, dst_cols], in_=pb6.rearrange("p (a b) -> p a b", b=128),
                                           func=AF.Copy),
             reads=[("ps", 6), ("ps", 7)], writes=[dst_tok])

    hT_all = [("hT", t) for t in range(NT)]

    def evac_act(out_ap, in_ap, scale, reads, writes):
        P.op("act", lambda e: e.activation(out=out_ap, in_=in_ap, func=AF.Copy, scale=scale), reads=reads, writes=writes)

    def next_bank(lo=0, hi=6):
        b = lo + bank_rr["n"] % (hi - lo)
        bank_rr["n"] += 1
        return b

    def proj_feat(slot, j, dst, dst_slot, col_off, scale, tokname):
        for th in range(2):
            b = next_bank()
            pairs = [(ring[slot][:, cc, j * 128:(j + 1) * 128], hT[:, cc, th * 512:(th + 1) * 512]) for cc in range(16)]
            mm_group(ps[:, b, :], pairs, reads=[("ring", slot)] + hT_all[th * 4:th * 4 + 4], writes=[("ps", b)])
            evac_act(dst[:, dst_slot, col_off + th * 512: col_off + (th + 1) * 512], ps[:, b, :], scale,
                     reads=[("ps", b)], writes=[(tokname, dst_slot, 4 * th + k) for k in range(4)])

    cnt = {"ot": 0, "pt": 0, "s": 0, "l": 0}
    pend = {"y": None, "x2": None}
    KV_ALIAS = ["gtab", "btab", ("hb", 0), ("hb", 1), "kvf"]

    def softmax_pv(isA, h_slot, t, nk, s_psum, sink_col, v_aps, s_banks, li):
        L = Lb[li]
        ltok = ("L", li)

        def x2_part():
            src = L[:, 0:nk] if s_psum is None else s_psum
            src_reads = [ltok] if s_psum is None else [("ps", b) for b in s_banks]
            P.op("dve", lambda e: e.reduce_max(out=smc(SM_MX), in_=src, axis=AX.X), reads=src_reads, writes=["mx"])
            if sink_col is not None:
                P.op("dve", lambda e: e.tensor_scalar(out=smc(SM_NEGM), in0=smc(SM_MX),
                                                      scalar1=sinkb[:, sink_col:sink_col + 1],
                                                      scalar2=-1.0, op0=ALU.max, op1=ALU.mult),
                     reads=["mx", "sinkb"], writes=["negm"])
            else:
                P.op("dve", lambda e: e.tensor_scalar(out=smc(SM_NEGM), in0=smc(SM_MX), scalar1=-1.0, scalar2=None,
                                                      op0=ALU.mult), reads=["mx"], writes=["negm"])
            P.op("act", lambda e: e.activation(out=L[:, 0:nk], in_=src, func=AF.Exp, bias=smc(SM_NEGM), scale=1.0,
                                               accum_out=smc(SM_RS)),
                 reads=src_reads + ["negm"], writes=[ltok, "rs"])
            if sink_col is not None:
                P.op("act", lambda e: e.activation(out=smc(SM_ES), in_=sinkb[:, sink_col:sink_col + 1], func=AF.Exp,
                                                   bias=smc(SM_NEGM), scale=1.0), reads=["negm", "sinkb"], writes=["es"])
                P.op("dve", lambda e: e.tensor_tensor(out=smc(SM_DEN), in0=smc(SM_RS), in1=smc(SM_ES), op=ALU.add),
                     reads=["rs", "es"], writes=["den"])
                P.op("dve", lambda e: e.reciprocal(out=smc(SM_RDEN), in_=smc(SM_DEN)), reads=["den"], writes=["rden"])
            else:
                P.op("dve", lambda e: e.reciprocal(out=smc(SM_RDEN), in_=smc(SM_RS)), reads=["rs"], writes=["rden"])
            pi = cnt["pt"] % 2
            cnt["pt"] += 1
            Pn = Pnb[pi]
            P.op("act", lambda e: e.activation(out=Pn[:, 0:nk], in_=L[:, 0:nk], func=AF.Identity,
                                               scale=smc(SM_RDEN), bias=0.0),
                 reads=[ltok, "rden"], writes=[("Pn", pi)])

            def y_part():
                softmax_y(isA, h_slot, t, nk, v_aps, pi)
            return y_part

        xprev = pend["x2"]
        pend["x2"] = x2_part
        if xprev is not None:
            if pend["y"] is not None:
                pend["y"]()
            pend["y"] = xprev()

    def flush_y():
        if pend["y"] is not None:
            pend["y"]()
            pend["y"] = None
        if pend["x2"] is not None:
            ylast = pend["x2"]()
            pend["x2"] = None
            ylast()

    def softmax_y(isA, h_slot, t, nk, v_aps, pi):
        Pn = Pnb[pi]
        nb = nk // 128
        if nb <= 8:
            pbank = [6 + pi]
            pdst = psb(6 + pi)
        else:
            pbank = [6, 7]
            pdst = pb6
        srcs = [(pdst[:, k * 128:(k + 1) * 128], Pn[:, k * 128:(k + 1) * 128]) for k in range(nb)]
        transposes(srcs, reads=[("Pn", pi)], writes=[("ps", b) for b in pbank])
        P.op("act", lambda e: e.activation(out=PT[pi][:, 0:nb, :],
                                           in_=pdst[:, 0:nb * 128].rearrange("p (a b) -> p a b", b=128), func=AF.Copy),
             reads=[("ps", b) for b in pbank], writes=[("PT", pi)])
        ob = (4 + cnt["ot"] % 2) if isA else 4
        cnt["ot"] += 1
        pairs = [(v_aps[k][0], PT[pi][:, k, :]) for k in range(nb)]
        vreads = sorted(set(v_aps[k][1] for k in range(nb)), key=str)
        mm_group(ps[:, ob, 0:128], pairs, reads=[("PT", pi)] + vreads, writes=[("ps", ob)])
        P.op("act", lambda e: e.activation(out=qT[:, h_slot, t * 128:(t + 1) * 128], in_=ps[:, ob, 0:128], func=AF.Copy),
             reads=[("ps", ob)], writes=[("qT", h_slot, t)])

    def load_gb(layer, i):
        P.dma("sp", lambda e: e.dma_start(out=gtab, in_=lng_all[layer, i:i + 1, :].partition_broadcast(128)), "misc",
              writes=["gtab"])
        P.dma("sp", lambda e: e.dma_start(out=btab, in_=lnb_all[layer, i:i + 1, :].partition_broadcast(128)), "misc",
              writes=["btab"])

    hout_v = hout.rearrange("(n p) d -> p n d", p=128)

    def layer_norm_all(store, make_hT, halo_send):
        def s1(t):
            par = t % 2
            st_, mv_ = stt2[par], mv2b[par]
            sd_, rstd_, nmr_ = smc(9 + par), smc(11 + par), smc(13 + par)
            for q4 in range(4):
                P.op("dve", lambda e, q4=q4: e.bn_stats(out=st_[:, q4, :], in_=h[:, t, q4 * 512:(q4 + 1) * 512]),
                     reads=[("h", t)], writes=[("stt", par, q4)])
            P.op("dve", lambda e: e.bn_aggr(out=mv_, in_=st_.rearrange("p a b -> p (a b)")),
                 reads=[("stt", par, q4) for q4 in range(4)], writes=[("mv2", par)])
            P.op("act", lambda e: e.activation(out=sd_, in_=mv_[:, 1:2], func=AF.Sqrt, bias=epsb[:, 0:1], scale=1.0),
                 reads=[("mv2", par), "epsb"], writes=[("sd", par)])
            P.op("dve", lambda e: e.reciprocal(out=rstd_, in_=sd_), reads=[("sd", par)], writes=[("rstd", par)])
            P.op("dve", lambda e: e.scalar_tensor_tensor(out=nmr_, in0=mv_[:, 0:1], scalar=-1.0, in1=rstd_,
                                                         op0=ALU.mult, op1=ALU.mult),
                 reads=[("mv2", par), ("rstd", par)], writes=[("nmr", par)])

        def s2a(t):
            par = t % 2
            P.op("act", lambda e: e.activation(out=h[:, t, :], in_=h[:, t, :], func=AF.Identity, scale=smc(11 + par),
                                               bias=smc(13 + par)),
                 reads=[("h", t), ("rstd", par), ("nmr", par)], writes=[("h", t)])
            P.op("pool", lambda e: e.tensor_tensor(out=h[:, t, :], in0=h[:, t, :], in1=gtab, op=ALU.mult),
                 reads=[("h", t), "gtab"], writes=[("h", t)])

        def s2b(t):
            P.op("dve", lambda e: e.tensor_tensor(out=h[:, t, :], in0=h[:, t, :], in1=btab, op=ALU.add),
                 reads=[("h", t), "btab"], writes=[("h", t)])
            if store:
                P.dma("sp", lambda e: e.dma_start(out=hout_v[:, t, :], in_=h[:, t, :]), "hstore%d" % (t % 4),
                      reads=[("h", t)])
            if make_hT:
                i = t % 2
                P.op("act", lambda e: e.activation(out=hb[i], in_=h[:, t, :], func=AF.Copy),
                     reads=[("h", t)], writes=[("hb", i)])
                if halo_send and t == NT - 1:
                    P.dma("sp", lambda e: e.dma_start(out=halo_in.ap(), in_=h[:, t, :]), "halo", reads=[("h", t)],
                          writes=["halo_in"])
                    P.special("pool", lambda e: e.collective_compute(
                        "AllGather", ALU.bypass, replica_groups=groups, ins=[halo_in.ap().opt()],
                        outs=[halo_out.ap().opt()]), "halo", reads=["halo_in"], writes=["halo_out"])

        def s3(t):
            if make_hT:
                i = t % 2
                to_hT(hb[i], ("hb", i), hT, slice(t * 128, (t + 1) * 128), ("hT", t))

        for n in range(NT + 2):
            if 0 <= n - 1 < NT:
                s2a(n - 1)
            if n < NT:
                s1(n)
            if 0 <= n - 2 < NT:
                s3(n - 2)
            if 0 <= n - 1 < NT:
                s2b(n - 1)

    xin_v = xin.rearrange("(n p) d -> p n d", p=128)
    for t in range(NT):
        P.dma("sp", lambda e, t=t: e.dma_start(out=h[:, t, :], in_=xin_v[:, t, :]), "hload%d" % (t % 4), writes=[("h", t)])
    for t in range(NT):
        i = t % 2
        P.op("act", lambda e, t=t, i=i: e.activation(out=hb[i], in_=h[:, t, :], func=AF.Copy),
             reads=[("h", t)], writes=[("hb", i)])
        to_hT(hb[i], ("hb", i), hT, slice(t * 128, (t + 1) * 128), ("hT", t))
    setup_A()

    for layer in range(nlayers):
        isA = layer < 2
        last = layer == nlayers - 1
        w_mkv = w_mem_kv[layer]
        w_o = w_o_all[layer]
        w_up = w_up_all[layer]
        w_dn = w_dn_all[layer]
        if layer > 0:
            P.barrier()
        if isA:
            w_qkv = w_in_a[layer]
            if layer == 0:
                P.dma("pool", lambda e: e.dma_start(out=hhb, in_=xhalo), "misc2", writes=["hhb"])
            else:
                P.dma("pool", lambda e: e.dma_start(out=hhb, in_=halo_out.ap()[0:128, :]), "misc2", reads=["halo_out"],
                      writes=["hhb"])
            to_hT(hhb, "hhb", hTh, slice(0, 128), "hTh")
            units = [("q", 0), ("q", 1), ("q", 2), ("k", 3), ("v", 4), ("qm", 5)]
        else:
            w_qkv = w_q_b[layer - 2]
            if layer == 2:
                setup_B()
            units = [("q", 0), ("q", 1), ("q", 2), ("qm", 3)]
            kparts = [(kout_ap[0:128, :], kT[:, :, 0:1024]), (kin_ap, kT[:, :, 1024:2048])]
            for si, di in kparts:
                P.dma("pool", lambda e, si=si, di=di: e.dma_start(out=di, in_=si.rearrange("p (a b) -> p a b", b=1024)),
                      "misc2", reads=["kout", "vout", "kmout", "kin", "vin", "kmin"], writes=["kT"] + KV_ALIAS)
            vparts = [(vout_ap[0:128, :], vv[:, 0:8, :]), (vin_ap, vv[:, 8:16, :])]
            for si, di in vparts:
                P.dma("pool", lambda e, si=si, di=di: e.dma_start(out=di, in_=si.rearrange("p (a b) -> p a b", b=512)),
                      "misc2", reads=["kout", "vout", "kmout", "kin", "vin", "kmin"], writes=["vv"] + KV_ALIAS)
            mparts = [(kmout_ap[0:128, 0:16], kmT[:, :, 0:4]), (kmin_ap[:, 0:16], kmT[:, :, 4:8])]
            for si, di in mparts:
                P.dma("pool", lambda e, si=si, di=di: e.dma_start(out=di, in_=si.rearrange("p (a b) -> p a b", b=4)),
                      "misc2", reads=["kout", "vout", "kmout", "kin", "vin", "kmin"], writes=["kmT"] + (A_TABLE_TOKS if layer == 2 else []))
        P.mark("L%d proj" % layer)
        for typ, u in units:
            slot = load_unit_cols(w_qkv, u * 512)
            if typ in ("q", "qm"):
                base = (u * 4) if typ == "q" else 12
                for j in range(4):
                    proj_feat(slot, j, qT, base + j, 0, SC, "qT")
            elif typ == "k":
                for j in range(4):
                    proj_feat(slot, j, kT, j, 128, 1.0, "kTt")
                    b = next_bank()
                    pairs = [(ring[slot][:, cc, j * 128:(j + 1) * 128], hTh[:, cc, :]) for cc in range(16)]
                    mm_group(ps[:, b, 0:128], pairs, reads=[("ring", slot), "hTh"], writes=[("ps", b)])
                    evac_act(kT[:, j, 0:128], ps[:, b, 0:128], 1.0, reads=[("ps", b)], writes=[("kTh", j)])
            elif typ == "v":
                for t in range(-1, NT):
                    b = next_bank()
                    if t < 0:
                        pairs = [(hTh[:, cc, :], ring[slot][:, cc, :]) for cc in range(16)]
                        rd = ["hTh"]
                    else:
                        pairs = [(hT[:, cc, t * 128:(t + 1) * 128], ring[slot][:, cc, :]) for cc in range(16)]
                        rd = [("hT", t)]
                    mm_group(ps[:, b, :], pairs, reads=[("ring", slot)] + rd, writes=[("ps", b)])
                    evac_act(vv[:, t + 1, :], ps[:, b, :], 1.0, reads=[("ps", b)], writes=[("vvt", t + 1)])
        P.mark("L%d memkv" % layer)
        P.barrier()
        P.dma("pool", lambda e: e.dma_start(out=memb, in_=mem.rearrange("(n p) d -> p n d", p=128)), "misc2",
              writes=["memb"] + hT_all)
        for mt in range(2):
            srcs = [(pb6[:, cc * 128:(cc + 1) * 128], memb[:, mt, cc * 128:(cc + 1) * 128]) for cc in range(16)]
            transposes(srcs, reads=["memb"], writes=[("ps", 6), ("ps", 7)])
            P.op("act", lambda e, mt=mt: e.activation(out=memT[:, :, mt * 128:(mt + 1) * 128],
                                                      in_=pb6.rearrange("p (a b) -> p a b", b=128), func=AF.Copy),
                 reads=[("ps", 6), ("ps", 7)], writes=["memT"])
        slot = load_unit_cols(w_mkv, 0)
        for j in range(4):
            b = next_bank()
            pairs = [(ring[slot][:, cc, j * 128:(j + 1) * 128], memT[:, cc, :]) for cc in range(16)]
            mm_group(ps[:, b, 0:256], pairs, reads=[("ring", slot), "memT"], writes=[("ps", b)])
            evac_act(mkT[:, j, :], ps[:, b, 0:256], 1.0, reads=[("ps", b)], writes=["mkT"])
        slot = load_unit_cols(w_mkv, 512)
        for mt in range(2):
            b = next_bank()
            pairs = [(memT[:, cc, mt * 128:(mt + 1) * 128], ring[slot][:, cc, :]) for cc in range(16)]
            mm_group(ps[:, b, :], pairs, reads=[("ring", slot), "memT"], writes=[("ps", b)])
            evac_act(mv[:, mt, :], ps[:, b, :], 1.0, reads=[("ps", b)], writes=["mv"])
        P.barrier()
        P.mark("L%d att" % layer)
        for hh in range(12):
            kvh = hh // 3
            for t in range(NT):
                qap = qT[:, hh, t * 128:(t + 1) * 128]
                if isA:
                    sb = cnt["s"] % 4
                    cnt["s"] += 1
                    mm_group(ps[:, sb, 0:256], [(qap, kT[:, kvh, t * 128:t * 128 + 256])],
                             reads=[("qT", hh, t), ("kTh", kvh)] + [("kTt", kvh, k) for k in range(NT)],
                             writes=[("ps", sb)])
                    dtab = dist0 if t == 0 else dist
                    li = cnt["l"] % 2
                    cnt["l"] += 1
                    L = Lb[li]
                    P.op("dve", lambda e, dtab=dtab, sb=sb, hh=hh, L=L: e.scalar_tensor_tensor(
                        out=L[:, 0:256], in0=dtab, scalar=-SLOPES[hh], in1=ps[:, sb, 0:256], op0=ALU.mult, op1=ALU.add),
                        reads=[("ps", sb), "dist", "dist0"], writes=[("L", li)])
                    v_aps = [(vv[:, t + k, kvh * 128:(kvh + 1) * 128], ("vvt", t + k)) for k in range(2)]
                    softmax_pv(True, hh, t, 256, None, layer * 12 + hh, v_aps, None, li)
                else:
                    jl = t // 2
                    nk = 1024 + (t + 1) * 128
                    nmm = (nk + 511) // 512
                    fns = []
                    for m in range(nmm):
                        w = min(512, nk - m * 512)
                        fns.append(lambda e, m=m, w=w, qap=qap, kvh=kvh: e.matmul(
                            out=ps[:, m, 0:w], lhsT=qap, rhs=kT[:, kvh, m * 512:m * 512 + w], start=True, stop=True))
                    fns.append(lambda e, qap=qap, kvh=kvh: e.matmul(out=ps[:, 5, 0:8], lhsT=qap, rhs=kmT[:, kvh, :],
                                                                    start=True, stop=True))
                    P.op("pe", fns, reads=[("qT", hh, t), "kT", "kmT"],
                         writes=[("ps", m) for m in range(nmm)] + [("ps", 5)])
                    P.op("dve", lambda e, jl=jl: e.tensor_tensor(out=gm, in0=ps[:, 5, 0:8], in1=gbias[:, jl, :], op=ALU.add),
                         reads=[("ps", 5), "gbias"], writes=["gm"])
                    P.op("dve", lambda e: e.max(out=top8, in_=gm), reads=["gm"], writes=["top8"])
                    P.op("dve", lambda e: e.tensor_scalar(out=bm, in0=gm, scalar1=top8[:, 2:3], scalar2=NEG,
                                                          op0=ALU.is_lt, op1=ALU.mult), reads=["gm", "top8"], writes=["bm"])
                    P.op("dve", lambda e, jl=jl: e.tensor_tensor(out=bm, in0=bm, in1=gbias[:, jl, :], op=ALU.add),
                         reads=["bm", "gbias"], writes=["bm"])
                    sall = ps[:, 0:4, :].rearrange("p a b -> p (a b)")
                    li = cnt["l"] % 2
                    cnt["l"] += 1
                    L = Lb[li]
                    P.op("dve", lambda e, nk=nk, hh=hh, sall=sall, L=L: e.scalar_tensor_tensor(
                        out=L[:, 0:nk], in0=R0[:, 0:nk], scalar=SLOPES[hh], in1=sall[:, 0:nk], op0=ALU.mult, op1=ALU.add),
                        reads=[("ps", m) for m in range(nmm)] + ["R0"], writes=[("L", li)])
                    nblk = 4 + jl
                    L3 = L[:, 0:nblk * 256].rearrange("p (a b) -> p a b", b=256)
                    bmb = bm[:, 0:nblk].unsqueeze(2).to_broadcast([128, nblk, 256])
                    P.op("pool", lambda e, L3=L3, bmb=bmb: e.tensor_tensor(out=L3, in0=L3, in1=bmb, op=ALU.add),
                         reads=[("L", li), "bm"], writes=[("L", li)])
                    P.op("pool", lambda e, nk=nk, L=L: e.tensor_tensor(out=L[:, nk - 128:nk], in0=L[:, nk - 128:nk],
                                                                       in1=caus, op=ALU.add),
                         reads=[("L", li), "caus"], writes=[("L", li)])
                    v_aps = [(vv[:, k, kvh * 128:(kvh + 1) * 128], "vv") for k in range(nk // 128)]
                    softmax_pv(False, hh, t, nk, None, None, v_aps, None, li)
        for j in range(4):
            for t in range(NT):
                sb = cnt["s"] % 4
                cnt["s"] += 1
                mm_group(ps[:, sb, 0:256], [(qT[:, 12 + j, t * 128:(t + 1) * 128], mkT[:, j, :])],
                         reads=[("qT", 12 + j, t), "mkT"], writes=[("ps", sb)])
                v_aps = [(mv[:, k, j * 128:(j + 1) * 128], "mv") for k in range(2)]
                li = cnt["l"] % 2
                cnt["l"] += 1
                softmax_pv(isA, 12 + j, t, 256, ps[:, sb, 0:256], None, v_aps, [sb], li)
        flush_y()
        P.mark("L%d wo" % layer)
        P.barrier()
        load_gb(layer, 0)
        for dq in range(4):
            slot = load_unit_cols(w_o, dq * 512)
            for t in range(NT):
                b = next_bank(0, 6)
                pairs = [(qT[:, cc, t * 128:(t + 1) * 128], ring[slot][:, cc, :]) for cc in range(16)]
                mm_group(ps[:, b, :], pairs, reads=[("ring", slot)] + [("qT", s, t) for s in range(16)],
                         writes=[("ps", b)])
                P.op("dve", lambda e, t=t, dq=dq, b=b: e.scalar_tensor_tensor(
                    out=h[:, t, dq * 512:(dq + 1) * 512], in0=h[:, t, dq * 512:(dq + 1) * 512], scalar=ALPHA,
                    in1=ps[:, b, :], op0=ALU.mult, op1=ALU.add), reads=[("ps", b), ("h", t)], writes=[("h", t)])
        NG = 16
        plan = [("up", 0)]
        for g in range(1, NG):
            plan += [("up", g), ("dn", g - 1)]
        plan += [("dn", NG - 1)]
        plan_slot = {}
        plan_pos = {"n": 0}

        def plan_issue():
            k = plan_pos["n"]
            if k >= len(plan):
                return
            plan_pos["n"] += 1
            typ, g = plan[k]
            plan_slot[(typ, g)] = load_unit_cols(w_up, g * 512) if typ == "up" else load_unit_rows(w_dn, g * 512)

        plan_issue()
        plan_issue()
        P.mark("L%d ln1" % layer)
        P.barrier()
        layer_norm_all(False, True, False)
        P.mark("L%d ffn" % layer)
        P.barrier()
        load_gb(layer, 1)
        upb = {"n": 0}
        dnb = {"n": 0}
        rt = {"n": 0}

        def ffn_up(g):
            slot = plan_slot[("up", g)]
            ai = g % 2
            for j in range(4):
                for th in range(2):
                    b = upb["n"] % 4
                    upb["n"] += 1
                    pairs = [(ring[slot][:, cc, j * 128:(j + 1) * 128], hT[:, cc, th * 512:(th + 1) * 512])
                             for cc in range(16)]
                    mm_group(ps[:, b, :], pairs, reads=[("ring", slot)] + hT_all[th * 4:th * 4 + 4], writes=[("ps", b)])
                    ri = rt["n"] % 2
                    rt["n"] += 1
                    P.op("act", lambda e, b=b, ri=ri: e.activation(out=rtmp[ri], in_=ps[:, b, :], func=AF.Relu),
                         reads=[("ps", b)], writes=[("rtmp", ri)])
                    P.op("pool", lambda e, ri=ri, ai=ai, j=j, th=th: e.tensor_tensor(
                        out=actT[ai][:, j, th * 512:(th + 1) * 512], in0=rtmp[ri], in1=rtmp[ri], op=ALU.mult),
                        reads=[("rtmp", ri)], writes=[("actT", ai, j, th)])

        def ffn_down(g):
            slot = plan_slot[("dn", g)]
            ai = g % 2
            for t in range(NT):
                for dq in range(4):
                    b = 4 + dnb["n"] % 4
                    dnb["n"] += 1
                    pairs = [(actT[ai][:, j, t * 128:(t + 1) * 128], ringd[slot][:, j, dq * 512:(dq + 1) * 512])
                             for j in range(4)]
                    mm_group(ps[:, b, :], pairs, reads=[("ring", slot)] + [("actT", ai, j, t // 4) for j in range(4)],
                             writes=[("ps", b)])
                    hs = h[:, t, dq * 512:(dq + 1) * 512]
                    if g == 0:
                        P.op("dve", lambda e, hs=hs, b=b: e.scalar_tensor_tensor(out=hs, in0=hs, scalar=ALPHA,
                                                                                 in1=ps[:, b, :], op0=ALU.mult, op1=ALU.add),
                             reads=[("ps", b), ("h", t)], writes=[("h", t)])
                    else:
                        P.op("dve", lambda e, hs=hs, b=b: e.tensor_tensor(out=hs, in0=hs, in1=ps[:, b, :], op=ALU.add),
                             reads=[("ps", b), ("h", t)], writes=[("h", t)])

        for g in range(NG):
            ffn_up(g)
            plan_issue()
            if g > 0:
                ffn_down(g - 1)
                plan_issue()
        ffn_down(NG - 1)
        P.mark("L%d ln2" % layer)
        P.barrier()
        layer_norm_all(last, not last, layer == 0 and nlayers > 1)
        if layer == 1 and nlayers > 2:
            P.barrier()
            for u in range(2):
                slot = load_unit_cols(w_kvs, u * 512)
                if u == 0:
                    for j in range(4):
                        for th in range(2):
                            b = next_bank()
                            pairs = [(ring[slot][:, cc, j * 128:(j + 1) * 128], hT[:, cc, th * 512:(th + 1) * 512])
                                     for cc in range(16)]
                            mm_group(ps[:, b, :], pairs, reads=[("ring", slot)] + hT_all[th * 4:th * 4 + 4],
                                     writes=[("ps", b)])
                            P.op("act", lambda e, b=b: e.activation(out=kvf, in_=ps[:, b, :], func=AF.Copy),
                                 reads=[("ps", b)], writes=["kvf"])
                            co = j * 1024 + th * 512
                            P.dma("sp", lambda e, co=co: e.dma_start(out=kin_ap[:, co:co + 512], in_=kvf),
                                  "kvst", reads=["kvf"], writes=["kin"])
                            P.op("dve", lambda e, j=j, th=th: e.tensor_reduce(
                                out=kmacc[:, j, 2 * th:2 * th + 2], in_=kvf.rearrange("p (a b) -> p a b", b=256),
                                axis=AX.X, op=ALU.add), reads=["kvf"], writes=["kmacc"])
                    P.op("dve", lambda e: e.tensor_scalar(out=kmacc, in0=kmacc, scalar1=1.0 / 256.0, scalar2=None,
                                                          op0=ALU.mult), reads=["kmacc"], writes=["kmacc"])
                else:
                    for t in range(NT):
                        b = next_bank()
                        pairs = [(hT[:, cc, t * 128:(t + 1) * 128], ring[slot][:, cc, :]) for cc in range(16)]
                        mm_group(ps[:, b, :], pairs, reads=[("ring", slot), ("hT", t)], writes=[("ps", b)])
                        P.op("act", lambda e, b=b: e.activation(out=kvf, in_=ps[:, b, :], func=AF.Copy),
                             reads=[("ps", b)], writes=["kvf"])
                        co = t * 512
                        P.dma("sp", lambda e, co=co: e.dma_start(out=vin_ap[:, co:co + 512], in_=kvf), "kvst",
                              reads=["kvf"], writes=["vin"])
            P.op("dve", lambda e: e.memset(kvf[:, 0:256], 0.0), writes=["kvf"])
            P.op("dve", lambda e: e.tensor_copy(out=kvf[:, 0:16].rearrange("p (a b) -> p a b", b=4), in_=kmacc),
                 reads=["kmacc", "kvf"], writes=["kvf"])
            P.dma("sp", lambda e: e.dma_start(out=kmin_ap, in_=kvf[:, 0:256]), "kvst2", reads=["kvf"], writes=["kmin"])
            for nm, ti, to in (("kx", kin, kout), ("vx", vin, vout), ("kmx", kmin, kmout)):
                P.special("pool", lambda e, ti=ti, to=to: e.collective_compute(
                    "AllGather", ALU.bypass, replica_groups=groups, ins=[ti.ap().opt()], outs=[to.ap().opt()]),
                    nm, reads=[ti.name], writes=[to.name])
    P.mark("end")
    P.emit()
    nc._marks = P.marks
    return nc


_CACHE = {}
_DBG = None


def _prep(x, mem, w_in_a, sinks_a, w_q_b, w_kv_shared, w_mem_kv, w_o, w_up, w_down, ln_g, ln_b, ncores=8):
    f = lambda a: np.ascontiguousarray(np.asarray(a, dtype=np.float32))
    x = f(x); mem = f(mem)
    shared = {"w_in_a": f(w_in_a), "sinks": f(sinks_a).reshape(1, 24), "w_q_b": f(w_q_b), "w_kvs": f(w_kv_shared),
              "w_mem_kv": f(w_mem_kv), "w_o": f(w_o), "w_up": f(w_up), "w_dn": f(w_down), "lng": f(ln_g), "lnb": f(ln_b)}
    zeros_halo = np.zeros((128, 2048), np.float32)
    in_maps = []
    for c in range(ncores):
        b, hf = c // 2, c % 2
        m = dict(shared)
        m["x_own"] = np.ascontiguousarray(x[b, hf * 1024:(hf + 1) * 1024])
        m["x_halo"] = np.ascontiguousarray(x[b, 896:1024]) if hf == 1 else zeros_halo
        m["halobig"] = np.full((128, 1), 0.0 if hf == 1 else BIG, np.float32)
        m["pastmask"] = np.full((128, 1), 0.0 if hf == 1 else NEG, np.float32)
        m["mem"] = np.ascontiguousarray(mem[b])
        in_maps.append(m)
    return in_maps


def kernel(x, mem, w_in_a, sinks_a, w_q_b, w_kv_shared, w_mem_kv, w_o, w_up, w_down, ln_g, ln_b):
    ncores = 8
    if "nc" not in _CACHE:
        _CACHE["nc"] = build_fused(4, ncores)
    nc = _CACHE["nc"]
    in_maps = _prep(x, mem, w_in_a, sinks_a, w_q_b, w_kv_shared, w_mem_kv, w_o, w_up, w_down, ln_g, ln_b, ncores)
    res = run_bass_kernel_spmd(nc, in_maps, core_ids=list(range(ncores)))
    out = np.empty((4, 2048, 2048), np.float32)
    for c in range(ncores):
        b, hf = c // 2, c % 2
        out[b, hf * 1024:(hf + 1) * 1024] = res.results[c]["hout"]
    return out
```
